# Optimizing a Trainium2 kernel written in Bass

```python
import jax, jax.numpy as jnp
from jax import lax
import numpy as np

D_MODEL = 1024
BATCH = 16
SEQ = 2048
DEPTH = 2

HEAD_DIM_A = 64
N_HEADS_A = 6
D_A = N_HEADS_A * HEAD_DIM_A
HEAD_DIM_B = 96
N_HEADS_B = 4
D_B = N_HEADS_B * HEAD_DIM_B
POOL_WINDOWS = (2, 4, 8, 16)
N_GROUPS_C = len(POOL_WINDOWS)
GROUP_DIM_C = 64
D_C = N_GROUPS_C * GROUP_DIM_C
D_MIX = D_A + D_B + D_C
D_IN = 2 * D_A + 2 * D_B + D_C
CONV_WIDTH = 31
CHUNK = 128
D_FF = ((8 * D_MODEL // 3 + 255) // 256) * 256
RMS_EPS = 1e-6
LN_EPS = 1e-5

kernel_name = "hybrid_conv_gmlp_pool_block"


def rms_norm(x, g):
    xf = x.astype(jnp.float32)
    y = xf * lax.rsqrt(jnp.mean(xf * xf, axis=-1, keepdims=True) + RMS_EPS)
    return (y * g.astype(jnp.float32)).astype(x.dtype)


def layer_norm(x, g, b):
    xf = x.astype(jnp.float32)
    mu = jnp.mean(xf, axis=-1, keepdims=True)
    xc = xf - mu
    var = jnp.mean(xc * xc, axis=-1, keepdims=True)
    y = xc * lax.rsqrt(var + LN_EPS) * g.astype(jnp.float32) + b.astype(jnp.float32)
    return y.astype(x.dtype)


def conformer_conv(z, conv_w, conv_b, ln_g, ln_b, w_pw):
    a, gate = jnp.split(z, 2, axis=-1)
    y = a * jax.nn.sigmoid(gate)
    y = lax.conv_general_dilated(
        y, conv_w[:, None, :], window_strides=(1,),
        padding=[(CONV_WIDTH - 1, 0)],
        dimension_numbers=("NWC", "WIO", "NWC"),
        feature_group_count=D_A) + conv_b
    y = jax.nn.silu(layer_norm(y, ln_g, ln_b))
    return y @ w_pw


def spatial_gating(z, ln_g, ln_b, w_s, b_s):
    bsz, seq, _ = z.shape
    z = jax.nn.gelu(z)
    u, v = jnp.split(z, 2, axis=-1)
    v = layer_norm(v, ln_g, ln_b)
    v = v.reshape(bsz, seq // CHUNK, CHUNK, N_HEADS_B, HEAD_DIM_B)
    causal = jnp.tril(jnp.ones((CHUNK, CHUNK), dtype=bool))
    w = jnp.where(causal[None], w_s, jnp.zeros_like(w_s))
    s = jnp.einsum('hts,bnshd->bnthd', w, v) + b_s.T[:, :, None]
    return u * s.reshape(bsz, seq, D_B)


def multiscale_pool(z, w_pool, pool_scale):
    bsz, seq, _ = z.shape
    zf = z.astype(jnp.float32)
    cs0 = jnp.concatenate([jnp.zeros((bsz, 1, D_C), jnp.float32), jnp.cumsum(zf, axis=1)], axis=1)
    t = jnp.arange(seq, dtype=jnp.float32)[:, None]
    outs = []
    for g, w in enumerate(POOL_WINDOWS):
        sl = slice(g * GROUP_DIM_C, (g + 1) * GROUP_DIM_C)
        c = cs0[..., sl]
        upper = c[:, 1:]
        lower = jnp.concatenate([jnp.zeros((bsz, w - 1, GROUP_DIM_C), jnp.float32),
                                 c[:, :seq - w + 1]], axis=1)
        cnt = jnp.minimum(t + 1.0, float(w))
        outs.append((upper - lower) / cnt - zf[..., sl])
    p = jnp.stack(outs, axis=2).astype(z.dtype)
    y = jnp.einsum('bsgi,gio->bsgo', p, w_pool).reshape(bsz, seq, D_C)
    return y * pool_scale


def setup_inputs(seed: int = 0) -> dict:
    key = jax.random.key(seed)
    ks = jax.random.split(key, 20)
    n = lambda k, shape, s: jax.random.normal(k, shape, jnp.float32) * s
    L = DEPTH
    return {
        "x": n(ks[0], (BATCH, SEQ, D_MODEL), 1.0),
        "norm1_g": 1.0 + n(ks[1], (L, D_MODEL), 0.02),
        "w_in": n(ks[2], (L, D_MODEL, D_IN), D_MODEL ** -0.5),
        "conv_w": n(ks[3], (L, CONV_WIDTH, D_A), CONV_WIDTH ** -0.5),
        "conv_b": n(ks[4], (L, D_A), 0.01),
        "conv_ln_g": 1.0 + n(ks[5], (L, D_A), 0.02),
        "conv_ln_b": n(ks[6], (L, D_A), 0.01),
        "w_pw": n(ks[7], (L, D_A, D_A), D_A ** -0.5),
        "sg_ln_g": 1.0 + n(ks[8], (L, D_B), 0.02),
        "sg_ln_b": n(ks[9], (L, D_B), 0.01),
        "w_s": n(ks[10], (L, N_HEADS_B, CHUNK, CHUNK), CHUNK ** -0.5),
        "b_s": 1.0 + n(ks[11], (L, N_HEADS_B, CHUNK), 0.02),
        "w_pool": n(ks[12], (L, N_GROUPS_C, GROUP_DIM_C, GROUP_DIM_C), GROUP_DIM_C ** -0.5),
        "pool_scale": 1.0 + n(ks[13], (L, D_C), 0.02),
        "w_out": n(ks[14], (L, D_MIX, D_MODEL), D_MIX ** -0.5),
        "norm2_g": 1.0 + n(ks[15], (L, D_MODEL), 0.02),
        "w_gate_up": n(ks[16], (L, D_MODEL, 2 * D_FF), D_MODEL ** -0.5),
        "w_down": n(ks[17], (L, D_FF, D_MODEL), D_FF ** -0.5),
        "final_g": 1.0 + n(ks[18], (D_MODEL,), 0.02),
    }


def reference(x, norm1_g, w_in, conv_w, conv_b, conv_ln_g, conv_ln_b, w_pw,
              sg_ln_g, sg_ln_b, w_s, b_s, w_pool, pool_scale, w_out,
              norm2_g, w_gate_up, w_down, final_g):
    for l in range(DEPTH):
        h = rms_norm(x, norm1_g[l])
        z = h @ w_in[l]
        za = z[..., :2 * D_A]
        zb = z[..., 2 * D_A:2 * D_A + 2 * D_B]
        zc = z[..., 2 * D_A + 2 * D_B:]
        ya = conformer_conv(za, conv_w[l], conv_b[l], conv_ln_g[l], conv_ln_b[l], w_pw[l])
        yb = spatial_gating(zb, sg_ln_g[l], sg_ln_b[l], w_s[l], b_s[l])
        yc = multiscale_pool(zc, w_pool[l], pool_scale[l])
        x = x + jnp.concatenate([ya, yb, yc], axis=-1) @ w_out[l]
        h = rms_norm(x, norm2_g[l])
        gate, up = jnp.split(h @ w_gate_up[l], 2, axis=-1)
        x = x + (jax.nn.silu(gate) * up) @ w_down[l]
    return rms_norm(x, final_g)
```

```python
import numpy as np
import concourse.bass as bass
import concourse.mybir as mybir
from concourse.bass_utils import run_bass_kernel_spmd

F32 = mybir.dt.float32
BF16 = mybir.dt.bfloat16
AF = mybir.ActivationFunctionType
ALU = mybir.AluOpType

D = 1024
S = 2048
NT = 512
NJ = S // NT
DA = 384
DB = 384
DC = 256
DIN = 1792
DFF = 2816
NF = DFF // 128
CW = 31
HALO = CW - 1
ZH = 16
RMS_EPS = 1e-6
LN_EPS = 1e-5
N_CORES = 8
CHUNKS = [(0, 6), (6, 6), (12, 6), (18, 4)]
NRING = 3
NACT = 6
NPE = 18


class Buf:
    __slots__ = ("name", "w", "r", "region")

    def __init__(self, name, region=None):
        self.name = name
        self.w = None
        self.r = []
        self.region = region


class Eng:
    def __init__(self, name):
        self.name = name
        self.ops = []
        self.gen = 0
        self.cnt = 0
        self.waited = {}
        self.dma_rr = 0
        self.dma_vals = {}

    @property
    def semkey(self):
        return ("e", self.name, self.gen)


SEM_LIMIT = 30000
N_DMA_SEMS = {"gpsimd": 20, "sync": 12, "scalar": 4}


class Prog:
    def __init__(self):
        self.eng = {n: Eng(n) for n in ("tensor", "vector", "scalar", "gpsimd", "sync")}
        self.bufs = {}
        self.fence = {}
        self.region_last = {}
        self.region_dma = {}
        self.final_tokens = []

    def buf(self, *key, region=None):
        b = self.bufs.get(key)
        if b is None:
            b = Buf(key, region)
            self.bufs[key] = b
        return b

    def new_phase(self, regions):
        for r in regions:
            toks = list(self.region_last.get(r, {}).values()) + list(self.region_dma.get(r, []))
            self.fence[r] = toks
            self.region_dma[r] = []

    def add(self, engname, fn, reads=(), writes=(), dma=False, final=False):
        e = self.eng[engname]
        deps = {}

        def need(tok):
            if tok is None:
                return
            k, v = tok
            if deps.get(k, 0) < v:
                deps[k] = v

        for b in reads:
            need(b.w)
            if b.region is not None:
                for t in self.fence.get(b.region, ()):
                    need(t)
        for b in writes:
            need(b.w)
            for t in b.r:
                need(t)
            if b.region is not None:
                for t in self.fence.get(b.region, ()):
                    need(t)
        if dma:
            n = N_DMA_SEMS[engname]
            idx = e.dma_rr % n
            e.dma_rr += 1
            k = ("d", engname, idx)
            prev = e.dma_vals.get(k, 0)
            if prev:
                need((k, prev))
            tok = (k, prev + 16)
            e.dma_vals[k] = prev + 16
        else:
            if e.cnt >= SEM_LIMIT:
                e.gen += 1
                e.cnt = 0
            e.cnt += 1
            tok = (e.semkey, e.cnt)
        waits = []
        for k, v in deps.items():
            if engname == "tensor" and k == e.semkey:
                continue
            if e.waited.get(k, 0) >= v:
                continue
            e.waited[k] = v
            waits.append((k, v))
        e.ops.append((fn, waits, tok, dma))
        ws = set(id(b) for b in writes)
        for b in writes:
            b.w = tok
            b.r = []
        for b in reads:
            if id(b) not in ws:
                b.r.append(tok)
        for b in list(reads) + list(writes):
            if b.region is not None:
                if dma:
                    self.region_dma.setdefault(b.region, []).append(tok)
                else:
                    self.region_last.setdefault(b.region, {})[e.semkey[:2]] = tok
        if final:
            self.final_tokens.append(tok)
        return tok

    def emit(self, nc):
        keys = set()
        for e in self.eng.values():
            for fn, waits, tok, dma in e.ops:
                keys.add(tok[0])
                for k, v in waits:
                    keys.add(k)
        sems = {}
        for i, k in enumerate(sorted(keys, key=str)):
            sems[k] = nc.alloc_semaphore("s%d" % i)
        finals = self.final_tokens
        with nc.Block() as block:
            def mk(e):
                def body(eh):
                    for fn, waits, tok, dma in e.ops:
                        for k, v in waits:
                            eh.wait_ge(sems[k], v)
                        ins = fn(eh)
                        ins.then_inc(sems[tok[0]], 16 if dma else 1)
                    if e.name == "sync":
                        for k, v in finals:
                            eh.wait_ge(sems[k], v)
                return body
            for name, e in self.eng.items():
                if not e.ops and name != "sync":
                    continue
                getattr(block, name)(mk(e))


def build_program(nseq=2, depth=2):
    nc = bass.Bass("TRN2", target_bir_lowering=False)
    P = Prog()
    L = depth

    def din(name, shape):
        return nc.dram_tensor(name, list(shape), F32, kind="ExternalInput").ap()

    x_d = din("x", [nseq, S, D])
    norm1_g_d = din("norm1_g", [L, D])
    w_in_d = din("w_in", [L, D, DIN])
    conv_w_d = din("conv_w", [L, CW, DA])
    conv_b_d = din("conv_b", [L, DA])
    conv_ln_g_d = din("conv_ln_g", [L, DA])
    conv_ln_b_d = din("conv_ln_b", [L, DA])
    w_pw_d = din("w_pw", [L, DA, DA])
    sg_ln_g_d = din("sg_ln_g", [L, DB])
    sg_ln_b_d = din("sg_ln_b", [L, DB])
    w_s_d = din("w_s", [L, 4, 128, 128])
    b_s_d = din("b_s", [L, 4, 128])
    w_pool_d = din("w_pool", [L, 4, 64, 64])
    pool_scale_d = din("pool_scale", [L, DC])
    w_out_d = din("w_out", [L, D, D])
    norm2_g_d = din("norm2_g", [L, D])
    w_gu_d = din("w_gate_up", [L, D, 2 * DFF])
    w_down_d = din("w_down", [L, DFF, D])
    final_g_d = din("final_g", [D])
    out_d = nc.dram_tensor("out", [nseq, S, D], F32, kind="ExternalOutput").ap()

    base = (nc.sbuf_base + 31) // 32 * 32
    top = nc.sbuf_top
    cur = [base]

    def alloc(name, shape, dt, at=None):
        nbytes = int(np.prod(shape[1:])) * (4 if dt == F32 else 2)
        nbytes = (nbytes + 31) // 32 * 32
        if at is None:
            off = cur[0]
            cur[0] += nbytes
        else:
            off = at
        assert off + nbytes <= top, (name, off, nbytes, top)
        return nc.alloc_sbuf_tensor_at(name, list(shape), dt, offset=off), off, nbytes

    class Region:
        def __init__(self, name, size):
            self.name = name
            self.size = size
            self.base = cur[0]
            cur[0] += size
            assert cur[0] <= top, ("region overflow", name, cur[0], top)

        def carve(self):
            return Carver(self)

    class Carver:
        def __init__(self, reg):
            self.reg = reg
            self.off = reg.base

        def alloc(self, name, shape, dt):
            t, off, nb = alloc(name, shape, dt, at=self.off)
            self.off += nb
            assert self.off <= self.reg.base + self.reg.size, ("carve overflow", self.reg.name, name)
            return t

    ident = alloc("ident", [128, 128], F32)[0]
    onesF = alloc("onesF", [128, 128], F32)[0]
    mask = alloc("mask", [128, 128], F32)[0]
    identb = alloc("identb", [128, 128], BF16)[0]
    mhalf = alloc("mhalf", [128, 16], F32)[0]
    invcnt = alloc("invcnt", [128, 2, 16], F32)[0]
    ocS = alloc("ocS", [128, 2], BF16)[0]
    pv = alloc("pv", [128, L, 32], F32)[0]
    gFv = alloc("gFv", [128, 8], F32)[0]
    cwT = alloc("cwT", [128, L, 3, CW], F32)[0]
    BT = alloc("BT", [128, 4, 128], F32)[0]
    sgB = alloc("sgB", [128, 2, DB], F32)[0]
    sqr = [alloc("sqr%d" % i, [128, NT], BF16)[0] for i in range(2)]
    rtok = alloc("rtok", [128, 16], F32)[0]
    rtok2 = alloc("rtok2", [128, 16], F32)[0]
    rtokB = alloc("rtokB", [128, 16], F32)[0]
    dg = [alloc("dg%d" % i, [128, 128], F32)[0] for i in range(2)]
    bnst = [alloc("bnst%d" % i, [128, 32], F32)[0] for i in range(2)]
    xT = alloc("xT", [128, 8, S], F32)[0]
    w_in = alloc("w_in_sb", [128, 8, DIN], BF16)[0]
    w_pw = alloc("w_pw_sb", [128, 3, DA], BF16)[0]
    wsT = alloc("wsT", [128, 4, 128], BF16)[0]
    wpl = alloc("wpl", [128, 2, 128], BF16)[0]
    RR = Region("R", 24576)
    RH = Region("H", 32768)
    RA = Region("A", 24576)
    RD = Region("D", 12288)
    RE = Region("E", (top - cur[0]) // 32 * 32)

    c = RR.carve()
    wo_a = c.alloc("wo_a", [128, 3, D], BF16)
    wo_b = c.alloc("wo_b", [128, 4, D], BF16)
    wo_c = c.alloc("wo_c", [128, 2, D], BF16)
    c = RR.carve()
    ring = [c.alloc("ring%d" % i, [128, 2, 8, 256], BF16) for i in range(NRING)]
    c = RH.carve()
    h2T = c.alloc("h2T", [128, 8, S], BF16)
    c = RH.carve()
    hT = c.alloc("hT", [128, 8, NT], BF16)
    ya = c.alloc("ya", [128, 3, NT], BF16)
    yb = c.alloc("yb", [128, 4, NT], BF16)
    yc = c.alloc("yc", [128, 2, NT], BF16)
    vg = c.alloc("vg", [128, 4, DB], F32)
    vn = c.alloc("vn", [128, 4, DB], BF16)
    wsst = c.alloc("wsst", [128, 4, 128], F32)
    ybuf1 = c.alloc("ybuf1", [128, 3, HALO + NT], BF16)
    c = RA.carve()
    act = c.alloc("act", [128, 6, S], BF16)
    c = RH.carve()
    xstage = [c.alloc("xstage%d" % i, [128, D], F32) for i in range(8)]
    c = RA.carve()
    vst = c.alloc("vst", [32, 128], F32)
    cwst = c.alloc("cwst", [32, DA], F32)
    c = RA.carve()
    yn = c.alloc("yn", [128, 8, NT], F32)
    ostage = [c.alloc("ostage%d" % i, [128, D], F32) for i in range(2)]
    c = RA.carve()
    ybuf = c.alloc("ybuf", [128, 3, HALO + NT], BF16)
    acc = c.alloc("acc", [128, 3, NT], F32)
    accb = c.alloc("accb", [128, 3, NT], BF16)
    sqb = c.alloc("sqb", [128, 3, NT], BF16)
    sil = c.alloc("sil", [128, 3, NT], BF16)
    diag_tiles = []
    cA = c
    c = RD.carve()
    wd = c.alloc("wd", [128, 6, D], BF16)
    c = RD.carve()
    ug = c.alloc("ug", [128, 4, NT], BF16)
    zc = c.alloc("zc", [128, 2, ZH + NT], F32)
    pp = c.alloc("pp", [128, 2, NT], BF16)
    cD = c
    c = RE.carve()
    sA = c.alloc("sA", [128, 2, ZH + NT], F32)
    sB = c.alloc("sB", [128, 2, ZH + NT], F32)
    tlnv = [sA[:, 0, 0:NT], sB[:, 0, 0:NT]]
    cE = c
    cR = RR.carve()
    cR.off = RR.base + 18432
    for cc, rn in ((cR, "R"), (cA, "A"), (cD, "D"), (cE, "E")):
        while cc.off + 256 <= cc.reg.base + cc.reg.size and len(diag_tiles) < 3 * NPE:
            diag_tiles.append((cc.alloc("diag%d" % len(diag_tiles), [128, 128], BF16), rn))
    assert len(diag_tiles) == 3 * NPE, len(diag_tiles)

    banks = [nc.alloc_psum_tensor("bank%d" % i, [128, 512], F32) for i in range(8)]
    bank_rr = [0]

    def next_bank():
        for _ in range(8):
            i = bank_rr[0] % 8
            bank_rr[0] += 1
            b_ = P.buf("bank", i)
            if b_.w is None or len(b_.r) > 0:
                return banks[i], b_
        raise AssertionError("all PSUM banks are held by writes whose readers were not emitted yet")

    B = P.buf

    def dma(q, out, in_, reads=(), writes=(), final=False, nonc=False):
        def fn(e, out=out, in_=in_):
            if nonc:
                return e.dma_start(out=out, in_=in_, allow_slow_non_contiguous=True)
            return e.dma_start(out=out, in_=in_)
        return P.add(q, fn, reads=reads, writes=writes, dma=True, final=final)

    def mm(out, pairs, reads, writes, first=True, last=True):
        def fn(e, out=out, pairs=pairs, first=first, last=last):
            n = len(pairs)
            ins = None
            for i, (l, r) in enumerate(pairs):
                ins = e.matmul(out, l, r, start=(first and i == 0), stop=(last and i == n - 1))
            return ins
        return P.add("tensor", fn, reads=reads, writes=writes)

    def mms(groups, reads, writes):
        def fn(e, groups=groups):
            ins = None
            for out, pairs in groups:
                n = len(pairs)
                for i, (l, r) in enumerate(pairs):
                    ins = e.matmul(out, l, r, start=(i == 0), stop=(i == n - 1))
            return ins
        return P.add("tensor", fn, reads=reads, writes=writes)

    def transposes(items, reads, writes):
        def fn(e, items=items):
            ins = None
            for out, in_, idn in items:
                ins = e.transpose(out, in_, idn)
            return ins
        return P.add("tensor", fn, reads=reads, writes=writes)

    def act_op(out, in_, func, reads, writes, scale=None, bias=None):
        def fn(e, out=out, in_=in_, func=func, scale=scale, bias=bias):
            kw = {}
            if scale is not None:
                kw["scale"] = scale
            if bias is not None:
                kw["bias"] = bias
            return e.activation(out=out, in_=in_, func=func, **kw)
        return P.add("scalar", fn, reads=reads, writes=writes)

    def vec(fn, reads, writes):
        return P.add("vector", fn, reads=reads, writes=writes)

    def pool(fn, reads, writes):
        return P.add("gpsimd", fn, reads=reads, writes=writes)

    def tt(out, in0, in1, op, reads, writes, eng="vector"):
        return P.add(eng, lambda e, out=out, in0=in0, in1=in1, op=op: e.tensor_tensor(out=out, in0=in0, in1=in1, op=op),
                     reads=reads, writes=writes)

    def ts(out, in0, s1, s2, op0, op1, reads, writes, eng="vector"):
        def fn(e, out=out, in0=in0, s1=s1, s2=s2, op0=op0, op1=op1):
            if op1 is None:
                return e.tensor_scalar(out=out, in0=in0, scalar1=s1, scalar2=None, op0=op0)
            return e.tensor_scalar(out=out, in0=in0, scalar1=s1, scalar2=s2, op0=op0, op1=op1)
        return P.add(eng, fn, reads=reads, writes=writes)

    def stt(out, in0, scalar, in1, op0, op1, reads, writes):
        return vec(lambda e, out=out, in0=in0, scalar=scalar, in1=in1, op0=op0, op1=op1:
                   e.scalar_tensor_tensor(out=out, in0=in0, scalar=scalar, in1=in1, op0=op0, op1=op1),
                   reads=reads, writes=writes)

    def x_dma(s, j):
        for q in range(4):
            tb = 4 * j + q
            dma("sync", xstage[tb % 8][:], x_d[s, tb * 128:(tb + 1) * 128, :], [], [B("xstage", tb % 8, region="H")])

    x_dma(0, 0)
    x_dma(0, 1)

    bC = B("consts")
    pool(lambda e: e.memset(onesF[:], 1.0), [], [bC])
    pool(lambda e: e.affine_select(out=ident[:], in_=onesF[:], pattern=[[-1, 128]], compare_op=ALU.is_equal,
                                   fill=0.0, base=0, channel_multiplier=1), [bC], [bC])
    pool(lambda e: e.affine_select(out=mask[:], in_=onesF[:], pattern=[[1, 128]], compare_op=ALU.is_ge,
                                   fill=0.0, base=0, channel_multiplier=-1), [bC], [bC])
    pool(lambda e: e.memset(mhalf[:], -0.5), [], [bC])
    pool(lambda e: e.tensor_copy(out=identb[:], in_=ident[:]), [bC], [bC])
    pool(lambda e: e.memset(ocS[:, 0:1], 1.0 / 1024.0), [], [bC])
    pool(lambda e: e.memset(ocS[:, 1:2], 1.0), [], [bC])
    for m in range(2):
        pool(lambda e, m=m: e.iota(invcnt[:, m, :], [[1, ZH]], base=1, channel_multiplier=0,
                                    allow_small_or_imprecise_dtypes=True), [], [bC])
    for m, p0, w in ((0, 0, 2.0), (0, 64, 4.0), (1, 0, 8.0), (1, 64, 16.0)):
        ts(invcnt[p0:p0 + 64, m, :], invcnt[p0:p0 + 64, m, :], w, None, ALU.min, None, [bC], [bC])
    vec(lambda e: e.reciprocal(out=invcnt[:], in_=invcnt[:]), [bC], [bC])

    PV_G1, PV_G2, PV_CB, PV_LNG, PV_LNB, PV_PS = 0, 8, 16, 19, 22, 25
    bPV = B("pv")

    WIN_BLOCKS = [(0, 384), (384, 768), (768, 1152), (1152, 1536), (1536, 1792)]

    def load_mixer_w1(l):
        wv = w_in_d[l].rearrange("(k p) c -> p k c", p=128)
        for bi, (c0, c1) in enumerate(WIN_BLOCKS):
            dma("gpsimd", w_in[:, :, c0:c1], wv[:, :, c0:c1], [], [B("w_in", bi)])
        dma("gpsimd", w_pw[:], w_pw_d[l].rearrange("(k p) c -> p k c", p=128), [], [B("w_pw")])
        pool(lambda e: e.memset(wpl[:], 0.0), [], [B("wpl")])
        for g in range(4):
            m, h = g // 2, g % 2
            dma("gpsimd", wpl[64 * h:64 * h + 64, m, 64 * h:64 * h + 64], w_pool_d[l, g], [], [B("wpl")])

    def load_mixer_wout(l):
        dma("gpsimd", wo_a[:], w_out_d[l, 0:DA, :].rearrange("(k p) n -> p k n", p=128), [], [B("wo_a", region="R")])
        dma("gpsimd", wo_b[0:96], w_out_d[l, DA:DA + DB, :].rearrange("(k p) n -> p k n", p=96), [],
            [B("wo_b", region="R")])
        dma("gpsimd", wo_c[:], w_out_d[l, DA + DB:D, :].rearrange("(k p) n -> p k n", p=128), [],
            [B("wo_c", region="R")])

    def load_ring(l, fp, slot):
        wv = w_gu_d[l].rearrange("(k p) c -> p k c", p=128)
        dma("gpsimd", ring[slot][:, 0], wv[:, :, 256 * fp:256 * fp + 256], [], [B("ring", slot, 0, region="R")])
        dma("gpsimd", ring[slot][:, 1], wv[:, :, DFF + 256 * fp:DFF + 256 * fp + 256], [],
            [B("ring", slot, 1, region="R")])

    def load_wd(l, ci):
        f0, nf = CHUNKS[ci]
        dma("gpsimd", wd[:, 0:nf, :], w_down_d[l, f0 * 128:(f0 + nf) * 128, :].rearrange("(f p) n -> p f n", p=128),
            [], [B("wd", region="D")])

    def rms_tok_stats(j, dst_rtok):
        cols = slice(j * NT, (j + 1) * NT)
        bk, bb = next_bank()
        groups = [[] for _ in range(4)]
        rd = []
        for k in range(8):
            sq = sqr[k % 2]
            bsq = B("sqr", k % 2)
            act_op(sq[:], xT[:, k, cols], AF.Square, [B("xT", k, j)], [bsq])
            mmg = []
            for q in range(4):
                mmg.append((bk[:, q:q + 1], sq[:, q * 128:(q + 1) * 128], ocS[:, 0:1], k))
            def fn(e, mmg=mmg):
                ins = None
                for out, l, r, k in mmg:
                    ins = e.matmul(out, l, r, start=(k == 0 and out is mmg[0][0]), stop=(k == 7), skip_group_check=True)
                return ins
            P.add("tensor", fn, reads=[bsq, bC], writes=[bb])
        brt = B("rtok", id(dst_rtok), j)
        ts(dst_rtok[:, 4 * j:4 * j + 4], bk[:, 0:4], RMS_EPS, None, ALU.add, None, [bb], [brt])
        pool(lambda e, j=j: e.tensor_tensor(out=dst_rtok[:, 4 * j:4 * j + 4], in0=dst_rtok[:, 4 * j:4 * j + 4],
                                             in1=mhalf[:, 0:4], op=ALU.pow), [brt, bC], [brt])
        return brt

    def bcast_tok(src_cols, reads):
        bk, bb = next_bank()
        for q in range(4):
            d = dg[q % 2]
            bd = B("dg", q % 2)
            ts(d[:], ident[:], src_cols[q], None, ALU.mult, None, list(reads) + [bC], [bd])
            mm(bk[:, q * 128:(q + 1) * 128], [(onesF[:], d[:])], [bd, bC], [bb])
        return bk, bb

    def rms_apply(j, brt, src_rtok, gcol, dst, dst_bufs, dst_cols, out_f32=False):
        bk, bb = bcast_tok([src_rtok[:, 4 * j + q:4 * j + q + 1] for q in range(4)], [brt])
        cols = slice(j * NT, (j + 1) * NT)
        for k in range(8):
            stt(dst[:, k, dst_cols], xT[:, k, cols], gcol(k), bk[:], ALU.mult, ALU.mult,
                [B("xT", k, j), bb, bPV], [dst_bufs(k)])

    def pvcol(l, i):
        return pv[:, l, i:i + 1]

    load_mixer_w1(0)
    load_mixer_wout(0)

    def param_prep():
        for l in range(L):
            bvst = B("vst", region="A")
            srcs = [(norm1_g_d[l], 0, 8), (norm2_g_d[l], 8, 8), (conv_b_d[l], 16, 3), (conv_ln_g_d[l], 19, 3),
                    (conv_ln_b_d[l], 22, 3), (pool_scale_d[l], 25, 2)]
            for src, r0, nr in srcs:
                dma("sync", vst[r0:r0 + nr, :], src.rearrange("(r c) -> r c", c=128), [], [bvst])
            bk, bb = next_bank()
            transposes([(bk[:, 0:27], vst[0:27, :], ident[0:27, 0:27])], [bvst, bC], [bb])
            vec(lambda e, bk=bk, l=l: e.tensor_copy(out=pv[:, l, 0:27], in_=bk[:, 0:27]), [bb], [bPV])
            bcw = B("cwst", region="A")
            dma("sync", cwst[0:CW, :], conv_w_d[l], [], [bcw])
            bk, bb = next_bank()
            transposes([(bk[:, ct * 32:ct * 32 + CW], cwst[0:CW, ct * 128:(ct + 1) * 128], ident[0:CW, 0:CW])
                        for ct in range(3)], [bcw, bC], [bb])
            vec(lambda e, bk=bk, l=l: e.tensor_copy(out=cwT[:, l, :, :],
                                                     in_=bk[:, 0:96].rearrange("p (c k) -> p c k", k=32)[:, :, 0:CW]),
                [bb], [bPV])
        bvst = B("vst", region="A")
        dma("sync", vst[0:8, :], final_g_d.rearrange("(r c) -> r c", c=128), [], [bvst])
        bk, bb = next_bank()
        transposes([(bk[:, 0:8], vst[0:8, :], ident[0:8, 0:8])], [bvst, bC], [bb])
        vec(lambda e, bk=bk: e.tensor_copy(out=gFv[:], in_=bk[:, 0:8]), [bb], [bPV])

    pend = {}

    def x_tr(s, j):
        for k in range(8):
            bk, bb = next_bank()
            transposes([(bk[:, q * 128:(q + 1) * 128], xstage[(4 * j + q) % 8][:, k * 128:(k + 1) * 128], ident[:])
                        for q in range(4)], [B("xstage", (4 * j + q) % 8, region="H") for q in range(4)] + [bC], [bb])
            if k % 2 == 0:
                act_op(xT[:, k, j * NT:(j + 1) * NT], bk[:], AF.Copy, [bb], [B("xT", k, j)])
            else:
                vec(lambda e, bk=bk, k=k, j=j: e.tensor_copy(out=xT[:, k, j * NT:(j + 1) * NT], in_=bk[:]),
                    [bb], [B("xT", k, j)])
        pend[j] = rms_tok_stats(j, rtok)

    for s in range(nseq):
        if s == 0:
            for j in range(NJ):
                x_tr(0, j)
                if j + 2 < NJ:
                    x_dma(0, j + 2)
            param_prep()

        for l in range(L):
            P.new_phase(["A", "H", "D", "E"])
            bbs = B("BT")
            dma("sync", BT[0:96].rearrange("p h t -> p (h t)"),
                b_s_d[l].rearrange("h t -> (h t)").partition_broadcast(96), [], [bbs])
            bsgB = B("sgB")
            dma("sync", sgB[:, 0, :], sg_ln_g_d[l].partition_broadcast(128), [], [bsgB])
            dma("sync", sgB[:, 1, :], sg_ln_b_d[l].partition_broadcast(128), [], [bsgB])
            bdg = [B("diag", i, region=diag_tiles[i][1]) for i in range(3 * NPE)]
            bws = B("wsst", region="H")
            dma("sync", wsst[:], w_s_d[l].rearrange("h t s -> t h s"), [], [bws])
            bk, bb = next_bank()
            transposes([(bk[:, h * 128:(h + 1) * 128], wsst[:, h, :], ident[:]) for h in range(4)], [bws, bC], [bb])
            bwsT = B("wsT")
            for h in range(4):
                tt(wsT[:, h, :], bk[:, h * 128:(h + 1) * 128], mask[:], ALU.mult, [bb, bC], [bwsT])

            brts = [pend[j] for j in range(NJ)]

            ybufs = [ybuf, ybuf1]
            by = [[B("ybuf", i, ct, region=("A", "H")[i]) for ct in range(3)] for i in range(2)]
            bz = B("zc", region="D")
            pool(lambda e: e.memset(ybuf[:, :, 0:HALO], 0.0), [], by[0])
            pool(lambda e: e.memset(zc[:, :, 0:ZH], 0.0), [], [bz])
            bh = [B("hT", k, region="H") for k in range(8)]
            bug = B("ug", region="D")
            bvg = [B("vg", cq, region="H") for cq in range(4)]
            bacc = [B("acc", ct, region="A") for ct in range(3)]
            baccb = [B("accb", ct, region="A") for ct in range(3)]
            bsqb = [B("sqb", ct, region="A") for ct in range(3)]
            pslots = [(accb[:, i, :], baccb[i]) for i in range(3)] + [(sqb[:, i, :], bsqb[i]) for i in range(3)]
            pslot_rr = [0]
            bsil = B("sil", region="A")
            bya = B("ya", region="H")
            byb = B("yb", region="H")
            byc = B("yc", region="H")
            bvn = [B("vn", cq, region="H") for cq in range(4)]
            bst = B("bnst")
            bsA = B("sA", region="E")
            bsB = B("sB", region="E")
            bpp = B("pp", region="D")
            blt = B("lnt")
            W = ZH + NT

            def proj(c0, m, bi):
                bk, bb = next_bank()
                mm(bk[0:m, :], [(w_in[:, k, c0:c0 + m], hT[:, k, :]) for k in range(8)],
                   bh + [B("w_in", bi)], [bb])
                return bk, bb

            def fa(j):
                rms_apply(j, brts[j], rtok, lambda k, l=l: pvcol(l, PV_G1 + k), hT, lambda k: bh[k], slice(0, NT))

            def fb1(j):
                for h in range(4):
                    ku, bu = proj(2 * DA + 96 * h, 96, 2)
                    act_op(ug[0:96, h, :], ku[0:96, :], AF.Gelu, [bu], [bug])
                for cq in range(4):
                    bk, bb = next_bank()
                    mm(bk[:, 0:DB], [(hT[:, k, cq * 128:(cq + 1) * 128], w_in[:, k, 2 * DA + DB:2 * DA + 2 * DB])
                                     for k in range(8)], bh + [B("w_in", 3)], [bb])
                    act_op(vg[:, cq, :], bk[:, 0:DB], AF.Gelu, [bb], [bvg[cq]])

            def fb2(j):
                yb_ = ybufs[j % 2]
                byj = by[j % 2]
                if j > 0:
                    for ct in range(3):
                        act_op(yb_[:, ct, 0:HALO], ybufs[(j - 1) % 2][:, ct, NT:NT + HALO], AF.Copy,
                               [by[(j - 1) % 2][ct]], [byj[ct]])
                for m in range(2):
                    kz, bzz = proj(2 * DA + 2 * DB + 128 * m, 128, 4)
                    act_op(zc[:, m, ZH:ZH + NT], kz[:], AF.Copy, [bzz], [bz])
                glu = []
                for ct in range(3):
                    ka, ba = proj(ct * 128, 128, 0)
                    kg, bg = proj(DA + ct * 128, 128, 1)
                    act_op(yb_[:, ct, HALO:HALO + NT], kg[:], AF.Sigmoid, [bg], [byj[ct]])
                    glu.append((ct, ka, ba))
                return glu

            def glu_mult(j, ct, ka, ba):
                yb_ = ybufs[j % 2]
                tt(yb_[:, ct, HALO:HALO + NT], yb_[:, ct, HALO:HALO + NT], ka[:], ALU.mult,
                   [by[j % 2][ct], ba], [by[j % 2][ct]])

            def lnv_stats(j):
                for cq in range(4):
                    vec(lambda e, cq=cq: e.bn_stats(out=bnst[0][:, cq * 6:cq * 6 + 6], in_=vg[:, cq, :]), [bvg[cq]], [bst])
                for cq in range(4):
                    vec(lambda e, cq=cq: e.bn_aggr(out=bnst[1][:, 2 * cq:2 * cq + 2], in_=bnst[0][:, cq * 6:cq * 6 + 6]),
                        [bst], [bst])
                mvv = bnst[1][:, 0:8].rearrange("p (q t) -> p q t", t=2)
                ts(bnst[1][:, 8:12], mvv[:, :, 1], LN_EPS, None, ALU.add, None, [bst], [bst])
                pool(lambda e: e.tensor_tensor(out=bnst[1][:, 8:12], in0=bnst[1][:, 8:12], in1=mhalf[:, 0:4], op=ALU.pow),
                     [bst, bC], [bst])

            def lnv_apply(j):
                for cq in range(4):
                    stt(vg[:, cq, :], vg[:, cq, :], bnst[1][:, 2 * cq:2 * cq + 1], sgB[:, 0, :], ALU.subtract, ALU.mult,
                        [bvg[cq], bst, bsgB], [bvg[cq]])
                    stt(vn[:, cq, :], vg[:, cq, :], bnst[1][:, 8 + cq:9 + cq], sgB[:, 1, :], ALU.mult, ALU.add,
                        [bvg[cq], bst, bsgB], [bvn[cq]])

            def pool_branch(j):
                tt(sA[:, :, 2:W], zc[:, :, 2:W], zc[:, :, 1:W - 1], ALU.add, [bz], [bsA])
                tt(sB[:, :, 4:W], sA[:, :, 4:W], sA[:, :, 2:W - 2], ALU.add, [bsA], [bsB])

                def pool_out(src, p0, m, w):
                    stt(pp[p0:p0 + 64, m, :], src[p0:p0 + 64, m, ZH:W], 1.0 / w, zc[p0:p0 + 64, m, ZH:W],
                        ALU.mult, ALU.subtract, [bsA, bsB, bz], [bpp])
                    if j == 0:
                        tt(src[p0:p0 + 64, m, ZH:2 * ZH], src[p0:p0 + 64, m, ZH:2 * ZH], invcnt[p0:p0 + 64, m, :],
                           ALU.mult, [bsA, bsB, bC], [bsA, bsB])
                        tt(pp[p0:p0 + 64, m, 0:ZH], src[p0:p0 + 64, m, ZH:2 * ZH], zc[p0:p0 + 64, m, ZH:2 * ZH],
                           ALU.subtract, [bsA, bsB, bz], [bpp])
                pool_out(sA, 0, 0, 2)
                pool_out(sB, 64, 0, 4)
                tt(sA[:, 1, 8:W], sB[:, 1, 8:W], sB[:, 1, 4:W - 4], ALU.add, [bsB], [bsA])
                pool_out(sA, 0, 1, 8)
                tt(sB[64:128, 1, 16:W], sA[64:128, 1, 16:W], sA[64:128, 1, 8:W - 8], ALU.add, [bsA], [bsB])
                pool_out(sB, 64, 1, 16)
                if j + 1 < NJ:
                    act_op(zc[:, :, 0:ZH], zc[:, :, NT:NT + ZH], AF.Copy, [bz], [bz])

            convps = {}

            def tap30(j):
                for ct in range(3):
                    act_op(acc[:, ct, :], ybufs[j % 2][:, ct, HALO:HALO + NT], AF.Identity, [by[j % 2][ct], bPV],
                           [bacc[ct]], scale=cwT[:, l, ct, CW - 1:CW], bias=pvcol(l, PV_CB + ct))

            def tap_ops(j):
                a_ = [(ct, k) for k in range(NPE, NPE + NACT) for ct in range(3)]
                d_ = [(ct, k) for k in range(NPE + NACT, CW - 1) for ct in range(3)]
                out_ = []
                for i in range(max(len(a_), len(d_))):
                    if i < len(d_):
                        out_.append(d_[i])
                    if i < len(a_):
                        out_.append(a_[i])
                return out_

            def tap(j, ct, k):
                if k >= NPE + NACT:
                    stt(acc[:, ct, :], ybufs[j % 2][:, ct, k:k + NT], cwT[:, l, ct, k:k + 1], acc[:, ct, :],
                        ALU.mult, ALU.add, [by[j % 2][ct], bPV, bacc[ct]], [bacc[ct]])
                    return
                slot, bsl = pslots[pslot_rr[0] % len(pslots)]
                pslot_rr[0] += 1
                act_op(slot, ybufs[j % 2][:, ct, k:k + NT], AF.Identity, [by[j % 2][ct], bPV], [bsl],
                       scale=cwT[:, l, ct, k:k + 1])
                bk, bb = convps[j][ct]
                mm(bk[:], [(identb[:], slot)], [bsl, bC], [bb], first=False, last=(k == NPE + NACT - 1))

            def conv_pe(j):
                convps[j] = []
                for ct in range(3):
                    bk, bb = next_bank()
                    mm(bk[:], [(diag_tiles[ct * NPE + k][0][:], ybufs[j % 2][:, ct, k:k + NT]) for k in range(NPE)],
                       [by[j % 2][ct]] + bdg[ct * NPE:(ct + 1) * NPE], [bb], first=True, last=(NACT == 0))
                    convps[j].append((bk, bb))

            def acc_add(j):
                for ct in range(3):
                    tt(acc[:, ct, :], acc[:, ct, :], convps[j][ct][0][:], ALU.add, [bacc[ct], convps[j][ct][1]],
                       [bacc[ct]])

            fa(0)
            for ct in range(3):
                for k in range(NPE):
                    ts(diag_tiles[ct * NPE + k][0][:], ident[:], cwT[:, l, ct, k:k + 1], None, ALU.mult, None,
                       [bC, bPV], [bdg[ct * NPE + k]])
            fb1(0)
            for g_ in fb2(0):
                glu_mult(0, *g_)
            lnv_stats(0)
            lnv_apply(0)
            pool_branch(0)
            tap30(0)
            conv_pe(0)
            for ct, k in tap_ops(0):
                tap(0, ct, k)

            brts2 = []
            for j in range(NJ):
                cols = slice(j * NT, (j + 1) * NT)
                nj = j + 1 if j + 1 < NJ else None
                acc_add(j)
                for ct in range(3):
                    act_op(accb[:, ct, :], acc[:, ct, :], AF.Copy, [bacc[ct]], [baccb[ct]])
                    act_op(sqb[:, ct, :], acc[:, ct, :], AF.Square, [bacc[ct]], [bsqb[ct]])
                if nj is not None:
                    fa(nj)
                pmb = []
                for cq in range(4):
                    bk2, bb2 = next_bank()
                    mms([(bk2[0:96, h * 128:(h + 1) * 128], [(vn[:, cq, 96 * h:96 * h + 96], wsT[:, h, :])])
                         for h in range(4)], [bvn[cq], bwsT], [bb2])
                    pmb.append((bk2, bb2))
                for m in range(2):
                    bk2, bb2 = next_bank()
                    mm(bk2[:], [(wpl[:, m, :], pp[:, m, :])], [bpp, B("wpl")], [bb2])
                    act_op(yc[:, m, :], bk2[:], AF.Identity, [bb2, bPV], [byc], scale=pvcol(l, PV_PS + m))
                bk, bb = next_bank()
                groups = []
                for q in range(4):
                    groups.append((bk[:, 2 * q:2 * q + 1],
                                   [(accb[:, ct, q * 128:(q + 1) * 128], ocS[:, 1:2]) for ct in range(3)]))
                    groups.append((bk[:, 2 * q + 1:2 * q + 2],
                                   [(sqb[:, ct, q * 128:(q + 1) * 128], ocS[:, 1:2]) for ct in range(3)]))
                mms(groups, baccb + bsqb + [bC], [bb])
                ts(rtok2[:, 0:8], bk[:, 0:8], 1.0 / DA, None, ALU.mult, None, [bb], [blt])
                me = rtok2[:, 0:8].rearrange("p (q t) -> p q t", t=2)
                tt(rtok2[:, 8:12], me[:, :, 0], me[:, :, 0], ALU.mult, [blt], [blt])
                tt(rtok2[:, 8:12], me[:, :, 1], rtok2[:, 8:12], ALU.subtract, [blt], [blt])
                ts(rtok2[:, 8:12], rtok2[:, 8:12], LN_EPS, None, ALU.add, None, [blt], [blt])
                pool(lambda e: e.tensor_tensor(out=rtok2[:, 8:12], in0=rtok2[:, 8:12], in1=mhalf[:, 0:4], op=ALU.pow),
                     [blt, bC], [blt])
                for cq in range(4):
                    bk2, bb2 = pmb[cq]
                    t = tlnv[cq % 2]
                    bt = (bsA, bsB)[cq % 2]
                    tt(t[0:96].rearrange("p (h t) -> p h t", t=128), bk2[0:96, :].rearrange("p (h t) -> p h t", t=128),
                       BT[0:96], ALU.add, [bb2, bbs], [bt])
                    tt(yb[0:96, :, cq * 128:(cq + 1) * 128], ug[0:96, :, cq * 128:(cq + 1) * 128],
                       t[0:96].rearrange("p (h t) -> p h t", t=128), ALU.mult, [bug, bt], [byb])
                if nj is not None:
                    fb1(nj)
                stt(rtok2[:, 12:16], me[:, :, 0], -1.0, rtok2[:, 8:12], ALU.mult, ALU.mult, [blt], [blt])
                krs, brs = bcast_tok([rtok2[:, 8 + q:9 + q] for q in range(4)], [blt])
                knm, bnm = bcast_tok([rtok2[:, 12 + q:13 + q] for q in range(4)], [blt])
                for ct in range(3):
                    t = tlnv[ct % 2]
                    bt = (bsA, bsB)[ct % 2]
                    tt(t, acc[:, ct, :], krs[:], ALU.mult, [bacc[ct], brs], [bt])
                    tt(t, t, knm[:], ALU.add, [bt, bnm], [bt])
                    act_op(sil[:, ct, :], t, AF.Silu, [bt, bPV], [bsil],
                           scale=pvcol(l, PV_LNG + ct), bias=pvcol(l, PV_LNB + ct))
                glu = []
                if nj is not None:
                    lnv_stats(nj)
                    glu = fb2(nj)
                    lnv_apply(nj)
                    for g_ in glu:
                        glu_mult(nj, *g_)
                for co in range(3):
                    bk2, bb2 = next_bank()
                    mm(bk2[:], [(w_pw[:, ci, co * 128:(co + 1) * 128], sil[:, ci, :]) for ci in range(3)],
                       [bsil, B("w_pw")], [bb2])
                    act_op(ya[:, co, :], bk2[:], AF.Copy, [bb2], [bya])
                if nj is not None:
                    pool_branch(nj)
                    tap30(nj)
                    conv_pe(nj)
                ptaps = tap_ops(nj) if nj is not None else []
                per = (len(ptaps) + 7) // 8
                for n in range(8):
                    ncol = slice(n * 128, (n + 1) * 128)
                    bk2, bb2 = next_bank()
                    pairs = [(wo_b[0:96, h, ncol], yb[0:96, h, :]) for h in range(4)]
                    pairs += [(wo_c[:, m, ncol], yc[:, m, :]) for m in range(2)]
                    pairs += [(wo_a[:, i, ncol], ya[:, i, :]) for i in range(3)]
                    mm(bk2[:], pairs, [bya, byb, byc, B("wo_a", region="R"), B("wo_b", region="R"),
                                       B("wo_c", region="R")], [bb2])
                    for ct, k in ptaps[n * per:(n + 1) * per]:
                        tap(nj, ct, k)
                    tt(xT[:, n, cols], xT[:, n, cols], bk2[:], ALU.add, [B("xT", n, j), bb2], [B("xT", n, j)])
                brts2.append(rms_tok_stats(j, rtokB))

            P.new_phase(["A", "H", "D", "E", "R"])
            nfp = NF // 2
            for fp in range(min(NRING, nfp)):
                load_ring(l, fp, fp % NRING)
            load_wd(l, 0)
            bh2 = [[B("h2T", k, j, region="H") for k in range(8)] for j in range(NJ)]
            nxt = None
            if l + 1 < L:
                nxt = l + 1
            elif s + 1 < nseq:
                nxt = 0
            if nxt is not None:
                load_mixer_w1(nxt)
            for ci, (f0, nf) in enumerate(CHUNKS):
                bact = [[B("act", fi, j, region="A") for j in range(NJ)] for fi in range(nf)]
                for fi in range(nf):
                    f = f0 + fi
                    fp, half = f // 2, f % 2
                    slot = fp % NRING
                    fc = slice(half * 128, half * 128 + 128)
                    for j in range(NJ):
                        cols = slice(j * NT, (j + 1) * NT)
                        if ci == 0 and fi == 0:
                            rms_apply(j, brts2[j], rtokB, lambda k, l=l: pvcol(l, PV_G2 + k), h2T,
                                      lambda k, j=j: bh2[j][k], slice(j * NT, (j + 1) * NT))
                        kg, bg = next_bank()
                        mm(kg[:], [(ring[slot][:, 0, k, fc], h2T[:, k, cols]) for k in range(8)],
                           bh2[j] + [B("ring", slot, 0, region="R")], [bg])
                        ku, bu = next_bank()
                        mm(ku[:], [(ring[slot][:, 1, k, fc], h2T[:, k, cols]) for k in range(8)],
                           bh2[j] + [B("ring", slot, 1, region="R")], [bu])
                        t = tlnv[(fi * NJ + j) % 2]
                        bt = B(("sA", "sB")[(fi * NJ + j) % 2], region="E")
                        act_op(t, kg[:], AF.Silu, [bg], [bt])
                        tt(act[:, fi, cols], t, ku[:], ALU.mult, [bt, bu], [bact[fi][j]])
                    if half == 1 and fp + NRING < nfp:
                        load_ring(l, fp + NRING, slot)
                last = ci + 1 == len(CHUNKS)
                order = [(n, j) for j in range(NJ) for n in range(8)] if last else \
                        [(n, j) for n in range(8) for j in range(NJ)]
                for n, j in order:
                    ncol = slice(n * 128, (n + 1) * 128)
                    cols = slice(j * NT, (j + 1) * NT)
                    bk, bb = next_bank()
                    mm(bk[:], [(wd[:, fi, ncol], act[:, fi, cols]) for fi in range(nf)],
                       [bact[fi][j] for fi in range(nf)] + [B("wd", region="D")], [bb])
                    tt(xT[:, n, cols], xT[:, n, cols], bk[:], ALU.add, [B("xT", n, j), bb], [B("xT", n, j)])
                    if last and n == 7:
                        pend[j] = rms_tok_stats(j, rtok)
                if ci + 1 < len(CHUNKS):
                    load_wd(l, ci + 1)
            if nxt is not None:
                P.new_phase(["R"])
                load_mixer_wout(nxt)

        P.new_phase(["A", "H", "D", "E"])
        if s + 1 < nseq:
            x_dma(s + 1, 0)
            x_dma(s + 1, 1)
        byn = [B("yn", k, region="A") for k in range(8)]

        def fin_apply(j):
            rms_apply(j, pend[j], rtok, lambda k: gFv[:, k:k + 1], yn, lambda k: byn[k], slice(0, NT))

        def fin_store(j):
            for q in range(4):
                tb = j * 4 + q
                os_ = ostage[tb % 2]
                bos = B("ostage", tb % 2, region="A")
                for hf in range(2):
                    bk, bb = next_bank()
                    transposes([(bk[:, kk * 128:(kk + 1) * 128], yn[:, hf * 4 + kk, q * 128:(q + 1) * 128], ident[:])
                                for kk in range(4)], byn[hf * 4:hf * 4 + 4] + [bC], [bb])
                    if hf == 0:
                        act_op(os_[:, 0:512], bk[:], AF.Copy, [bb], [bos])
                    else:
                        vec(lambda e, bk=bk, os_=os_: e.tensor_copy(out=os_[:, 512:1024], in_=bk[:]), [bb], [bos])
                dma("sync", out_d[s, tb * 128:(tb + 1) * 128, :], os_[:], [bos], [], final=True)

        fin_apply(0)
        for j in range(NJ):
            fin_store(j)
            if j + 1 < NJ:
                fin_apply(j + 1)
            if s + 1 < nseq:
                x_tr(s + 1, j)
                if j + 2 < NJ:
                    x_dma(s + 1, j + 2)

    P.emit(nc)
    return nc


_NC_CACHE = {}


def _get_nc(nseq, depth):
    key = (nseq, depth)
    if key not in _NC_CACHE:
        _NC_CACHE[key] = build_program(nseq=nseq, depth=depth)
    return _NC_CACHE[key]


_WNAMES = ["norm1_g", "w_in", "conv_w", "conv_b", "conv_ln_g", "conv_ln_b", "w_pw", "sg_ln_g", "sg_ln_b",
           "w_s", "b_s", "w_pool", "pool_scale", "w_out", "norm2_g", "w_gate_up", "w_down", "final_g"]


def kernel(**inputs):
    x = np.ascontiguousarray(np.asarray(inputs["x"], dtype=np.float32))
    bsz = x.shape[0]
    nseq = bsz // N_CORES
    depth = int(np.asarray(inputs["w_in"]).shape[0])
    nc = _get_nc(nseq, depth)
    ws = {n: np.ascontiguousarray(np.asarray(inputs[n], dtype=np.float32)) for n in _WNAMES}
    in_maps = []
    for c in range(N_CORES):
        m = dict(ws)
        m["x"] = x[c * nseq:(c + 1) * nseq]
        in_maps.append(m)
    res = run_bass_kernel_spmd(nc, in_maps, core_ids=list(range(N_CORES)))
    return np.concatenate([np.asarray(r["out"]) for r in res.results], axis=0).astype(np.float32)
```

```python
import numpy as np
import concourse.bass as bass
import concourse.mybir as mybir
from concourse.bass_utils import run_bass_kernel_spmd

F32 = mybir.dt.float32
BF16 = mybir.dt.bfloat16
AF = mybir.ActivationFunctionType
ALU = mybir.AluOpType

D = 1024
S = 2048
NT = 512
NJ = S // NT
DA = 384
DB = 384
DC = 256
DIN = 1792
DFF = 2816
NF = DFF // 128
CW = 31
HALO = CW - 1
ZH = 16
RMS_EPS = 1e-6
LN_EPS = 1e-5
N_CORES = 8
CHUNKS = [(0, 6), (6, 6), (12, 6), (18, 4)]
NRING = 3
NACT = 6
NPE = 18


class Buf:
    __slots__ = ("name", "w", "r", "region")

    def __init__(self, name, region=None):
        self.name = name
        self.w = None
        self.r = []
        self.region = region


class Eng:
    def __init__(self, name):
        self.name = name
        self.ops = []
        self.gen = 0
        self.cnt = 0
        self.waited = {}
        self.dma_rr = 0
        self.dma_vals = {}

    @property
    def semkey(self):
        return ("e", self.name, self.gen)


SEM_LIMIT = 30000
N_DMA_SEMS = {"gpsimd": 20, "sync": 12, "scalar": 4}


class Prog:
    def __init__(self):
        self.eng = {n: Eng(n) for n in ("tensor", "vector", "scalar", "gpsimd", "sync")}
        self.bufs = {}
        self.fence = {}
        self.region_last = {}
        self.region_dma = {}
        self.final_tokens = []

    def buf(self, *key, region=None):
        b = self.bufs.get(key)
        if b is None:
            b = Buf(key, region)
            self.bufs[key] = b
        return b

    def new_phase(self, regions):
        for r in regions:
            toks = list(self.region_last.get(r, {}).values()) + list(self.region_dma.get(r, []))
            self.fence[r] = toks
            self.region_dma[r] = []

    def add(self, engname, fn, reads=(), writes=(), dma=False, final=False):
        e = self.eng[engname]
        deps = {}

        def need(tok):
            if tok is None:
                return
            k, v = tok
            if deps.get(k, 0) < v:
                deps[k] = v

        for b in reads:
            need(b.w)
            if b.region is not None:
                for t in self.fence.get(b.region, ()):
                    need(t)
        for b in writes:
            need(b.w)
            for t in b.r:
                need(t)
            if b.region is not None:
                for t in self.fence.get(b.region, ()):
                    need(t)
        if dma:
            n = N_DMA_SEMS[engname]
            idx = e.dma_rr % n
            e.dma_rr += 1
            k = ("d", engname, idx)
            prev = e.dma_vals.get(k, 0)
            if prev:
                need((k, prev))
            tok = (k, prev + 16)
            e.dma_vals[k] = prev + 16
        else:
            if e.cnt >= SEM_LIMIT:
                e.gen += 1
                e.cnt = 0
            e.cnt += 1
            tok = (e.semkey, e.cnt)
        waits = []
        for k, v in deps.items():
            if engname == "tensor" and k == e.semkey:
                continue
            if e.waited.get(k, 0) >= v:
                continue
            e.waited[k] = v
            waits.append((k, v))
        e.ops.append((fn, waits, tok, dma))
        ws = set(id(b) for b in writes)
        for b in writes:
            b.w = tok
            b.r = []
        for b in reads:
            if id(b) not in ws:
                b.r.append(tok)
        for b in list(reads) + list(writes):
            if b.region is not None:
                if dma:
                    self.region_dma.setdefault(b.region, []).append(tok)
                else:
                    self.region_last.setdefault(b.region, {})[e.semkey[:2]] = tok
        if final:
            self.final_tokens.append(tok)
        return tok

    def emit(self, nc):
        keys = set()
        for e in self.eng.values():
            for fn, waits, tok, dma in e.ops:
                keys.add(tok[0])
                for k, v in waits:
                    keys.add(k)
        sems = {}
        for i, k in enumerate(sorted(keys, key=str)):
            sems[k] = nc.alloc_semaphore("s%d" % i)
        finals = self.final_tokens
        with nc.Block() as block:
            def mk(e):
                def body(eh):
                    for fn, waits, tok, dma in e.ops:
                        for k, v in waits:
                            eh.wait_ge(sems[k], v)
                        ins = fn(eh)
                        ins.then_inc(sems[tok[0]], 16 if dma else 1)
                    if e.name == "sync":
                        for k, v in finals:
                            eh.wait_ge(sems[k], v)
                return body
            for name, e in self.eng.items():
                if not e.ops and name != "sync":
                    continue
                getattr(block, name)(mk(e))


def build_program(nseq=2, depth=2):
    nc = bass.Bass("TRN2", target_bir_lowering=False)
    P = Prog()
    L = depth

    def din(name, shape):
        return nc.dram_tensor(name, list(shape), F32, kind="ExternalInput").ap()

    x_d = din("x", [nseq, S, D])
    norm1_g_d = din("norm1_g", [L, D])
    w_in_d = din("w_in", [L, D, DIN])
    conv_w_d = din("conv_w", [L, CW, DA])
    conv_b_d = din("conv_b", [L, DA])
    conv_ln_g_d = din("conv_ln_g", [L, DA])
    conv_ln_b_d = din("conv_ln_b", [L, DA])
    w_pw_d = din("w_pw", [L, DA, DA])
    sg_ln_g_d = din("sg_ln_g", [L, DB])
    sg_ln_b_d = din("sg_ln_b", [L, DB])
    w_s_d = din("w_s", [L, 4, 128, 128])
    b_s_d = din("b_s", [L, 4, 128])
    w_pool_d = din("w_pool", [L, 4, 64, 64])
    pool_scale_d = din("pool_scale", [L, DC])
    w_out_d = din("w_out", [L, D, D])
    norm2_g_d = din("norm2_g", [L, D])
    w_gu_d = din("w_gate_up", [L, D, 2 * DFF])
    w_down_d = din("w_down", [L, DFF, D])
    final_g_d = din("final_g", [D])
    out_d = nc.dram_tensor("out", [nseq, S, D], F32, kind="ExternalOutput").ap()

    base = (nc.sbuf_base + 31) // 32 * 32
    top = nc.sbuf_top
    cur = [base]

    def alloc(name, shape, dt, at=None):
        nbytes = int(np.prod(shape[1:])) * (4 if dt == F32 else 2)
        nbytes = (nbytes + 31) // 32 * 32
        if at is None:
            off = cur[0]
            cur[0] += nbytes
        else:
            off = at
        assert off + nbytes <= top, (name, off, nbytes, top)
        return nc.alloc_sbuf_tensor_at(name, list(shape), dt, offset=off), off, nbytes

    class Region:
        def __init__(self, name, size):
            self.name = name
            self.size = size
            self.base = cur[0]
            cur[0] += size
            assert cur[0] <= top, ("region overflow", name, cur[0], top)

        def carve(self):
            return Carver(self)

    class Carver:
        def __init__(self, reg):
            self.reg = reg
            self.off = reg.base

        def alloc(self, name, shape, dt):
            t, off, nb = alloc(name, shape, dt, at=self.off)
            self.off += nb
            assert self.off <= self.reg.base + self.reg.size, ("carve overflow", self.reg.name, name)
            return t

    ident = alloc("ident", [128, 128], F32)[0]
    onesF = alloc("onesF", [128, 128], F32)[0]
    mask = alloc("mask", [128, 128], F32)[0]
    identb = alloc("identb", [128, 128], BF16)[0]
    mhalf = alloc("mhalf", [128, 16], F32)[0]
    invcnt = alloc("invcnt", [128, 2, 16], F32)[0]
    ocS = alloc("ocS", [128, 2], BF16)[0]
    pv = alloc("pv", [128, L, 32], F32)[0]
    gFv = alloc("gFv", [128, 8], F32)[0]
    cwT = alloc("cwT", [128, L, 3, CW], F32)[0]
    BT = alloc("BT", [128, 4, 128], F32)[0]
    sgB = alloc("sgB", [128, 2, DB], F32)[0]
    sqr = [alloc("sqr%d" % i, [128, NT], BF16)[0] for i in range(2)]
    rtok = alloc("rtok", [128, 16], F32)[0]
    rtok2 = alloc("rtok2", [128, 16], F32)[0]
    rtokB = alloc("rtokB", [128, 16], F32)[0]
    dg = [alloc("dg%d" % i, [128, 128], F32)[0] for i in range(2)]
    bnst = [alloc("bnst%d" % i, [128, 32], F32)[0] for i in range(2)]
    xT = alloc("xT", [128, 8, S], F32)[0]
    w_in = alloc("w_in_sb", [128, 8, DIN], BF16)[0]
    w_pw = alloc("w_pw_sb", [128, 3, DA], BF16)[0]
    wsT = alloc("wsT", [128, 4, 128], BF16)[0]
    wpl = alloc("wpl", [128, 2, 128], BF16)[0]
    RR = Region("R", 24576)
    RH = Region("H", 32768)
    RA = Region("A", 24576)
    RD = Region("D", 12288)
    RE = Region("E", (top - cur[0]) // 32 * 32)

    c = RR.carve()
    wo_a = c.alloc("wo_a", [128, 3, D], BF16)
    wo_b = c.alloc("wo_b", [128, 4, D], BF16)
    wo_c = c.alloc("wo_c", [128, 2, D], BF16)
    c = RR.carve()
    ring = [c.alloc("ring%d" % i, [128, 2, 8, 256], BF16) for i in range(NRING)]
    c = RH.carve()
    h2T = c.alloc("h2T", [128, 8, S], BF16)
    c = RH.carve()
    hT = c.alloc("hT", [128, 8, NT], BF16)
    ya = c.alloc("ya", [128, 3, NT], BF16)
    yb = c.alloc("yb", [128, 4, NT], BF16)
    yc = c.alloc("yc", [128, 2, NT], BF16)
    vg = c.alloc("vg", [128, 4, DB], F32)
    vn = c.alloc("vn", [128, 4, DB], BF16)
    wsst = c.alloc("wsst", [128, 4, 128], F32)
    ybuf1 = c.alloc("ybuf1", [128, 3, HALO + NT], BF16)
    c = RA.carve()
    act = c.alloc("act", [128, 6, S], BF16)
    c = RH.carve()
    xstage = [c.alloc("xstage%d" % i, [128, D], F32) for i in range(8)]
    c = RA.carve()
    vst = c.alloc("vst", [32, 128], F32)
    cwst = c.alloc("cwst", [32, DA], F32)
    c = RA.carve()
    yn = c.alloc("yn", [128, 8, NT], F32)
    ostage = [c.alloc("ostage%d" % i, [128, D], F32) for i in range(2)]
    c = RA.carve()
    ybuf = c.alloc("ybuf", [128, 3, HALO + NT], BF16)
    acc = c.alloc("acc", [128, 3, NT], F32)
    accb = c.alloc("accb", [128, 3, NT], BF16)
    sqb = c.alloc("sqb", [128, 3, NT], BF16)
    sil = c.alloc("sil", [128, 3, NT], BF16)
    diag_tiles = []
    cA = c
    c = RD.carve()
    wd = c.alloc("wd", [128, 6, D], BF16)
    c = RD.carve()
    ug = c.alloc("ug", [128, 4, NT], BF16)
    zc = c.alloc("zc", [128, 2, ZH + NT], F32)
    pp = c.alloc("pp", [128, 2, NT], BF16)
    cD = c
    c = RE.carve()
    sA = c.alloc("sA", [128, 2, ZH + NT], F32)
    sB = c.alloc("sB", [128, 2, ZH + NT], F32)
    tlnv = [sA[:, 0, 0:NT], sB[:, 0, 0:NT]]
    cE = c
    cR = RR.carve()
    cR.off = RR.base + 18432
    for cc, rn in ((cR, "R"), (cA, "A"), (cD, "D"), (cE, "E")):
        while cc.off + 256 <= cc.reg.base + cc.reg.size and len(diag_tiles) < 3 * NPE:
            diag_tiles.append((cc.alloc("diag%d" % len(diag_tiles), [128, 128], BF16), rn))
    assert len(diag_tiles) == 3 * NPE, len(diag_tiles)

    banks = [nc.alloc_psum_tensor("bank%d" % i, [128, 512], F32) for i in range(8)]
    bank_rr = [0]

    def next_bank():
        for _ in range(8):
            i = bank_rr[0] % 8
            bank_rr[0] += 1
            b_ = P.buf("bank", i)
            if b_.w is None or len(b_.r) > 0:
                return banks[i], b_
        raise AssertionError("all PSUM banks are held by writes whose readers were not emitted yet")

    B = P.buf

    def dma(q, out, in_, reads=(), writes=(), final=False, nonc=False):
        def fn(e, out=out, in_=in_):
            if nonc:
                return e.dma_start(out=out, in_=in_, allow_slow_non_contiguous=True)
            return e.dma_start(out=out, in_=in_)
        return P.add(q, fn, reads=reads, writes=writes, dma=True, final=final)

    def mm(out, pairs, reads, writes, first=True, last=True):
        def fn(e, out=out, pairs=pairs, first=first, last=last):
            n = len(pairs)
            ins = None
            for i, (l, r) in enumerate(pairs):
                ins = e.matmul(out, l, r, start=(first and i == 0), stop=(last and i == n - 1))
            return ins
        return P.add("tensor", fn, reads=reads, writes=writes)

    def mms(groups, reads, writes):
        def fn(e, groups=groups):
            ins = None
            for out, pairs in groups:
                n = len(pairs)
                for i, (l, r) in enumerate(pairs):
                    ins = e.matmul(out, l, r, start=(i == 0), stop=(i == n - 1))
            return ins
        return P.add("tensor", fn, reads=reads, writes=writes)

    def transposes(items, reads, writes):
        def fn(e, items=items):
            ins = None
            for out, in_, idn in items:
                ins = e.transpose(out, in_, idn)
            return ins
        return P.add("tensor", fn, reads=reads, writes=writes)

    def act_op(out, in_, func, reads, writes, scale=None, bias=None):
        def fn(e, out=out, in_=in_, func=func, scale=scale, bias=bias):
            kw = {}
            if scale is not None:
                kw["scale"] = scale
            if bias is not None:
                kw["bias"] = bias
            return e.activation(out=out, in_=in_, func=func, **kw)
        return P.add("scalar", fn, reads=reads, writes=writes)

    def vec(fn, reads, writes):
        return P.add("vector", fn, reads=reads, writes=writes)

    def pool(fn, reads, writes):
        return P.add("gpsimd", fn, reads=reads, writes=writes)

    def tt(out, in0, in1, op, reads, writes, eng="vector"):
        return P.add(eng, lambda e, out=out, in0=in0, in1=in1, op=op: e.tensor_tensor(out=out, in0=in0, in1=in1, op=op),
                     reads=reads, writes=writes)

    def ts(out, in0, s1, s2, op0, op1, reads, writes, eng="vector"):
        def fn(e, out=out, in0=in0, s1=s1, s2=s2, op0=op0, op1=op1):
            if op1 is None:
                return e.tensor_scalar(out=out, in0=in0, scalar1=s1, scalar2=None, op0=op0)
            return e.tensor_scalar(out=out, in0=in0, scalar1=s1, scalar2=s2, op0=op0, op1=op1)
        return P.add(eng, fn, reads=reads, writes=writes)

    def stt(out, in0, scalar, in1, op0, op1, reads, writes):
        return vec(lambda e, out=out, in0=in0, scalar=scalar, in1=in1, op0=op0, op1=op1:
                   e.scalar_tensor_tensor(out=out, in0=in0, scalar=scalar, in1=in1, op0=op0, op1=op1),
                   reads=reads, writes=writes)

    def x_dma(s, j):
        for q in range(4):
            tb = 4 * j + q
            dma("sync", xstage[tb % 8][:], x_d[s, tb * 128:(tb + 1) * 128, :], [], [B("xstage", tb % 8, region="H")])

    x_dma(0, 0)
    x_dma(0, 1)

    bC = B("consts")
    pool(lambda e: e.memset(onesF[:], 1.0), [], [bC])
    pool(lambda e: e.affine_select(out=ident[:], in_=onesF[:], pattern=[[-1, 128]], compare_op=ALU.is_equal,
                                   fill=0.0, base=0, channel_multiplier=1), [bC], [bC])
    pool(lambda e: e.affine_select(out=mask[:], in_=onesF[:], pattern=[[1, 128]], compare_op=ALU.is_ge,
                                   fill=0.0, base=0, channel_multiplier=-1), [bC], [bC])
    pool(lambda e: e.memset(mhalf[:], -0.5), [], [bC])
    pool(lambda e: e.tensor_copy(out=identb[:], in_=ident[:]), [bC], [bC])
    pool(lambda e: e.memset(ocS[:, 0:1], 1.0 / 1024.0), [], [bC])
    pool(lambda e: e.memset(ocS[:, 1:2], 1.0), [], [bC])
    for m in range(2):
        pool(lambda e, m=m: e.iota(invcnt[:, m, :], [[1, ZH]], base=1, channel_multiplier=0,
                                    allow_small_or_imprecise_dtypes=True), [], [bC])
    for m, p0, w in ((0, 0, 2.0), (0, 64, 4.0), (1, 0, 8.0), (1, 64, 16.0)):
        ts(invcnt[p0:p0 + 64, m, :], invcnt[p0:p0 + 64, m, :], w, None, ALU.min, None, [bC], [bC])
    vec(lambda e: e.reciprocal(out=invcnt[:], in_=invcnt[:]), [bC], [bC])

    PV_G1, PV_G2, PV_CB, PV_LNG, PV_LNB, PV_PS = 0, 8, 16, 19, 22, 25
    bPV = B("pv")

    WIN_BLOCKS = [(0, 384), (384, 768), (768, 1152), (1152, 1536), (1536, 1792)]

    def load_mixer_w1(l):
        wv = w_in_d[l].rearrange("(k p) c -> p k c", p=128)
        for bi, (c0, c1) in enumerate(WIN_BLOCKS):
            dma("gpsimd", w_in[:, :, c0:c1], wv[:, :, c0:c1], [], [B("w_in", bi)])
        dma("gpsimd", w_pw[:], w_pw_d[l].rearrange("(k p) c -> p k c", p=128), [], [B("w_pw")])
        pool(lambda e: e.memset(wpl[:], 0.0), [], [B("wpl")])
        for g in range(4):
            m, h = g // 2, g % 2
            dma("gpsimd", wpl[64 * h:64 * h + 64, m, 64 * h:64 * h + 64], w_pool_d[l, g], [], [B("wpl")])

    def load_mixer_wout(l):
        dma("gpsimd", wo_a[:], w_out_d[l, 0:DA, :].rearrange("(k p) n -> p k n", p=128), [], [B("wo_a", region="R")])
        dma("gpsimd", wo_b[0:96], w_out_d[l, DA:DA + DB, :].rearrange("(k p) n -> p k n", p=96), [],
            [B("wo_b", region="R")])
        dma("gpsimd", wo_c[:], w_out_d[l, DA + DB:D, :].rearrange("(k p) n -> p k n", p=128), [],
            [B("wo_c", region="R")])

    def load_ring(l, fp, slot):
        wv = w_gu_d[l].rearrange("(k p) c -> p k c", p=128)
        dma("gpsimd", ring[slot][:, 0], wv[:, :, 256 * fp:256 * fp + 256], [], [B("ring", slot, 0, region="R")])
        dma("gpsimd", ring[slot][:, 1], wv[:, :, DFF + 256 * fp:DFF + 256 * fp + 256], [],
            [B("ring", slot, 1, region="R")])

    def load_wd(l, ci):
        f0, nf = CHUNKS[ci]
        dma("gpsimd", wd[:, 0:nf, :], w_down_d[l, f0 * 128:(f0 + nf) * 128, :].rearrange("(f p) n -> p f n", p=128),
            [], [B("wd", region="D")])

    def rms_tok_stats(j, dst_rtok):
        cols = slice(j * NT, (j + 1) * NT)
        bk, bb = next_bank()
        groups = [[] for _ in range(4)]
        rd = []
        for k in range(8):
            sq = sqr[k % 2]
            bsq = B("sqr", k % 2)
            act_op(sq[:], xT[:, k, cols], AF.Square, [B("xT", k, j)], [bsq])
            mmg = []
            for q in range(4):
                mmg.append((bk[:, q:q + 1], sq[:, q * 128:(q + 1) * 128], ocS[:, 0:1], k))
            def fn(e, mmg=mmg):
                ins = None
                for out, l, r, k in mmg:
                    ins = e.matmul(out, l, r, start=(k == 0 and out is mmg[0][0]), stop=(k == 7), skip_group_check=True)
                return ins
            P.add("tensor", fn, reads=[bsq, bC], writes=[bb])
        brt = B("rtok", id(dst_rtok), j)
        ts(dst_rtok[:, 4 * j:4 * j + 4], bk[:, 0:4], RMS_EPS, None, ALU.add, None, [bb], [brt])
        pool(lambda e, j=j: e.tensor_tensor(out=dst_rtok[:, 4 * j:4 * j + 4], in0=dst_rtok[:, 4 * j:4 * j + 4],
                                             in1=mhalf[:, 0:4], op=ALU.pow), [brt, bC], [brt])
        return brt

    def bcast_tok(src_cols, reads):
        bk, bb = next_bank()
        for q in range(4):
            d = dg[q % 2]
            bd = B("dg", q % 2)
            ts(d[:], ident[:], src_cols[q], None, ALU.mult, None, list(reads) + [bC], [bd])
            mm(bk[:, q * 128:(q + 1) * 128], [(onesF[:], d[:])], [bd, bC], [bb])
        return bk, bb

    def rms_apply(j, brt, src_rtok, gcol, dst, dst_bufs, dst_cols, out_f32=False):
        bk, bb = bcast_tok([src_rtok[:, 4 * j + q:4 * j + q + 1] for q in range(4)], [brt])
        cols = slice(j * NT, (j + 1) * NT)
        for k in range(8):
            stt(dst[:, k, dst_cols], xT[:, k, cols], gcol(k), bk[:], ALU.mult, ALU.mult,
                [B("xT", k, j), bb, bPV], [dst_bufs(k)])

    def pvcol(l, i):
        return pv[:, l, i:i + 1]

    load_mixer_w1(0)
    load_mixer_wout(0)

    def param_prep():
        for l in range(L):
            bvst = B("vst", region="A")
            srcs = [(norm1_g_d[l], 0, 8), (norm2_g_d[l], 8, 8), (conv_b_d[l], 16, 3), (conv_ln_g_d[l], 19, 3),
                    (conv_ln_b_d[l], 22, 3), (pool_scale_d[l], 25, 2)]
            for src, r0, nr in srcs:
                dma("sync", vst[r0:r0 + nr, :], src.rearrange("(r c) -> r c", c=128), [], [bvst])
            bk, bb = next_bank()
            transposes([(bk[:, 0:27], vst[0:27, :], ident[0:27, 0:27])], [bvst, bC], [bb])
            vec(lambda e, bk=bk, l=l: e.tensor_copy(out=pv[:, l, 0:27], in_=bk[:, 0:27]), [bb], [bPV])
            bcw = B("cwst", region="A")
            dma("sync", cwst[0:CW, :], conv_w_d[l], [], [bcw])
            bk, bb = next_bank()
            transposes([(bk[:, ct * 32:ct * 32 + CW], cwst[0:CW, ct * 128:(ct + 1) * 128], ident[0:CW, 0:CW])
                        for ct in range(3)], [bcw, bC], [bb])
            vec(lambda e, bk=bk, l=l: e.tensor_copy(out=cwT[:, l, :, :],
                                                     in_=bk[:, 0:96].rearrange("p (c k) -> p c k", k=32)[:, :, 0:CW]),
                [bb], [bPV])
        bvst = B("vst", region="A")
        dma("sync", vst[0:8, :], final_g_d.rearrange("(r c) -> r c", c=128), [], [bvst])
        bk, bb = next_bank()
        transposes([(bk[:, 0:8], vst[0:8, :], ident[0:8, 0:8])], [bvst, bC], [bb])
        vec(lambda e, bk=bk: e.tensor_copy(out=gFv[:], in_=bk[:, 0:8]), [bb], [bPV])

    pend = {}

    def x_tr(s, j):
        for k in range(8):
            bk, bb = next_bank()
            transposes([(bk[:, q * 128:(q + 1) * 128], xstage[(4 * j + q) % 8][:, k * 128:(k + 1) * 128], ident[:])
                        for q in range(4)], [B("xstage", (4 * j + q) % 8, region="H") for q in range(4)] + [bC], [bb])
            if k % 2 == 0:
                act_op(xT[:, k, j * NT:(j + 1) * NT], bk[:], AF.Copy, [bb], [B("xT", k, j)])
            else:
                vec(lambda e, bk=bk, k=k, j=j: e.tensor_copy(out=xT[:, k, j * NT:(j + 1) * NT], in_=bk[:]),
                    [bb], [B("xT", k, j)])
        pend[j] = rms_tok_stats(j, rtok)

    for s in range(nseq):
        if s == 0:
            for j in range(NJ):
                x_tr(0, j)
                if j + 2 < NJ:
                    x_dma(0, j + 2)
            param_prep()

        for l in range(L):
            P.new_phase(["A", "H", "D", "E"])
            bbs = B("BT")
            dma("sync", BT[0:96].rearrange("p h t -> p (h t)"),
                b_s_d[l].rearrange("h t -> (h t)").partition_broadcast(96), [], [bbs])
            bsgB = B("sgB")
            dma("sync", sgB[:, 0, :], sg_ln_g_d[l].partition_broadcast(128), [], [bsgB])
            dma("sync", sgB[:, 1, :], sg_ln_b_d[l].partition_broadcast(128), [], [bsgB])
            bdg = [B("diag", i, region=diag_tiles[i][1]) for i in range(3 * NPE)]
            bws = B("wsst", region="H")
            dma("sync", wsst[:], w_s_d[l].rearrange("h t s -> t h s"), [], [bws])
            bk, bb = next_bank()
            transposes([(bk[:, h * 128:(h + 1) * 128], wsst[:, h, :], ident[:]) for h in range(4)], [bws, bC], [bb])
            bwsT = B("wsT")
            for h in range(4):
                tt(wsT[:, h, :], bk[:, h * 128:(h + 1) * 128], mask[:], ALU.mult, [bb, bC], [bwsT])

            brts = [pend[j] for j in range(NJ)]

            ybufs = [ybuf, ybuf1]
            by = [[B("ybuf", i, ct, region=("A", "H")[i]) for ct in range(3)] for i in range(2)]
            bz = B("zc", region="D")
            pool(lambda e: e.memset(ybuf[:, :, 0:HALO], 0.0), [], by[0])
            pool(lambda e: e.memset(zc[:, :, 0:ZH], 0.0), [], [bz])
            bh = [B("hT", k, region="H") for k in range(8)]
            bug = B("ug", region="D")
            bvg = [B("vg", cq, region="H") for cq in range(4)]
            bacc = [B("acc", ct, region="A") for ct in range(3)]
            baccb = [B("accb", ct, region="A") for ct in range(3)]
            bsqb = [B("sqb", ct, region="A") for ct in range(3)]
            pslots = [(accb[:, i, :], baccb[i]) for i in range(3)] + [(sqb[:, i, :], bsqb[i]) for i in range(3)]
            pslot_rr = [0]
            bsil = B("sil", region="A")
            bya = B("ya", region="H")
            byb = B("yb", region="H")
            byc = B("yc", region="H")
            bvn = [B("vn", cq, region="H") for cq in range(4)]
            bst = B("bnst")
            bsA = B("sA", region="E")
            bsB = B("sB", region="E")
            bpp = B("pp", region="D")
            blt = B("lnt")
            W = ZH + NT

            def proj(c0, m, bi):
                bk, bb = next_bank()
                mm(bk[0:m, :], [(w_in[:, k, c0:c0 + m], hT[:, k, :]) for k in range(8)],
                   bh + [B("w_in", bi)], [bb])
                return bk, bb

            def fa(j):
                rms_apply(j, brts[j], rtok, lambda k, l=l: pvcol(l, PV_G1 + k), hT, lambda k: bh[k], slice(0, NT))

            def fb1(j):
                for h in range(4):
                    ku, bu = proj(2 * DA + 96 * h, 96, 2)
                    act_op(ug[0:96, h, :], ku[0:96, :], AF.Gelu, [bu], [bug])
                for cq in range(4):
                    bk, bb = next_bank()
                    mm(bk[:, 0:DB], [(hT[:, k, cq * 128:(cq + 1) * 128], w_in[:, k, 2 * DA + DB:2 * DA + 2 * DB])
                                     for k in range(8)], bh + [B("w_in", 3)], [bb])
                    act_op(vg[:, cq, :], bk[:, 0:DB], AF.Gelu, [bb], [bvg[cq]])

            def fb2(j):
                yb_ = ybufs[j % 2]
                byj = by[j % 2]
                if j > 0:
                    for ct in range(3):
                        act_op(yb_[:, ct, 0:HALO], ybufs[(j - 1) % 2][:, ct, NT:NT + HALO], AF.Copy,
                               [by[(j - 1) % 2][ct]], [byj[ct]])
                for m in range(2):
                    kz, bzz = proj(2 * DA + 2 * DB + 128 * m, 128, 4)
                    act_op(zc[:, m, ZH:ZH + NT], kz[:], AF.Copy, [bzz], [bz])
                glu = []
                for ct in range(3):
                    ka, ba = proj(ct * 128, 128, 0)
                    kg, bg = proj(DA + ct * 128, 128, 1)
                    act_op(yb_[:, ct, HALO:HALO + NT], kg[:], AF.Sigmoid, [bg], [byj[ct]])
                    glu.append((ct, ka, ba))
                return glu

            def glu_mult(j, ct, ka, ba):
                yb_ = ybufs[j % 2]
                tt(yb_[:, ct, HALO:HALO + NT], yb_[:, ct, HALO:HALO + NT], ka[:], ALU.mult,
                   [by[j % 2][ct], ba], [by[j % 2][ct]])

            def lnv_stats(j):
                for cq in range(4):
                    vec(lambda e, cq=cq: e.bn_stats(out=bnst[0][:, cq * 6:cq * 6 + 6], in_=vg[:, cq, :]), [bvg[cq]], [bst])
                for cq in range(4):
                    vec(lambda e, cq=cq: e.bn_aggr(out=bnst[1][:, 2 * cq:2 * cq + 2], in_=bnst[0][:, cq * 6:cq * 6 + 6]),
                        [bst], [bst])
                mvv = bnst[1][:, 0:8].rearrange("p (q t) -> p q t", t=2)
                ts(bnst[1][:, 8:12], mvv[:, :, 1], LN_EPS, None, ALU.add, None, [bst], [bst])
                pool(lambda e: e.tensor_tensor(out=bnst[1][:, 8:12], in0=bnst[1][:, 8:12], in1=mhalf[:, 0:4], op=ALU.pow),
                     [bst, bC], [bst])

            def lnv_apply(j):
                for cq in range(4):
                    stt(vg[:, cq, :], vg[:, cq, :], bnst[1][:, 2 * cq:2 * cq + 1], sgB[:, 0, :], ALU.subtract, ALU.mult,
                        [bvg[cq], bst, bsgB], [bvg[cq]])
                    stt(vn[:, cq, :], vg[:, cq, :], bnst[1][:, 8 + cq:9 + cq], sgB[:, 1, :], ALU.mult, ALU.add,
                        [bvg[cq], bst, bsgB], [bvn[cq]])

            def pool_branch(j):
                tt(sA[:, :, 2:W], zc[:, :, 2:W], zc[:, :, 1:W - 1], ALU.add, [bz], [bsA])
                tt(sB[:, :, 4:W], sA[:, :, 4:W], sA[:, :, 2:W - 2], ALU.add, [bsA], [bsB])

                def pool_out(src, p0, m, w):
                    stt(pp[p0:p0 + 64, m, :], src[p0:p0 + 64, m, ZH:W], 1.0 / w, zc[p0:p0 + 64, m, ZH:W],
                        ALU.mult, ALU.subtract, [bsA, bsB, bz], [bpp])
                    if j == 0:
                        tt(src[p0:p0 + 64, m, ZH:2 * ZH], src[p0:p0 + 64, m, ZH:2 * ZH], invcnt[p0:p0 + 64, m, :],
                           ALU.mult, [bsA, bsB, bC], [bsA, bsB])
                        tt(pp[p0:p0 + 64, m, 0:ZH], src[p0:p0 + 64, m, ZH:2 * ZH], zc[p0:p0 + 64, m, ZH:2 * ZH],
                           ALU.subtract, [bsA, bsB, bz], [bpp])
                pool_out(sA, 0, 0, 2)
                pool_out(sB, 64, 0, 4)
                tt(sA[:, 1, 8:W], sB[:, 1, 8:W], sB[:, 1, 4:W - 4], ALU.add, [bsB], [bsA])
                pool_out(sA, 0, 1, 8)
                tt(sB[64:128, 1, 16:W], sA[64:128, 1, 16:W], sA[64:128, 1, 8:W - 8], ALU.add, [bsA], [bsB])
                pool_out(sB, 64, 1, 16)
                if j + 1 < NJ:
                    act_op(zc[:, :, 0:ZH], zc[:, :, NT:NT + ZH], AF.Copy, [bz], [bz])

            convps = {}

            def tap30(j):
                for ct in range(3):
                    act_op(acc[:, ct, :], ybufs[j % 2][:, ct, HALO:HALO + NT], AF.Identity, [by[j % 2][ct], bPV],
                           [bacc[ct]], scale=cwT[:, l, ct, CW - 1:CW], bias=pvcol(l, PV_CB + ct))

            def tap_ops(j):
                a_ = [(ct, k) for k in range(NPE, NPE + NACT) for ct in range(3)]
                d_ = [(ct, k) for k in range(NPE + NACT, CW - 1) for ct in range(3)]
                out_ = []
                for i in range(max(len(a_), len(d_))):
                    if i < len(d_):
                        out_.append(d_[i])
                    if i < len(a_):
                        out_.append(a_[i])
                return out_

            def tap(j, ct, k):
                if k >= NPE + NACT:
                    stt(acc[:, ct, :], ybufs[j % 2][:, ct, k:k + NT], cwT[:, l, ct, k:k + 1], acc[:, ct, :],
                        ALU.mult, ALU.add, [by[j % 2][ct], bPV, bacc[ct]], [bacc[ct]])
                    return
                slot, bsl = pslots[pslot_rr[0] % len(pslots)]
                pslot_rr[0] += 1
                act_op(slot, ybufs[j % 2][:, ct, k:k + NT], AF.Identity, [by[j % 2][ct], bPV], [bsl],
                       scale=cwT[:, l, ct, k:k + 1])
                bk, bb = convps[j][ct]
                mm(bk[:], [(identb[:], slot)], [bsl, bC], [bb], first=False, last=(k == NPE + NACT - 1))

            def conv_pe(j):
                convps[j] = []
                for ct in range(3):
                    bk, bb = next_bank()
                    mm(bk[:], [(diag_tiles[ct * NPE + k][0][:], ybufs[j % 2][:, ct, k:k + NT]) for k in range(NPE)],
                       [by[j % 2][ct]] + bdg[ct * NPE:(ct + 1) * NPE], [bb], first=True, last=(NACT == 0))
                    convps[j].append((bk, bb))

            def acc_add(j):
                for ct in range(3):
                    tt(acc[:, ct, :], acc[:, ct, :], convps[j][ct][0][:], ALU.add, [bacc[ct], convps[j][ct][1]],
                       [bacc[ct]])

            fa(0)
            for ct in range(3):
                for k in range(NPE):
                    ts(diag_tiles[ct * NPE + k][0][:], ident[:], cwT[:, l, ct, k:k + 1], None, ALU.mult, None,
                       [bC, bPV], [bdg[ct * NPE + k]])
            fb1(0)
            lnv_stats(0)
            glu0 = fb2(0)
            lnv_apply(0)
            for g_ in glu0:
                glu_mult(0, *g_)
            pool_branch(0)
            tap30(0)
            conv_pe(0)
            for ct, k in tap_ops(0):
                tap(0, ct, k)

            brts2 = {}
            for j in range(NJ):
                cols = slice(j * NT, (j + 1) * NT)
                nj = j + 1 if j + 1 < NJ else None
                acc_add(j)
                for ct in range(3):
                    act_op(accb[:, ct, :], acc[:, ct, :], AF.Copy, [bacc[ct]], [baccb[ct]])
                    act_op(sqb[:, ct, :], acc[:, ct, :], AF.Square, [bacc[ct]], [bsqb[ct]])
                if nj is not None:
                    fa(nj)
                pmb = []
                for cq in range(4):
                    bk2, bb2 = next_bank()
                    mms([(bk2[0:96, h * 128:(h + 1) * 128], [(vn[:, cq, 96 * h:96 * h + 96], wsT[:, h, :])])
                         for h in range(4)], [bvn[cq], bwsT], [bb2])
                    pmb.append((bk2, bb2))
                for m in range(2):
                    bk2, bb2 = next_bank()
                    mm(bk2[:], [(wpl[:, m, :], pp[:, m, :])], [bpp, B("wpl")], [bb2])
                    act_op(yc[:, m, :], bk2[:], AF.Identity, [bb2, bPV], [byc], scale=pvcol(l, PV_PS + m))
                bk, bb = next_bank()
                groups = []
                for q in range(4):
                    groups.append((bk[:, 2 * q:2 * q + 1],
                                   [(accb[:, ct, q * 128:(q + 1) * 128], ocS[:, 1:2]) for ct in range(3)]))
                    groups.append((bk[:, 2 * q + 1:2 * q + 2],
                                   [(sqb[:, ct, q * 128:(q + 1) * 128], ocS[:, 1:2]) for ct in range(3)]))
                mms(groups, baccb + bsqb + [bC], [bb])
                ts(rtok2[:, 0:8], bk[:, 0:8], 1.0 / DA, None, ALU.mult, None, [bb], [blt])
                me = rtok2[:, 0:8].rearrange("p (q t) -> p q t", t=2)
                tt(rtok2[:, 8:12], me[:, :, 0], me[:, :, 0], ALU.mult, [blt], [blt])
                tt(rtok2[:, 8:12], me[:, :, 1], rtok2[:, 8:12], ALU.subtract, [blt], [blt])
                ts(rtok2[:, 8:12], rtok2[:, 8:12], LN_EPS, None, ALU.add, None, [blt], [blt])
                pool(lambda e: e.tensor_tensor(out=rtok2[:, 8:12], in0=rtok2[:, 8:12], in1=mhalf[:, 0:4], op=ALU.pow),
                     [blt, bC], [blt])
                for cq in range(4):
                    bk2, bb2 = pmb[cq]
                    t = tlnv[cq % 2]
                    bt = (bsA, bsB)[cq % 2]
                    tt(t[0:96].rearrange("p (h t) -> p h t", t=128), bk2[0:96, :].rearrange("p (h t) -> p h t", t=128),
                       BT[0:96], ALU.add, [bb2, bbs], [bt])
                    tt(yb[0:96, :, cq * 128:(cq + 1) * 128], ug[0:96, :, cq * 128:(cq + 1) * 128],
                       t[0:96].rearrange("p (h t) -> p h t", t=128), ALU.mult, [bug, bt], [byb])
                if nj is not None:
                    fb1(nj)
                stt(rtok2[:, 12:16], me[:, :, 0], -1.0, rtok2[:, 8:12], ALU.mult, ALU.mult, [blt], [blt])
                krs, brs = bcast_tok([rtok2[:, 8 + q:9 + q] for q in range(4)], [blt])
                knm, bnm = bcast_tok([rtok2[:, 12 + q:13 + q] for q in range(4)], [blt])
                for ct in range(3):
                    t = tlnv[ct % 2]
                    bt = (bsA, bsB)[ct % 2]
                    tt(t, acc[:, ct, :], krs[:], ALU.mult, [bacc[ct], brs], [bt])
                    tt(t, t, knm[:], ALU.add, [bt, bnm], [bt])
                    act_op(sil[:, ct, :], t, AF.Silu, [bt, bPV], [bsil],
                           scale=pvcol(l, PV_LNG + ct), bias=pvcol(l, PV_LNB + ct))
                glu = []
                if nj is not None:
                    lnv_stats(nj)
                    glu = fb2(nj)
                    lnv_apply(nj)
                    for g_ in glu:
                        glu_mult(nj, *g_)
                for co in range(3):
                    bk2, bb2 = next_bank()
                    mm(bk2[:], [(w_pw[:, ci, co * 128:(co + 1) * 128], sil[:, ci, :]) for ci in range(3)],
                       [bsil, B("w_pw")], [bb2])
                    act_op(ya[:, co, :], bk2[:], AF.Copy, [bb2], [bya])
                if nj is not None:
                    pool_branch(nj)
                    tap30(nj)
                    conv_pe(nj)
                ptaps = tap_ops(nj) if nj is not None else []
                per = (len(ptaps) + 7) // 8
                for n in range(8):
                    ncol = slice(n * 128, (n + 1) * 128)
                    bk2, bb2 = next_bank()
                    pairs = [(wo_b[0:96, h, ncol], yb[0:96, h, :]) for h in range(4)]
                    pairs += [(wo_c[:, m, ncol], yc[:, m, :]) for m in range(2)]
                    pairs += [(wo_a[:, i, ncol], ya[:, i, :]) for i in range(3)]
                    mm(bk2[:], pairs, [bya, byb, byc, B("wo_a", region="R"), B("wo_b", region="R"),
                                       B("wo_c", region="R")], [bb2])
                    for ct, k in ptaps[n * per:(n + 1) * per]:
                        tap(nj, ct, k)
                    tt(xT[:, n, cols], xT[:, n, cols], bk2[:], ALU.add, [B("xT", n, j), bb2], [B("xT", n, j)])
                brts2[j] = rms_tok_stats(j, rtokB)

            P.new_phase(["A", "H", "D", "E", "R"])
            nfp = NF // 2
            for fp in range(min(NRING, nfp)):
                load_ring(l, fp, fp % NRING)
            load_wd(l, 0)
            bh2 = [[B("h2T", k, j, region="H") for k in range(8)] for j in range(NJ)]
            for j in range(NJ):
                rms_apply(j, brts2[j], rtokB, lambda k, l=l: pvcol(l, PV_G2 + k), h2T, lambda k, j=j: bh2[j][k],
                          slice(j * NT, (j + 1) * NT))
            nxt = None
            if l + 1 < L:
                nxt = l + 1
            elif s + 1 < nseq:
                nxt = 0
            if nxt is not None:
                load_mixer_w1(nxt)
            for ci, (f0, nf) in enumerate(CHUNKS):
                bact = [[B("act", fi, j, region="A") for j in range(NJ)] for fi in range(nf)]
                for fi in range(nf):
                    f = f0 + fi
                    fp, half = f // 2, f % 2
                    slot = fp % NRING
                    fc = slice(half * 128, half * 128 + 128)
                    for j in range(NJ):
                        cols = slice(j * NT, (j + 1) * NT)
                        kg, bg = next_bank()
                        mm(kg[:], [(ring[slot][:, 0, k, fc], h2T[:, k, cols]) for k in range(8)],
                           bh2[j] + [B("ring", slot, 0, region="R")], [bg])
                        ku, bu = next_bank()
                        mm(ku[:], [(ring[slot][:, 1, k, fc], h2T[:, k, cols]) for k in range(8)],
                           bh2[j] + [B("ring", slot, 1, region="R")], [bu])
                        t = tlnv[(fi * NJ + j) % 2]
                        bt = B(("sA", "sB")[(fi * NJ + j) % 2], region="E")
                        act_op(t, kg[:], AF.Silu, [bg], [bt])
                        tt(act[:, fi, cols], t, ku[:], ALU.mult, [bt, bu], [bact[fi][j]])
                    if half == 1 and fp + NRING < nfp:
                        load_ring(l, fp + NRING, slot)
                last = ci + 1 == len(CHUNKS)
                order = [(n, j) for j in range(NJ) for n in range(8)] if last else \
                        [(n, j) for n in range(8) for j in range(NJ)]
                for n, j in order:
                    ncol = slice(n * 128, (n + 1) * 128)
                    cols = slice(j * NT, (j + 1) * NT)
                    bk, bb = next_bank()
                    mm(bk[:], [(wd[:, fi, ncol], act[:, fi, cols]) for fi in range(nf)],
                       [bact[fi][j] for fi in range(nf)] + [B("wd", region="D")], [bb])
                    tt(xT[:, n, cols], xT[:, n, cols], bk[:], ALU.add, [B("xT", n, j), bb], [B("xT", n, j)])
                    if last and n == 7 and j > 0:
                        pend[j - 1] = rms_tok_stats(j - 1, rtok)
                if last:
                    pend[NJ - 1] = rms_tok_stats(NJ - 1, rtok)
                if ci + 1 < len(CHUNKS):
                    load_wd(l, ci + 1)
            if nxt is not None:
                P.new_phase(["R"])
                load_mixer_wout(nxt)

        P.new_phase(["A", "H", "D", "E"])
        if s + 1 < nseq:
            x_dma(s + 1, 0)
            x_dma(s + 1, 1)
        byn = [B("yn", k, region="A") for k in range(8)]

        def fin_apply(j):
            rms_apply(j, pend[j], rtok, lambda k: gFv[:, k:k + 1], yn, lambda k: byn[k], slice(0, NT))

        def fin_store(j):
            for q in range(4):
                tb = j * 4 + q
                os_ = ostage[tb % 2]
                bos = B("ostage", tb % 2, region="A")
                for hf in range(2):
                    bk, bb = next_bank()
                    transposes([(bk[:, kk * 128:(kk + 1) * 128], yn[:, hf * 4 + kk, q * 128:(q + 1) * 128], ident[:])
                                for kk in range(4)], byn[hf * 4:hf * 4 + 4] + [bC], [bb])
                    if hf == 0:
                        act_op(os_[:, 0:512], bk[:], AF.Copy, [bb], [bos])
                    else:
                        vec(lambda e, bk=bk, os_=os_: e.tensor_copy(out=os_[:, 512:1024], in_=bk[:]), [bb], [bos])
                dma("sync", out_d[s, tb * 128:(tb + 1) * 128, :], os_[:], [bos], [], final=True)

        fin_apply(0)
        for j in range(NJ):
            fin_store(j)
            if j + 1 < NJ:
                fin_apply(j + 1)
            if s + 1 < nseq:
                x_tr(s + 1, j)
                if j + 2 < NJ:
                    x_dma(s + 1, j + 2)

    P.emit(nc)
    return nc


_NC_CACHE = {}


def _get_nc(nseq, depth):
    key = (nseq, depth)
    if key not in _NC_CACHE:
        _NC_CACHE[key] = build_program(nseq=nseq, depth=depth)
    return _NC_CACHE[key]


_WNAMES = ["norm1_g", "w_in", "conv_w", "conv_b", "conv_ln_g", "conv_ln_b", "w_pw", "sg_ln_g", "sg_ln_b",
           "w_s", "b_s", "w_pool", "pool_scale", "w_out", "norm2_g", "w_gate_up", "w_down", "final_g"]


def kernel(**inputs):
    x = np.ascontiguousarray(np.asarray(inputs["x"], dtype=np.float32))
    bsz = x.shape[0]
    nseq = bsz // N_CORES
    depth = int(np.asarray(inputs["w_in"]).shape[0])
    nc = _get_nc(nseq, depth)
    ws = {n: np.ascontiguousarray(np.asarray(inputs[n], dtype=np.float32)) for n in _WNAMES}
    in_maps = []
    for c in range(N_CORES):
        m = dict(ws)
        m["x"] = x[c * nseq:(c + 1) * nseq]
        in_maps.append(m)
    res = run_bass_kernel_spmd(nc, in_maps, core_ids=list(range(N_CORES)))
    return np.concatenate([np.asarray(r["out"]) for r in res.results], axis=0).astype(np.float32)
```

```python
import numpy as np
import concourse.bass as bass
import concourse.mybir as mybir
from concourse.bass_utils import run_bass_kernel_spmd

F32 = mybir.dt.float32
BF16 = mybir.dt.bfloat16
AF = mybir.ActivationFunctionType
ALU = mybir.AluOpType

D = 1024
S = 2048
NT = 512
NJ = S // NT
DA = 384
DB = 384
DC = 256
DIN = 1792
DFF = 2816
NF = DFF // 128
CW = 31
HALO = CW - 1
ZH = 16
RMS_EPS = 1e-6
LN_EPS = 1e-5
N_CORES = 8
CHUNKS = [(0, 6), (6, 6), (12, 6), (18, 4)]
NRING = 3
NACT = 6
NPE = 18


class Buf:
    __slots__ = ("name", "w", "r", "region")

    def __init__(self, name, region=None):
        self.name = name
        self.w = None
        self.r = []
        self.region = region


class Eng:
    def __init__(self, name):
        self.name = name
        self.ops = []
        self.gen = 0
        self.cnt = 0
        self.waited = {}
        self.dma_rr = 0
        self.dma_vals = {}

    @property
    def semkey(self):
        return ("e", self.name, self.gen)


SEM_LIMIT = 30000
N_DMA_SEMS = {"gpsimd": 20, "sync": 12, "scalar": 4}


class Prog:
    def __init__(self):
        self.eng = {n: Eng(n) for n in ("tensor", "vector", "scalar", "gpsimd", "sync")}
        self.bufs = {}
        self.fence = {}
        self.region_last = {}
        self.region_dma = {}
        self.final_tokens = []

    def buf(self, *key, region=None):
        b = self.bufs.get(key)
        if b is None:
            b = Buf(key, region)
            self.bufs[key] = b
        return b

    def new_phase(self, regions):
        for r in regions:
            toks = list(self.region_last.get(r, {}).values()) + list(self.region_dma.get(r, []))
            self.fence[r] = toks
            self.region_dma[r] = []

    def add(self, engname, fn, reads=(), writes=(), dma=False, final=False):
        e = self.eng[engname]
        deps = {}

        def need(tok):
            if tok is None:
                return
            k, v = tok
            if deps.get(k, 0) < v:
                deps[k] = v

        for b in reads:
            need(b.w)
            if b.region is not None:
                for t in self.fence.get(b.region, ()):
                    need(t)
        for b in writes:
            need(b.w)
            for t in b.r:
                need(t)
            if b.region is not None:
                for t in self.fence.get(b.region, ()):
                    need(t)
        if dma:
            n = N_DMA_SEMS[engname]
            idx = e.dma_rr % n
            e.dma_rr += 1
            k = ("d", engname, idx)
            prev = e.dma_vals.get(k, 0)
            if prev:
                need((k, prev))
            tok = (k, prev + 16)
            e.dma_vals[k] = prev + 16
        else:
            if e.cnt >= SEM_LIMIT:
                e.gen += 1
                e.cnt = 0
            e.cnt += 1
            tok = (e.semkey, e.cnt)
        waits = []
        for k, v in deps.items():
            if engname == "tensor" and k == e.semkey:
                continue
            if e.waited.get(k, 0) >= v:
                continue
            e.waited[k] = v
            waits.append((k, v))
        e.ops.append((fn, waits, tok, dma))
        ws = set(id(b) for b in writes)
        for b in writes:
            b.w = tok
            b.r = []
        for b in reads:
            if id(b) not in ws:
                b.r.append(tok)
        for b in list(reads) + list(writes):
            if b.region is not None:
                if dma:
                    self.region_dma.setdefault(b.region, []).append(tok)
                else:
                    self.region_last.setdefault(b.region, {})[e.semkey[:2]] = tok
        if final:
            self.final_tokens.append(tok)
        return tok

    def emit(self, nc):
        keys = set()
        for e in self.eng.values():
            for fn, waits, tok, dma in e.ops:
                keys.add(tok[0])
                for k, v in waits:
                    keys.add(k)
        sems = {}
        for i, k in enumerate(sorted(keys, key=str)):
            sems[k] = nc.alloc_semaphore("s%d" % i)
        finals = self.final_tokens
        with nc.Block() as block:
            def mk(e):
                def body(eh):
                    for fn, waits, tok, dma in e.ops:
                        for k, v in waits:
                            eh.wait_ge(sems[k], v)
                        ins = fn(eh)
                        ins.then_inc(sems[tok[0]], 16 if dma else 1)
                    if e.name == "sync":
                        for k, v in finals:
                            eh.wait_ge(sems[k], v)
                return body
            for name, e in self.eng.items():
                if not e.ops and name != "sync":
                    continue
                getattr(block, name)(mk(e))


def build_program(nseq=2, depth=2):
    nc = bass.Bass("TRN2", target_bir_lowering=False)
    P = Prog()
    L = depth

    def din(name, shape):
        return nc.dram_tensor(name, list(shape), F32, kind="ExternalInput").ap()

    x_d = din("x", [nseq, S, D])
    norm1_g_d = din("norm1_g", [L, D])
    w_in_d = din("w_in", [L, D, DIN])
    conv_w_d = din("conv_w", [L, CW, DA])
    conv_b_d = din("conv_b", [L, DA])
    conv_ln_g_d = din("conv_ln_g", [L, DA])
    conv_ln_b_d = din("conv_ln_b", [L, DA])
    w_pw_d = din("w_pw", [L, DA, DA])
    sg_ln_g_d = din("sg_ln_g", [L, DB])
    sg_ln_b_d = din("sg_ln_b", [L, DB])
    w_s_d = din("w_s", [L, 4, 128, 128])
    b_s_d = din("b_s", [L, 4, 128])
    w_pool_d = din("w_pool", [L, 4, 64, 64])
    pool_scale_d = din("pool_scale", [L, DC])
    w_out_d = din("w_out", [L, D, D])
    norm2_g_d = din("norm2_g", [L, D])
    w_gu_d = din("w_gate_up", [L, D, 2 * DFF])
    w_down_d = din("w_down", [L, DFF, D])
    final_g_d = din("final_g", [D])
    out_d = nc.dram_tensor("out", [nseq, S, D], F32, kind="ExternalOutput").ap()

    base = (nc.sbuf_base + 31) // 32 * 32
    top = nc.sbuf_top
    cur = [base]

    def alloc(name, shape, dt, at=None):
        nbytes = int(np.prod(shape[1:])) * (4 if dt == F32 else 2)
        nbytes = (nbytes + 31) // 32 * 32
        if at is None:
            off = cur[0]
            cur[0] += nbytes
        else:
            off = at
        assert off + nbytes <= top, (name, off, nbytes, top)
        return nc.alloc_sbuf_tensor_at(name, list(shape), dt, offset=off), off, nbytes

    class Region:
        def __init__(self, name, size):
            self.name = name
            self.size = size
            self.base = cur[0]
            cur[0] += size
            assert cur[0] <= top, ("region overflow", name, cur[0], top)

        def carve(self):
            return Carver(self)

    class Carver:
        def __init__(self, reg):
            self.reg = reg
            self.off = reg.base

        def alloc(self, name, shape, dt):
            t, off, nb = alloc(name, shape, dt, at=self.off)
            self.off += nb
            assert self.off <= self.reg.base + self.reg.size, ("carve overflow", self.reg.name, name)
            return t

    ident = alloc("ident", [128, 128], F32)[0]
    onesF = alloc("onesF", [128, 128], F32)[0]
    mask = alloc("mask", [128, 128], F32)[0]
    identb = alloc("identb", [128, 128], BF16)[0]
    mhalf = alloc("mhalf", [128, 16], F32)[0]
    invcnt = alloc("invcnt", [128, 2, 16], F32)[0]
    ocS = alloc("ocS", [128, 2], BF16)[0]
    pv = alloc("pv", [128, L, 32], F32)[0]
    gFv = alloc("gFv", [128, 8], F32)[0]
    cwT = alloc("cwT", [128, L, 3, CW], F32)[0]
    BT = alloc("BT", [128, 4, 128], F32)[0]
    sgB = alloc("sgB", [128, 2, DB], F32)[0]
    sqr = [alloc("sqr%d" % i, [128, NT], BF16)[0] for i in range(2)]
    rtok = alloc("rtok", [128, 16], F32)[0]
    rtok2 = alloc("rtok2", [128, 16], F32)[0]
    rtokB = alloc("rtokB", [128, 16], F32)[0]
    dg = [alloc("dg%d" % i, [128, 128], F32)[0] for i in range(2)]
    bnst = [alloc("bnst%d" % i, [128, 32], F32)[0] for i in range(2)]
    xT = alloc("xT", [128, 8, S], F32)[0]
    w_in = alloc("w_in_sb", [128, 8, DIN], BF16)[0]
    w_pw = alloc("w_pw_sb", [128, 3, DA], BF16)[0]
    wsT = alloc("wsT", [128, 4, 128], BF16)[0]
    wpl = alloc("wpl", [128, 2, 128], BF16)[0]
    RR = Region("R", 24576)
    RH = Region("H", 32768)
    RA = Region("A", 24576)
    RD = Region("D", 12288)
    RE = Region("E", (top - cur[0]) // 32 * 32)

    c = RR.carve()
    wo_a = c.alloc("wo_a", [128, 3, D], BF16)
    wo_b = c.alloc("wo_b", [128, 4, D], BF16)
    wo_c = c.alloc("wo_c", [128, 2, D], BF16)
    c = RR.carve()
    ring = [c.alloc("ring%d" % i, [128, 2, 8, 256], BF16) for i in range(NRING)]
    c = RH.carve()
    h2T = c.alloc("h2T", [128, 8, S], BF16)
    c = RH.carve()
    hT = c.alloc("hT", [128, 8, NT], BF16)
    ya = c.alloc("ya", [128, 3, NT], BF16)
    yb = c.alloc("yb", [128, 4, NT], BF16)
    yc = c.alloc("yc", [128, 2, NT], BF16)
    vg = c.alloc("vg", [128, 4, DB], F32)
    vn = c.alloc("vn", [128, 4, DB], BF16)
    wsst = c.alloc("wsst", [128, 4, 128], F32)
    ybuf1 = c.alloc("ybuf1", [128, 3, HALO + NT], BF16)
    c = RA.carve()
    act = c.alloc("act", [128, 6, S], BF16)
    c = RH.carve()
    xstage = [c.alloc("xstage%d" % i, [128, D], F32) for i in range(8)]
    c = RA.carve()
    vst = c.alloc("vst", [32, 128], F32)
    cwst = c.alloc("cwst", [32, DA], F32)
    c = RA.carve()
    yn = c.alloc("yn", [128, 8, NT], F32)
    ostage = [c.alloc("ostage%d" % i, [128, D], F32) for i in range(2)]
    c = RA.carve()
    ybuf = c.alloc("ybuf", [128, 3, HALO + NT], BF16)
    acc = c.alloc("acc", [128, 3, NT], F32)
    accb = c.alloc("accb", [128, 3, NT], BF16)
    sqb = c.alloc("sqb", [128, 3, NT], BF16)
    sil = c.alloc("sil", [128, 3, NT], BF16)
    diag_tiles = []
    cA = c
    c = RD.carve()
    wd = c.alloc("wd", [128, 6, D], BF16)
    c = RD.carve()
    ug = c.alloc("ug", [128, 4, NT], BF16)
    zc = c.alloc("zc", [128, 2, ZH + NT], F32)
    pp = c.alloc("pp", [128, 2, NT], BF16)
    cD = c
    c = RE.carve()
    sA = c.alloc("sA", [128, 2, ZH + NT], F32)
    sB = c.alloc("sB", [128, 2, ZH + NT], F32)
    tlnv = [sA[:, 0, 0:NT], sB[:, 0, 0:NT]]
    cE = c
    cR = RR.carve()
    cR.off = RR.base + 18432
    for cc, rn in ((cR, "R"), (cA, "A"), (cD, "D"), (cE, "E")):
        while cc.off + 256 <= cc.reg.base + cc.reg.size and len(diag_tiles) < 3 * NPE:
            diag_tiles.append((cc.alloc("diag%d" % len(diag_tiles), [128, 128], BF16), rn))
    assert len(diag_tiles) == 3 * NPE, len(diag_tiles)

    banks = [nc.alloc_psum_tensor("bank%d" % i, [128, 512], F32) for i in range(8)]
    bank_rr = [0]

    def next_bank():
        for _ in range(8):
            i = bank_rr[0] % 8
            bank_rr[0] += 1
            b_ = P.buf("bank", i)
            if b_.w is None or len(b_.r) > 0:
                return banks[i], b_
        raise AssertionError("all PSUM banks are held by writes whose readers were not emitted yet")

    B = P.buf

    def dma(q, out, in_, reads=(), writes=(), final=False, nonc=False):
        def fn(e, out=out, in_=in_):
            if nonc:
                return e.dma_start(out=out, in_=in_, allow_slow_non_contiguous=True)
            return e.dma_start(out=out, in_=in_)
        return P.add(q, fn, reads=reads, writes=writes, dma=True, final=final)

    def mm(out, pairs, reads, writes, first=True, last=True):
        def fn(e, out=out, pairs=pairs, first=first, last=last):
            n = len(pairs)
            ins = None
            for i, (l, r) in enumerate(pairs):
                ins = e.matmul(out, l, r, start=(first and i == 0), stop=(last and i == n - 1))
            return ins
        return P.add("tensor", fn, reads=reads, writes=writes)

    def mms(groups, reads, writes):
        def fn(e, groups=groups):
            ins = None
            for out, pairs in groups:
                n = len(pairs)
                for i, (l, r) in enumerate(pairs):
                    ins = e.matmul(out, l, r, start=(i == 0), stop=(i == n - 1))
            return ins
        return P.add("tensor", fn, reads=reads, writes=writes)

    def transposes(items, reads, writes):
        def fn(e, items=items):
            ins = None
            for out, in_, idn in items:
                ins = e.transpose(out, in_, idn)
            return ins
        return P.add("tensor", fn, reads=reads, writes=writes)

    def act_op(out, in_, func, reads, writes, scale=None, bias=None):
        def fn(e, out=out, in_=in_, func=func, scale=scale, bias=bias):
            kw = {}
            if scale is not None:
                kw["scale"] = scale
            if bias is not None:
                kw["bias"] = bias
            return e.activation(out=out, in_=in_, func=func, **kw)
        return P.add("scalar", fn, reads=reads, writes=writes)

    def vec(fn, reads, writes):
        return P.add("vector", fn, reads=reads, writes=writes)

    def pool(fn, reads, writes):
        return P.add("gpsimd", fn, reads=reads, writes=writes)

    def tt(out, in0, in1, op, reads, writes, eng="vector"):
        return P.add(eng, lambda e, out=out, in0=in0, in1=in1, op=op: e.tensor_tensor(out=out, in0=in0, in1=in1, op=op),
                     reads=reads, writes=writes)

    def ts(out, in0, s1, s2, op0, op1, reads, writes, eng="vector"):
        def fn(e, out=out, in0=in0, s1=s1, s2=s2, op0=op0, op1=op1):
            if op1 is None:
                return e.tensor_scalar(out=out, in0=in0, scalar1=s1, scalar2=None, op0=op0)
            return e.tensor_scalar(out=out, in0=in0, scalar1=s1, scalar2=s2, op0=op0, op1=op1)
        return P.add(eng, fn, reads=reads, writes=writes)

    def stt(out, in0, scalar, in1, op0, op1, reads, writes):
        return vec(lambda e, out=out, in0=in0, scalar=scalar, in1=in1, op0=op0, op1=op1:
                   e.scalar_tensor_tensor(out=out, in0=in0, scalar=scalar, in1=in1, op0=op0, op1=op1),
                   reads=reads, writes=writes)

    def x_dma(s, j):
        for q in range(4):
            tb = 4 * j + q
            dma("sync", xstage[tb % 8][:], x_d[s, tb * 128:(tb + 1) * 128, :], [], [B("xstage", tb % 8, region="H")])

    x_dma(0, 0)
    x_dma(0, 1)

    bC = B("consts")
    pool(lambda e: e.memset(onesF[:], 1.0), [], [bC])
    pool(lambda e: e.affine_select(out=ident[:], in_=onesF[:], pattern=[[-1, 128]], compare_op=ALU.is_equal,
                                   fill=0.0, base=0, channel_multiplier=1), [bC], [bC])
    pool(lambda e: e.affine_select(out=mask[:], in_=onesF[:], pattern=[[1, 128]], compare_op=ALU.is_ge,
                                   fill=0.0, base=0, channel_multiplier=-1), [bC], [bC])
    pool(lambda e: e.memset(mhalf[:], -0.5), [], [bC])
    pool(lambda e: e.tensor_copy(out=identb[:], in_=ident[:]), [bC], [bC])
    pool(lambda e: e.memset(ocS[:, 0:1], 1.0 / 1024.0), [], [bC])
    pool(lambda e: e.memset(ocS[:, 1:2], 1.0), [], [bC])
    for m in range(2):
        pool(lambda e, m=m: e.iota(invcnt[:, m, :], [[1, ZH]], base=1, channel_multiplier=0,
                                    allow_small_or_imprecise_dtypes=True), [], [bC])
    for m, p0, w in ((0, 0, 2.0), (0, 64, 4.0), (1, 0, 8.0), (1, 64, 16.0)):
        ts(invcnt[p0:p0 + 64, m, :], invcnt[p0:p0 + 64, m, :], w, None, ALU.min, None, [bC], [bC])
    vec(lambda e: e.reciprocal(out=invcnt[:], in_=invcnt[:]), [bC], [bC])

    PV_G1, PV_G2, PV_CB, PV_LNG, PV_LNB, PV_PS = 0, 8, 16, 19, 22, 25
    bPV = B("pv")

    WIN_BLOCKS = [(0, 384), (384, 768), (768, 1152), (1152, 1536), (1536, 1792)]

    def load_mixer_w1(l):
        wv = w_in_d[l].rearrange("(k p) c -> p k c", p=128)
        for bi, (c0, c1) in enumerate(WIN_BLOCKS):
            dma("gpsimd", w_in[:, :, c0:c1], wv[:, :, c0:c1], [], [B("w_in", bi)])
        dma("gpsimd", w_pw[:], w_pw_d[l].rearrange("(k p) c -> p k c", p=128), [], [B("w_pw")])
        pool(lambda e: e.memset(wpl[:], 0.0), [], [B("wpl")])
        for g in range(4):
            m, h = g // 2, g % 2
            dma("gpsimd", wpl[64 * h:64 * h + 64, m, 64 * h:64 * h + 64], w_pool_d[l, g], [], [B("wpl")])

    def load_mixer_wout(l):
        dma("gpsimd", wo_a[:], w_out_d[l, 0:DA, :].rearrange("(k p) n -> p k n", p=128), [], [B("wo_a", region="R")])
        dma("gpsimd", wo_b[0:96], w_out_d[l, DA:DA + DB, :].rearrange("(k p) n -> p k n", p=96), [],
            [B("wo_b", region="R")])
        dma("gpsimd", wo_c[:], w_out_d[l, DA + DB:D, :].rearrange("(k p) n -> p k n", p=128), [],
            [B("wo_c", region="R")])

    def load_ring(l, fp, slot):
        wv = w_gu_d[l].rearrange("(k p) c -> p k c", p=128)
        dma("gpsimd", ring[slot][:, 0], wv[:, :, 256 * fp:256 * fp + 256], [], [B("ring", slot, 0, region="R")])
        dma("gpsimd", ring[slot][:, 1], wv[:, :, DFF + 256 * fp:DFF + 256 * fp + 256], [],
            [B("ring", slot, 1, region="R")])

    def load_wd(l, ci):
        f0, nf = CHUNKS[ci]
        dma("gpsimd", wd[:, 0:nf, :], w_down_d[l, f0 * 128:(f0 + nf) * 128, :].rearrange("(f p) n -> p f n", p=128),
            [], [B("wd", region="D")])

    def rms_tok_stats(j, dst_rtok):
        cols = slice(j * NT, (j + 1) * NT)
        bk, bb = next_bank()
        groups = [[] for _ in range(4)]
        rd = []
        for k in range(8):
            sq = sqr[k % 2]
            bsq = B("sqr", k % 2)
            act_op(sq[:], xT[:, k, cols], AF.Square, [B("xT", k, j)], [bsq])
            mmg = []
            for q in range(4):
                mmg.append((bk[:, q:q + 1], sq[:, q * 128:(q + 1) * 128], ocS[:, 0:1], k))
            def fn(e, mmg=mmg):
                ins = None
                for out, l, r, k in mmg:
                    ins = e.matmul(out, l, r, start=(k == 0 and out is mmg[0][0]), stop=(k == 7), skip_group_check=True)
                return ins
            P.add("tensor", fn, reads=[bsq, bC], writes=[bb])
        brt = B("rtok", id(dst_rtok), j)
        ts(dst_rtok[:, 4 * j:4 * j + 4], bk[:, 0:4], RMS_EPS, None, ALU.add, None, [bb], [brt])
        pool(lambda e, j=j: e.tensor_tensor(out=dst_rtok[:, 4 * j:4 * j + 4], in0=dst_rtok[:, 4 * j:4 * j + 4],
                                             in1=mhalf[:, 0:4], op=ALU.pow), [brt, bC], [brt])
        return brt

    def bcast_tok(src_cols, reads):
        bk, bb = next_bank()
        for q in range(4):
            d = dg[q % 2]
            bd = B("dg", q % 2)
            ts(d[:], ident[:], src_cols[q], None, ALU.mult, None, list(reads) + [bC], [bd])
            mm(bk[:, q * 128:(q + 1) * 128], [(onesF[:], d[:])], [bd, bC], [bb])
        return bk, bb

    def rms_bcast(j, brt, src_rtok):
        return bcast_tok([src_rtok[:, 4 * j + q:4 * j + q + 1] for q in range(4)], [brt])

    def rms_scale(j, bkb, gcol, dst, dst_bufs, dst_cols):
        bk, bb = bkb
        cols = slice(j * NT, (j + 1) * NT)
        for k in range(8):
            stt(dst[:, k, dst_cols], xT[:, k, cols], gcol(k), bk[:], ALU.mult, ALU.mult,
                [B("xT", k, j), bb, bPV], [dst_bufs(k)])

    def rms_apply(j, brt, src_rtok, gcol, dst, dst_bufs, dst_cols, out_f32=False):
        rms_scale(j, rms_bcast(j, brt, src_rtok), gcol, dst, dst_bufs, dst_cols)

    def pvcol(l, i):
        return pv[:, l, i:i + 1]

    load_mixer_w1(0)
    load_mixer_wout(0)

    def param_prep():
        for l in range(L):
            bvst = B("vst", region="A")
            srcs = [(norm1_g_d[l], 0, 8), (norm2_g_d[l], 8, 8), (conv_b_d[l], 16, 3), (conv_ln_g_d[l], 19, 3),
                    (conv_ln_b_d[l], 22, 3), (pool_scale_d[l], 25, 2)]
            for src, r0, nr in srcs:
                dma("sync", vst[r0:r0 + nr, :], src.rearrange("(r c) -> r c", c=128), [], [bvst])
            bk, bb = next_bank()
            transposes([(bk[:, 0:27], vst[0:27, :], ident[0:27, 0:27])], [bvst, bC], [bb])
            vec(lambda e, bk=bk, l=l: e.tensor_copy(out=pv[:, l, 0:27], in_=bk[:, 0:27]), [bb], [bPV])
            bcw = B("cwst", region="A")
            dma("sync", cwst[0:CW, :], conv_w_d[l], [], [bcw])
            bk, bb = next_bank()
            transposes([(bk[:, ct * 32:ct * 32 + CW], cwst[0:CW, ct * 128:(ct + 1) * 128], ident[0:CW, 0:CW])
                        for ct in range(3)], [bcw, bC], [bb])
            vec(lambda e, bk=bk, l=l: e.tensor_copy(out=cwT[:, l, :, :],
                                                     in_=bk[:, 0:96].rearrange("p (c k) -> p c k", k=32)[:, :, 0:CW]),
                [bb], [bPV])
        bvst = B("vst", region="A")
        dma("sync", vst[0:8, :], final_g_d.rearrange("(r c) -> r c", c=128), [], [bvst])
        bk, bb = next_bank()
        transposes([(bk[:, 0:8], vst[0:8, :], ident[0:8, 0:8])], [bvst, bC], [bb])
        vec(lambda e, bk=bk: e.tensor_copy(out=gFv[:], in_=bk[:, 0:8]), [bb], [bPV])

    pend = {}

    def x_tr(s, j):
        for k in range(8):
            bk, bb = next_bank()
            transposes([(bk[:, q * 128:(q + 1) * 128], xstage[(4 * j + q) % 8][:, k * 128:(k + 1) * 128], ident[:])
                        for q in range(4)], [B("xstage", (4 * j + q) % 8, region="H") for q in range(4)] + [bC], [bb])
            if k % 2 == 0:
                act_op(xT[:, k, j * NT:(j + 1) * NT], bk[:], AF.Copy, [bb], [B("xT", k, j)])
            else:
                vec(lambda e, bk=bk, k=k, j=j: e.tensor_copy(out=xT[:, k, j * NT:(j + 1) * NT], in_=bk[:]),
                    [bb], [B("xT", k, j)])
        pend[j] = rms_tok_stats(j, rtok)

    for s in range(nseq):
        if s == 0:
            for j in range(NJ):
                x_tr(0, j)
                if j + 2 < NJ:
                    x_dma(0, j + 2)
            param_prep()

        for l in range(L):
            P.new_phase(["A", "H", "D", "E"])
            bbs = B("BT")
            dma("sync", BT[0:96].rearrange("p h t -> p (h t)"),
                b_s_d[l].rearrange("h t -> (h t)").partition_broadcast(96), [], [bbs])
            bsgB = B("sgB")
            dma("sync", sgB[:, 0, :], sg_ln_g_d[l].partition_broadcast(128), [], [bsgB])
            dma("sync", sgB[:, 1, :], sg_ln_b_d[l].partition_broadcast(128), [], [bsgB])
            bdg = [B("diag", i, region=diag_tiles[i][1]) for i in range(3 * NPE)]
            bws = B("wsst", region="H")
            dma("sync", wsst[:], w_s_d[l].rearrange("h t s -> t h s"), [], [bws])
            bk, bb = next_bank()
            transposes([(bk[:, h * 128:(h + 1) * 128], wsst[:, h, :], ident[:]) for h in range(4)], [bws, bC], [bb])
            bwsT = B("wsT")
            for h in range(4):
                tt(wsT[:, h, :], bk[:, h * 128:(h + 1) * 128], mask[:], ALU.mult, [bb, bC], [bwsT])

            brts = [pend[j] for j in range(NJ)]

            ybufs = [ybuf, ybuf1]
            by = [[B("ybuf", i, ct, region=("A", "H")[i]) for ct in range(3)] for i in range(2)]
            bz = B("zc", region="D")
            pool(lambda e: e.memset(ybuf[:, :, 0:HALO], 0.0), [], by[0])
            pool(lambda e: e.memset(zc[:, :, 0:ZH], 0.0), [], [bz])
            bh = [B("hT", k, region="H") for k in range(8)]
            bug = B("ug", region="D")
            bvg = [B("vg", cq, region="H") for cq in range(4)]
            bacc = [B("acc", ct, region="A") for ct in range(3)]
            baccb = [B("accb", ct, region="A") for ct in range(3)]
            bsqb = [B("sqb", ct, region="A") for ct in range(3)]
            pslots = [(accb[:, i, :], baccb[i]) for i in range(3)] + [(sqb[:, i, :], bsqb[i]) for i in range(3)]
            pslot_rr = [0]
            bsil = B("sil", region="A")
            bya = B("ya", region="H")
            byb = B("yb", region="H")
            byc = B("yc", region="H")
            bvn = [B("vn", cq, region="H") for cq in range(4)]
            bst = B("bnst")
            bsA = B("sA", region="E")
            bsB = B("sB", region="E")
            bpp = B("pp", region="D")
            blt = B("lnt")
            W = ZH + NT

            def proj(c0, m, bi):
                bk, bb = next_bank()
                mm(bk[0:m, :], [(w_in[:, k, c0:c0 + m], hT[:, k, :]) for k in range(8)],
                   bh + [B("w_in", bi)], [bb])
                return bk, bb

            def fa(j):
                rms_apply(j, brts[j], rtok, lambda k, l=l: pvcol(l, PV_G1 + k), hT, lambda k: bh[k], slice(0, NT))

            def fb1(j):
                for h in range(4):
                    ku, bu = proj(2 * DA + 96 * h, 96, 2)
                    act_op(ug[0:96, h, :], ku[0:96, :], AF.Gelu, [bu], [bug])
                for cq in range(4):
                    bk, bb = next_bank()
                    mm(bk[:, 0:DB], [(hT[:, k, cq * 128:(cq + 1) * 128], w_in[:, k, 2 * DA + DB:2 * DA + 2 * DB])
                                     for k in range(8)], bh + [B("w_in", 3)], [bb])
                    act_op(vg[:, cq, :], bk[:, 0:DB], AF.Gelu, [bb], [bvg[cq]])

            def fb2(j):
                yb_ = ybufs[j % 2]
                byj = by[j % 2]
                if j > 0:
                    for ct in range(3):
                        act_op(yb_[:, ct, 0:HALO], ybufs[(j - 1) % 2][:, ct, NT:NT + HALO], AF.Copy,
                               [by[(j - 1) % 2][ct]], [byj[ct]])
                for m in range(2):
                    kz, bzz = proj(2 * DA + 2 * DB + 128 * m, 128, 4)
                    act_op(zc[:, m, ZH:ZH + NT], kz[:], AF.Copy, [bzz], [bz])
                glu = []
                for ct in range(3):
                    ka, ba = proj(ct * 128, 128, 0)
                    kg, bg = proj(DA + ct * 128, 128, 1)
                    act_op(yb_[:, ct, HALO:HALO + NT], kg[:], AF.Sigmoid, [bg], [byj[ct]])
                    glu.append((ct, ka, ba))
                return glu

            def glu_mult(j, ct, ka, ba):
                yb_ = ybufs[j % 2]
                tt(yb_[:, ct, HALO:HALO + NT], yb_[:, ct, HALO:HALO + NT], ka[:], ALU.mult,
                   [by[j % 2][ct], ba], [by[j % 2][ct]])

            def lnv_stats(j):
                for cq in range(4):
                    vec(lambda e, cq=cq: e.bn_stats(out=bnst[0][:, cq * 6:cq * 6 + 6], in_=vg[:, cq, :]), [bvg[cq]], [bst])
                for cq in range(4):
                    vec(lambda e, cq=cq: e.bn_aggr(out=bnst[1][:, 2 * cq:2 * cq + 2], in_=bnst[0][:, cq * 6:cq * 6 + 6]),
                        [bst], [bst])
                mvv = bnst[1][:, 0:8].rearrange("p (q t) -> p q t", t=2)
                ts(bnst[1][:, 8:12], mvv[:, :, 1], LN_EPS, None, ALU.add, None, [bst], [bst])
                pool(lambda e: e.tensor_tensor(out=bnst[1][:, 8:12], in0=bnst[1][:, 8:12], in1=mhalf[:, 0:4], op=ALU.pow),
                     [bst, bC], [bst])

            def lnv_apply(j):
                for cq in range(4):
                    stt(vg[:, cq, :], vg[:, cq, :], bnst[1][:, 2 * cq:2 * cq + 1], sgB[:, 0, :], ALU.subtract, ALU.mult,
                        [bvg[cq], bst, bsgB], [bvg[cq]])
                    stt(vn[:, cq, :], vg[:, cq, :], bnst[1][:, 8 + cq:9 + cq], sgB[:, 1, :], ALU.mult, ALU.add,
                        [bvg[cq], bst, bsgB], [bvn[cq]])

            def pool_branch(j):
                tt(sA[:, :, 2:W], zc[:, :, 2:W], zc[:, :, 1:W - 1], ALU.add, [bz], [bsA])
                tt(sB[:, :, 4:W], sA[:, :, 4:W], sA[:, :, 2:W - 2], ALU.add, [bsA], [bsB])

                def pool_out(src, p0, m, w):
                    stt(pp[p0:p0 + 64, m, :], src[p0:p0 + 64, m, ZH:W], 1.0 / w, zc[p0:p0 + 64, m, ZH:W],
                        ALU.mult, ALU.subtract, [bsA, bsB, bz], [bpp])
                    if j == 0:
                        tt(src[p0:p0 + 64, m, ZH:2 * ZH], src[p0:p0 + 64, m, ZH:2 * ZH], invcnt[p0:p0 + 64, m, :],
                           ALU.mult, [bsA, bsB, bC], [bsA, bsB])
                        tt(pp[p0:p0 + 64, m, 0:ZH], src[p0:p0 + 64, m, ZH:2 * ZH], zc[p0:p0 + 64, m, ZH:2 * ZH],
                           ALU.subtract, [bsA, bsB, bz], [bpp])
                pool_out(sA, 0, 0, 2)
                pool_out(sB, 64, 0, 4)
                tt(sA[:, 1, 8:W], sB[:, 1, 8:W], sB[:, 1, 4:W - 4], ALU.add, [bsB], [bsA])
                pool_out(sA, 0, 1, 8)
                tt(sB[64:128, 1, 16:W], sA[64:128, 1, 16:W], sA[64:128, 1, 8:W - 8], ALU.add, [bsA], [bsB])
                pool_out(sB, 64, 1, 16)
                if j + 1 < NJ:
                    act_op(zc[:, :, 0:ZH], zc[:, :, NT:NT + ZH], AF.Copy, [bz], [bz])

            convps = {}

            def tap30(j):
                for ct in range(3):
                    act_op(acc[:, ct, :], ybufs[j % 2][:, ct, HALO:HALO + NT], AF.Identity, [by[j % 2][ct], bPV],
                           [bacc[ct]], scale=cwT[:, l, ct, CW - 1:CW], bias=pvcol(l, PV_CB + ct))

            def tap_ops(j):
                a_ = [(ct, k) for k in range(NPE, NPE + NACT) for ct in range(3)]
                d_ = [(ct, k) for k in range(NPE + NACT, CW - 1) for ct in range(3)]
                out_ = []
                for i in range(max(len(a_), len(d_))):
                    if i < len(d_):
                        out_.append(d_[i])
                    if i < len(a_):
                        out_.append(a_[i])
                return out_

            def tap(j, ct, k):
                if k >= NPE + NACT:
                    stt(acc[:, ct, :], ybufs[j % 2][:, ct, k:k + NT], cwT[:, l, ct, k:k + 1], acc[:, ct, :],
                        ALU.mult, ALU.add, [by[j % 2][ct], bPV, bacc[ct]], [bacc[ct]])
                    return
                slot, bsl = pslots[pslot_rr[0] % len(pslots)]
                pslot_rr[0] += 1
                act_op(slot, ybufs[j % 2][:, ct, k:k + NT], AF.Identity, [by[j % 2][ct], bPV], [bsl],
                       scale=cwT[:, l, ct, k:k + 1])
                bk, bb = convps[j][ct]
                mm(bk[:], [(identb[:], slot)], [bsl, bC], [bb], first=False, last=(k == NPE + NACT - 1))

            def conv_pe(j):
                convps[j] = []
                for ct in range(3):
                    bk, bb = next_bank()
                    mm(bk[:], [(diag_tiles[ct * NPE + k][0][:], ybufs[j % 2][:, ct, k:k + NT]) for k in range(NPE)],
                       [by[j % 2][ct]] + bdg[ct * NPE:(ct + 1) * NPE], [bb], first=True, last=(NACT == 0))
                    convps[j].append((bk, bb))

            def acc_add(j):
                for ct in range(3):
                    tt(acc[:, ct, :], acc[:, ct, :], convps[j][ct][0][:], ALU.add, [bacc[ct], convps[j][ct][1]],
                       [bacc[ct]])

            fa(0)
            for ct in range(3):
                for k in range(NPE):
                    ts(diag_tiles[ct * NPE + k][0][:], ident[:], cwT[:, l, ct, k:k + 1], None, ALU.mult, None,
                       [bC, bPV], [bdg[ct * NPE + k]])
            fb1(0)
            lnv_stats(0)
            glu0 = fb2(0)
            lnv_apply(0)
            for g_ in glu0:
                glu_mult(0, *g_)
            pool_branch(0)
            tap30(0)
            conv_pe(0)
            for ct, k in tap_ops(0):
                tap(0, ct, k)

            brts2 = {}
            for j in range(NJ):
                cols = slice(j * NT, (j + 1) * NT)
                nj = j + 1 if j + 1 < NJ else None
                acc_add(j)
                for ct in range(3):
                    act_op(accb[:, ct, :], acc[:, ct, :], AF.Copy, [bacc[ct]], [baccb[ct]])
                    act_op(sqb[:, ct, :], acc[:, ct, :], AF.Square, [bacc[ct]], [bsqb[ct]])
                if nj is not None:
                    fa(nj)
                pmb = []
                for cq in range(4):
                    bk2, bb2 = next_bank()
                    mms([(bk2[0:96, h * 128:(h + 1) * 128], [(vn[:, cq, 96 * h:96 * h + 96], wsT[:, h, :])])
                         for h in range(4)], [bvn[cq], bwsT], [bb2])
                    pmb.append((bk2, bb2))
                for m in range(2):
                    bk2, bb2 = next_bank()
                    mm(bk2[:], [(wpl[:, m, :], pp[:, m, :])], [bpp, B("wpl")], [bb2])
                    act_op(yc[:, m, :], bk2[:], AF.Identity, [bb2, bPV], [byc], scale=pvcol(l, PV_PS + m))
                bk, bb = next_bank()
                groups = []
                for q in range(4):
                    groups.append((bk[:, 2 * q:2 * q + 1],
                                   [(accb[:, ct, q * 128:(q + 1) * 128], ocS[:, 1:2]) for ct in range(3)]))
                    groups.append((bk[:, 2 * q + 1:2 * q + 2],
                                   [(sqb[:, ct, q * 128:(q + 1) * 128], ocS[:, 1:2]) for ct in range(3)]))
                mms(groups, baccb + bsqb + [bC], [bb])
                ts(rtok2[:, 0:8], bk[:, 0:8], 1.0 / DA, None, ALU.mult, None, [bb], [blt])
                me = rtok2[:, 0:8].rearrange("p (q t) -> p q t", t=2)
                tt(rtok2[:, 8:12], me[:, :, 0], me[:, :, 0], ALU.mult, [blt], [blt])
                tt(rtok2[:, 8:12], me[:, :, 1], rtok2[:, 8:12], ALU.subtract, [blt], [blt])
                ts(rtok2[:, 8:12], rtok2[:, 8:12], LN_EPS, None, ALU.add, None, [blt], [blt])
                pool(lambda e: e.tensor_tensor(out=rtok2[:, 8:12], in0=rtok2[:, 8:12], in1=mhalf[:, 0:4], op=ALU.pow),
                     [blt, bC], [blt])
                for cq in range(4):
                    bk2, bb2 = pmb[cq]
                    t = tlnv[cq % 2]
                    bt = (bsA, bsB)[cq % 2]
                    tt(t[0:96].rearrange("p (h t) -> p h t", t=128), bk2[0:96, :].rearrange("p (h t) -> p h t", t=128),
                       BT[0:96], ALU.add, [bb2, bbs], [bt])
                    tt(yb[0:96, :, cq * 128:(cq + 1) * 128], ug[0:96, :, cq * 128:(cq + 1) * 128],
                       t[0:96].rearrange("p (h t) -> p h t", t=128), ALU.mult, [bug, bt], [byb])
                if nj is not None:
                    fb1(nj)
                stt(rtok2[:, 12:16], me[:, :, 0], -1.0, rtok2[:, 8:12], ALU.mult, ALU.mult, [blt], [blt])
                krs, brs = bcast_tok([rtok2[:, 8 + q:9 + q] for q in range(4)], [blt])
                knm, bnm = bcast_tok([rtok2[:, 12 + q:13 + q] for q in range(4)], [blt])
                for ct in range(3):
                    t = tlnv[ct % 2]
                    bt = (bsA, bsB)[ct % 2]
                    tt(t, acc[:, ct, :], krs[:], ALU.mult, [bacc[ct], brs], [bt])
                    tt(t, t, knm[:], ALU.add, [bt, bnm], [bt])
                    act_op(sil[:, ct, :], t, AF.Silu, [bt, bPV], [bsil],
                           scale=pvcol(l, PV_LNG + ct), bias=pvcol(l, PV_LNB + ct))
                glu = []
                if nj is not None:
                    lnv_stats(nj)
                    glu = fb2(nj)
                    lnv_apply(nj)
                    for g_ in glu:
                        glu_mult(nj, *g_)
                for co in range(3):
                    bk2, bb2 = next_bank()
                    mm(bk2[:], [(w_pw[:, ci, co * 128:(co + 1) * 128], sil[:, ci, :]) for ci in range(3)],
                       [bsil, B("w_pw")], [bb2])
                    act_op(ya[:, co, :], bk2[:], AF.Copy, [bb2], [bya])
                if nj is not None:
                    pool_branch(nj)
                    tap30(nj)
                    conv_pe(nj)
                ptaps = tap_ops(nj) if nj is not None else []
                per = (len(ptaps) + 7) // 8
                for n in range(8):
                    ncol = slice(n * 128, (n + 1) * 128)
                    bk2, bb2 = next_bank()
                    pairs = [(wo_b[0:96, h, ncol], yb[0:96, h, :]) for h in range(4)]
                    pairs += [(wo_c[:, m, ncol], yc[:, m, :]) for m in range(2)]
                    pairs += [(wo_a[:, i, ncol], ya[:, i, :]) for i in range(3)]
                    mm(bk2[:], pairs, [bya, byb, byc, B("wo_a", region="R"), B("wo_b", region="R"),
                                       B("wo_c", region="R")], [bb2])
                    for ct, k in ptaps[n * per:(n + 1) * per]:
                        tap(nj, ct, k)
                    tt(xT[:, n, cols], xT[:, n, cols], bk2[:], ALU.add, [B("xT", n, j), bb2], [B("xT", n, j)])
                brts2[j] = rms_tok_stats(j, rtokB)

            P.new_phase(["A", "H", "D", "E", "R"])
            nfp = NF // 2
            for fp in range(min(NRING, nfp)):
                load_ring(l, fp, fp % NRING)
            load_wd(l, 0)
            bh2 = [[B("h2T", k, j, region="H") for k in range(8)] for j in range(NJ)]
            n2b = [rms_bcast(j, brts2[j], rtokB) for j in range(NJ)]

            def n2scale(j):
                rms_scale(j, n2b[j], lambda k, l=l: pvcol(l, PV_G2 + k), h2T, lambda k, j=j: bh2[j][k],
                          slice(j * NT, (j + 1) * NT))
            n2scale(0)
            nxt = None
            if l + 1 < L:
                nxt = l + 1
            elif s + 1 < nseq:
                nxt = 0
            if nxt is not None:
                load_mixer_w1(nxt)
            for ci, (f0, nf) in enumerate(CHUNKS):
                bact = [[B("act", fi, j, region="A") for j in range(NJ)] for fi in range(nf)]
                for fi in range(nf):
                    f = f0 + fi
                    fp, half = f // 2, f % 2
                    slot = fp % NRING
                    fc = slice(half * 128, half * 128 + 128)
                    for j in range(NJ):
                        cols = slice(j * NT, (j + 1) * NT)
                        kg, bg = next_bank()
                        mm(kg[:], [(ring[slot][:, 0, k, fc], h2T[:, k, cols]) for k in range(8)],
                           bh2[j] + [B("ring", slot, 0, region="R")], [bg])
                        ku, bu = next_bank()
                        mm(ku[:], [(ring[slot][:, 1, k, fc], h2T[:, k, cols]) for k in range(8)],
                           bh2[j] + [B("ring", slot, 1, region="R")], [bu])
                        t = tlnv[(fi * NJ + j) % 2]
                        bt = B(("sA", "sB")[(fi * NJ + j) % 2], region="E")
                        act_op(t, kg[:], AF.Silu, [bg], [bt])
                        if ci == 0 and fi == 0 and j + 1 < NJ:
                            n2scale(j + 1)
                        tt(act[:, fi, cols], t, ku[:], ALU.mult, [bt, bu], [bact[fi][j]])
                    if half == 1 and fp + NRING < nfp:
                        load_ring(l, fp + NRING, slot)
                last = ci + 1 == len(CHUNKS)
                order = [(n, j) for j in range(NJ) for n in range(8)] if last else \
                        [(n, j) for n in range(8) for j in range(NJ)]
                for n, j in order:
                    ncol = slice(n * 128, (n + 1) * 128)
                    cols = slice(j * NT, (j + 1) * NT)
                    bk, bb = next_bank()
                    mm(bk[:], [(wd[:, fi, ncol], act[:, fi, cols]) for fi in range(nf)],
                       [bact[fi][j] for fi in range(nf)] + [B("wd", region="D")], [bb])
                    tt(xT[:, n, cols], xT[:, n, cols], bk[:], ALU.add, [B("xT", n, j), bb], [B("xT", n, j)])
                    if last and n == 7 and j > 0:
                        pend[j - 1] = rms_tok_stats(j - 1, rtok)
                if last:
                    pend[NJ - 1] = rms_tok_stats(NJ - 1, rtok)
                if ci + 1 < len(CHUNKS):
                    load_wd(l, ci + 1)
            if nxt is not None:
                P.new_phase(["R"])
                load_mixer_wout(nxt)

        P.new_phase(["A", "H", "D", "E"])
        if s + 1 < nseq:
            x_dma(s + 1, 0)
            x_dma(s + 1, 1)
        byn = [B("yn", k, region="A") for k in range(8)]

        def fin_apply(j):
            rms_apply(j, pend[j], rtok, lambda k: gFv[:, k:k + 1], yn, lambda k: byn[k], slice(0, NT))

        def fin_store(j):
            for q in range(4):
                tb = j * 4 + q
                os_ = ostage[tb % 2]
                bos = B("ostage", tb % 2, region="A")
                for hf in range(2):
                    bk, bb = next_bank()
                    transposes([(bk[:, kk * 128:(kk + 1) * 128], yn[:, hf * 4 + kk, q * 128:(q + 1) * 128], ident[:])
                                for kk in range(4)], byn[hf * 4:hf * 4 + 4] + [bC], [bb])
                    if hf == 0:
                        act_op(os_[:, 0:512], bk[:], AF.Copy, [bb], [bos])
                    else:
                        vec(lambda e, bk=bk, os_=os_: e.tensor_copy(out=os_[:, 512:1024], in_=bk[:]), [bb], [bos])
                dma("sync", out_d[s, tb * 128:(tb + 1) * 128, :], os_[:], [bos], [], final=True)

        fin_apply(0)
        for j in range(NJ):
            fin_store(j)
            if j + 1 < NJ:
                fin_apply(j + 1)
            if s + 1 < nseq:
                x_tr(s + 1, j)
                if j + 2 < NJ:
                    x_dma(s + 1, j + 2)

    P.emit(nc)
    return nc


_NC_CACHE = {}


def _get_nc(nseq, depth):
    key = (nseq, depth)
    if key not in _NC_CACHE:
        _NC_CACHE[key] = build_program(nseq=nseq, depth=depth)
    return _NC_CACHE[key]


_WNAMES = ["norm1_g", "w_in", "conv_w", "conv_b", "conv_ln_g", "conv_ln_b", "w_pw", "sg_ln_g", "sg_ln_b",
           "w_s", "b_s", "w_pool", "pool_scale", "w_out", "norm2_g", "w_gate_up", "w_down", "final_g"]


def kernel(**inputs):
    x = np.ascontiguousarray(np.asarray(inputs["x"], dtype=np.float32))
    bsz = x.shape[0]
    nseq = bsz // N_CORES
    depth = int(np.asarray(inputs["w_in"]).shape[0])
    nc = _get_nc(nseq, depth)
    ws = {n: np.ascontiguousarray(np.asarray(inputs[n], dtype=np.float32)) for n in _WNAMES}
    in_maps = []
    for c in range(N_CORES):
        m = dict(ws)
        m["x"] = x[c * nseq:(c + 1) * nseq]
        in_maps.append(m)
    res = run_bass_kernel_spmd(nc, in_maps, core_ids=list(range(N_CORES)))
    return np.concatenate([np.asarray(r["out"]) for r in res.results], axis=0).astype(np.float32)
```

```python
import numpy as np
import concourse.bass as bass
import concourse.mybir as mybir
from concourse.bass_utils import run_bass_kernel_spmd

F32 = mybir.dt.float32
BF16 = mybir.dt.bfloat16
AF = mybir.ActivationFunctionType
ALU = mybir.AluOpType

D = 1024
S = 2048
NT = 512
NJ = S // NT
DA = 384
DB = 384
DC = 256
DIN = 1792
DFF = 2816
NF = DFF // 128
CW = 31
HALO = CW - 1
ZH = 16
RMS_EPS = 1e-6
LN_EPS = 1e-5
N_CORES = 8
CHUNKS = [(0, 6), (6, 6), (12, 6), (18, 4)]
NRING = 3
NACT = 6
NPE = 18


class Buf:
    __slots__ = ("name", "w", "r", "region")

    def __init__(self, name, region=None):
        self.name = name
        self.w = None
        self.r = []
        self.region = region


class Eng:
    def __init__(self, name):
        self.name = name
        self.ops = []
        self.gen = 0
        self.cnt = 0
        self.waited = {}
        self.dma_rr = 0
        self.dma_vals = {}

    @property
    def semkey(self):
        return ("e", self.name, self.gen)


SEM_LIMIT = 30000
N_DMA_SEMS = {"gpsimd": 20, "sync": 12, "scalar": 4}


class Prog:
    def __init__(self):
        self.eng = {n: Eng(n) for n in ("tensor", "vector", "scalar", "gpsimd", "sync")}
        self.bufs = {}
        self.fence = {}
        self.region_last = {}
        self.region_dma = {}
        self.final_tokens = []

    def buf(self, *key, region=None):
        b = self.bufs.get(key)
        if b is None:
            b = Buf(key, region)
            self.bufs[key] = b
        return b

    def new_phase(self, regions):
        for r in regions:
            toks = list(self.region_last.get(r, {}).values()) + list(self.region_dma.get(r, []))
            self.fence[r] = toks
            self.region_dma[r] = []

    def add(self, engname, fn, reads=(), writes=(), dma=False, final=False):
        e = self.eng[engname]
        deps = {}

        def need(tok):
            if tok is None:
                return
            k, v = tok
            if deps.get(k, 0) < v:
                deps[k] = v

        for b in reads:
            need(b.w)
            if b.region is not None:
                for t in self.fence.get(b.region, ()):
                    need(t)
        for b in writes:
            need(b.w)
            for t in b.r:
                need(t)
            if b.region is not None:
                for t in self.fence.get(b.region, ()):
                    need(t)
        if dma:
            n = N_DMA_SEMS[engname]
            idx = e.dma_rr % n
            e.dma_rr += 1
            k = ("d", engname, idx)
            prev = e.dma_vals.get(k, 0)
            if prev:
                need((k, prev))
            tok = (k, prev + 16)
            e.dma_vals[k] = prev + 16
        else:
            if e.cnt >= SEM_LIMIT:
                e.gen += 1
                e.cnt = 0
            e.cnt += 1
            tok = (e.semkey, e.cnt)
        waits = []
        for k, v in deps.items():
            if engname == "tensor" and k == e.semkey:
                continue
            if e.waited.get(k, 0) >= v:
                continue
            e.waited[k] = v
            waits.append((k, v))
        e.ops.append((fn, waits, tok, dma))
        ws = set(id(b) for b in writes)
        for b in writes:
            b.w = tok
            b.r = []
        for b in reads:
            if id(b) not in ws:
                b.r.append(tok)
        for b in list(reads) + list(writes):
            if b.region is not None:
                if dma:
                    self.region_dma.setdefault(b.region, []).append(tok)
                else:
                    self.region_last.setdefault(b.region, {})[e.semkey[:2]] = tok
        if final:
            self.final_tokens.append(tok)
        return tok

    def emit(self, nc):
        keys = set()
        for e in self.eng.values():
            for fn, waits, tok, dma in e.ops:
                keys.add(tok[0])
                for k, v in waits:
                    keys.add(k)
        sems = {}
        for i, k in enumerate(sorted(keys, key=str)):
            sems[k] = nc.alloc_semaphore("s%d" % i)
        finals = self.final_tokens
        with nc.Block() as block:
            def mk(e):
                def body(eh):
                    for fn, waits, tok, dma in e.ops:
                        for k, v in waits:
                            eh.wait_ge(sems[k], v)
                        ins = fn(eh)
                        ins.then_inc(sems[tok[0]], 16 if dma else 1)
                    if e.name == "sync":
                        for k, v in finals:
                            eh.wait_ge(sems[k], v)
                return body
            for name, e in self.eng.items():
                if not e.ops and name != "sync":
                    continue
                getattr(block, name)(mk(e))


def build_program(nseq=2, depth=2):
    nc = bass.Bass("TRN2", target_bir_lowering=False)
    P = Prog()
    L = depth

    def din(name, shape):
        return nc.dram_tensor(name, list(shape), F32, kind="ExternalInput").ap()

    x_d = din("x", [nseq, S, D])
    norm1_g_d = din("norm1_g", [L, D])
    w_in_d = din("w_in", [L, D, DIN])
    conv_w_d = din("conv_w", [L, CW, DA])
    conv_b_d = din("conv_b", [L, DA])
    conv_ln_g_d = din("conv_ln_g", [L, DA])
    conv_ln_b_d = din("conv_ln_b", [L, DA])
    w_pw_d = din("w_pw", [L, DA, DA])
    sg_ln_g_d = din("sg_ln_g", [L, DB])
    sg_ln_b_d = din("sg_ln_b", [L, DB])
    w_s_d = din("w_s", [L, 4, 128, 128])
    b_s_d = din("b_s", [L, 4, 128])
    w_pool_d = din("w_pool", [L, 4, 64, 64])
    pool_scale_d = din("pool_scale", [L, DC])
    w_out_d = din("w_out", [L, D, D])
    norm2_g_d = din("norm2_g", [L, D])
    w_gu_d = din("w_gate_up", [L, D, 2 * DFF])
    w_down_d = din("w_down", [L, DFF, D])
    final_g_d = din("final_g", [D])
    out_d = nc.dram_tensor("out", [nseq, S, D], F32, kind="ExternalOutput").ap()

    base = (nc.sbuf_base + 31) // 32 * 32
    top = nc.sbuf_top
    cur = [base]

    def alloc(name, shape, dt, at=None):
        nbytes = int(np.prod(shape[1:])) * (4 if dt == F32 else 2)
        nbytes = (nbytes + 31) // 32 * 32
        if at is None:
            off = cur[0]
            cur[0] += nbytes
        else:
            off = at
        assert off + nbytes <= top, (name, off, nbytes, top)
        return nc.alloc_sbuf_tensor_at(name, list(shape), dt, offset=off), off, nbytes

    class Region:
        def __init__(self, name, size):
            self.name = name
            self.size = size
            self.base = cur[0]
            cur[0] += size
            assert cur[0] <= top, ("region overflow", name, cur[0], top)

        def carve(self):
            return Carver(self)

    class Carver:
        def __init__(self, reg):
            self.reg = reg
            self.off = reg.base

        def alloc(self, name, shape, dt):
            t, off, nb = alloc(name, shape, dt, at=self.off)
            self.off += nb
            assert self.off <= self.reg.base + self.reg.size, ("carve overflow", self.reg.name, name)
            return t

    ident = alloc("ident", [128, 128], F32)[0]
    onesF = alloc("onesF", [128, 128], F32)[0]
    mask = alloc("mask", [128, 128], F32)[0]
    identb = alloc("identb", [128, 128], BF16)[0]
    mhalf = alloc("mhalf", [128, 16], F32)[0]
    invcnt = alloc("invcnt", [128, 2, 16], F32)[0]
    ocS = alloc("ocS", [128, 2], BF16)[0]
    pv = alloc("pv", [128, L, 32], F32)[0]
    gFv = alloc("gFv", [128, 8], F32)[0]
    cwT = alloc("cwT", [128, L, 3, CW], F32)[0]
    BT = alloc("BT", [128, 4, 128], F32)[0]
    sgB = alloc("sgB", [128, 2, DB], F32)[0]
    sqr = [alloc("sqr%d" % i, [128, NT], BF16)[0] for i in range(2)]
    rtok = alloc("rtok", [128, 16], F32)[0]
    rtok2 = alloc("rtok2", [128, 16], F32)[0]
    rtokB = alloc("rtokB", [128, 16], F32)[0]
    dg = [alloc("dg%d" % i, [128, 128], F32)[0] for i in range(2)]
    bnst = [alloc("bnst%d" % i, [128, 32], F32)[0] for i in range(2)]
    xT = alloc("xT", [128, 8, S], F32)[0]
    w_in = alloc("w_in_sb", [128, 8, DIN], BF16)[0]
    w_pw = alloc("w_pw_sb", [128, 3, DA], BF16)[0]
    wsT = alloc("wsT", [128, 4, 128], BF16)[0]
    wpl = alloc("wpl", [128, 2, 128], BF16)[0]
    RR = Region("R", 24576)
    RH = Region("H", 32768)
    RA = Region("A", 24576)
    RD = Region("D", 12288)
    RE = Region("E", (top - cur[0]) // 32 * 32)

    c = RR.carve()
    wo_a = c.alloc("wo_a", [128, 3, D], BF16)
    wo_b = c.alloc("wo_b", [128, 4, D], BF16)
    wo_c = c.alloc("wo_c", [128, 2, D], BF16)
    c = RR.carve()
    ring = [c.alloc("ring%d" % i, [128, 2, 8, 256], BF16) for i in range(NRING)]
    c = RH.carve()
    h2T = c.alloc("h2T", [128, 8, S], BF16)
    c = RH.carve()
    hT = c.alloc("hT", [128, 8, NT], BF16)
    ya = c.alloc("ya", [128, 3, NT], BF16)
    yb = c.alloc("yb", [128, 4, NT], BF16)
    yc = c.alloc("yc", [128, 2, NT], BF16)
    vg = c.alloc("vg", [128, 4, DB], F32)
    vn = c.alloc("vn", [128, 4, DB], BF16)
    wsst = c.alloc("wsst", [128, 4, 128], F32)
    ybuf1 = c.alloc("ybuf1", [128, 3, HALO + NT], BF16)
    c = RA.carve()
    act = c.alloc("act", [128, 6, S], BF16)
    c = RH.carve()
    xstage = [c.alloc("xstage%d" % i, [128, D], F32) for i in range(8)]
    c = RA.carve()
    vst = c.alloc("vst", [32, 128], F32)
    cwst = c.alloc("cwst", [32, DA], F32)
    c = RA.carve()
    ostage = [c.alloc("ostage%d" % i, [128, D], F32) for i in range(4)]
    gFB = c.alloc("gFB", [128, D], F32)
    c = RA.carve()
    ybuf = c.alloc("ybuf", [128, 3, HALO + NT], BF16)
    acc = c.alloc("acc", [128, 3, NT], F32)
    accb = c.alloc("accb", [128, 3, NT], BF16)
    sqb = c.alloc("sqb", [128, 3, NT], BF16)
    sil = c.alloc("sil", [128, 3, NT], BF16)
    diag_tiles = []
    cA = c
    c = RD.carve()
    wd = c.alloc("wd", [128, 6, D], BF16)
    c = RD.carve()
    ug = c.alloc("ug", [128, 4, NT], BF16)
    zc = c.alloc("zc", [128, 2, ZH + NT], F32)
    pp = c.alloc("pp", [128, 2, NT], BF16)
    cD = c
    c = RE.carve()
    sA = c.alloc("sA", [128, 2, ZH + NT], F32)
    sB = c.alloc("sB", [128, 2, ZH + NT], F32)
    tlnv = [sA[:, 0, 0:NT], sB[:, 0, 0:NT]]
    cE = c
    cR = RR.carve()
    cR.off = RR.base + 18432
    for cc, rn in ((cR, "R"), (cA, "A"), (cD, "D"), (cE, "E")):
        while cc.off + 256 <= cc.reg.base + cc.reg.size and len(diag_tiles) < 3 * NPE:
            diag_tiles.append((cc.alloc("diag%d" % len(diag_tiles), [128, 128], BF16), rn))
    assert len(diag_tiles) == 3 * NPE, len(diag_tiles)

    banks = [nc.alloc_psum_tensor("bank%d" % i, [128, 512], F32) for i in range(8)]
    bank_rr = [0]

    def next_bank():
        for _ in range(8):
            i = bank_rr[0] % 8
            bank_rr[0] += 1
            b_ = P.buf("bank", i)
            if b_.w is None or len(b_.r) > 0:
                return banks[i], b_
        raise AssertionError("all PSUM banks are held by writes whose readers were not emitted yet")

    B = P.buf

    def dma(q, out, in_, reads=(), writes=(), final=False, nonc=False):
        def fn(e, out=out, in_=in_):
            if nonc:
                return e.dma_start(out=out, in_=in_, allow_slow_non_contiguous=True)
            return e.dma_start(out=out, in_=in_)
        return P.add(q, fn, reads=reads, writes=writes, dma=True, final=final)

    def mm(out, pairs, reads, writes, first=True, last=True):
        def fn(e, out=out, pairs=pairs, first=first, last=last):
            n = len(pairs)
            ins = None
            for i, (l, r) in enumerate(pairs):
                ins = e.matmul(out, l, r, start=(first and i == 0), stop=(last and i == n - 1))
            return ins
        return P.add("tensor", fn, reads=reads, writes=writes)

    def mms(groups, reads, writes):
        def fn(e, groups=groups):
            ins = None
            for out, pairs in groups:
                n = len(pairs)
                for i, (l, r) in enumerate(pairs):
                    ins = e.matmul(out, l, r, start=(i == 0), stop=(i == n - 1))
            return ins
        return P.add("tensor", fn, reads=reads, writes=writes)

    def transposes(items, reads, writes):
        def fn(e, items=items):
            ins = None
            for out, in_, idn in items:
                ins = e.transpose(out, in_, idn)
            return ins
        return P.add("tensor", fn, reads=reads, writes=writes)

    def act_op(out, in_, func, reads, writes, scale=None, bias=None):
        def fn(e, out=out, in_=in_, func=func, scale=scale, bias=bias):
            kw = {}
            if scale is not None:
                kw["scale"] = scale
            if bias is not None:
                kw["bias"] = bias
            return e.activation(out=out, in_=in_, func=func, **kw)
        return P.add("scalar", fn, reads=reads, writes=writes)

    def vec(fn, reads, writes):
        return P.add("vector", fn, reads=reads, writes=writes)

    def pool(fn, reads, writes):
        return P.add("gpsimd", fn, reads=reads, writes=writes)

    def tt(out, in0, in1, op, reads, writes, eng="vector"):
        return P.add(eng, lambda e, out=out, in0=in0, in1=in1, op=op: e.tensor_tensor(out=out, in0=in0, in1=in1, op=op),
                     reads=reads, writes=writes)

    def ts(out, in0, s1, s2, op0, op1, reads, writes, eng="vector"):
        def fn(e, out=out, in0=in0, s1=s1, s2=s2, op0=op0, op1=op1):
            if op1 is None:
                return e.tensor_scalar(out=out, in0=in0, scalar1=s1, scalar2=None, op0=op0)
            return e.tensor_scalar(out=out, in0=in0, scalar1=s1, scalar2=s2, op0=op0, op1=op1)
        return P.add(eng, fn, reads=reads, writes=writes)

    def stt(out, in0, scalar, in1, op0, op1, reads, writes):
        return vec(lambda e, out=out, in0=in0, scalar=scalar, in1=in1, op0=op0, op1=op1:
                   e.scalar_tensor_tensor(out=out, in0=in0, scalar=scalar, in1=in1, op0=op0, op1=op1),
                   reads=reads, writes=writes)

    def x_dma(s, j):
        for q in range(4):
            tb = 4 * j + q
            dma("sync", xstage[tb % 8][:], x_d[s, tb * 128:(tb + 1) * 128, :], [], [B("xstage", tb % 8, region="H")])

    x_dma(0, 0)
    x_dma(0, 1)

    bC = B("consts")
    pool(lambda e: e.memset(onesF[:], 1.0), [], [bC])
    pool(lambda e: e.affine_select(out=ident[:], in_=onesF[:], pattern=[[-1, 128]], compare_op=ALU.is_equal,
                                   fill=0.0, base=0, channel_multiplier=1), [bC], [bC])
    pool(lambda e: e.affine_select(out=mask[:], in_=onesF[:], pattern=[[1, 128]], compare_op=ALU.is_ge,
                                   fill=0.0, base=0, channel_multiplier=-1), [bC], [bC])
    pool(lambda e: e.memset(mhalf[:], -0.5), [], [bC])
    pool(lambda e: e.tensor_copy(out=identb[:], in_=ident[:]), [bC], [bC])
    pool(lambda e: e.memset(ocS[:, 0:1], 1.0 / 1024.0), [], [bC])
    pool(lambda e: e.memset(ocS[:, 1:2], 1.0), [], [bC])
    for m in range(2):
        pool(lambda e, m=m: e.iota(invcnt[:, m, :], [[1, ZH]], base=1, channel_multiplier=0,
                                    allow_small_or_imprecise_dtypes=True), [], [bC])
    for m, p0, w in ((0, 0, 2.0), (0, 64, 4.0), (1, 0, 8.0), (1, 64, 16.0)):
        ts(invcnt[p0:p0 + 64, m, :], invcnt[p0:p0 + 64, m, :], w, None, ALU.min, None, [bC], [bC])
    vec(lambda e: e.reciprocal(out=invcnt[:], in_=invcnt[:]), [bC], [bC])

    PV_G1, PV_G2, PV_CB, PV_LNG, PV_LNB, PV_PS = 0, 8, 16, 19, 22, 25
    bPV = B("pv")

    WIN_BLOCKS = [(0, 384), (384, 768), (768, 1152), (1152, 1536), (1536, 1792)]

    def load_mixer_w1(l):
        wv = w_in_d[l].rearrange("(k p) c -> p k c", p=128)
        for bi, (c0, c1) in enumerate(WIN_BLOCKS):
            dma("gpsimd", w_in[:, :, c0:c1], wv[:, :, c0:c1], [], [B("w_in", bi)])
        dma("gpsimd", w_pw[:], w_pw_d[l].rearrange("(k p) c -> p k c", p=128), [], [B("w_pw")])
        pool(lambda e: e.memset(wpl[:], 0.0), [], [B("wpl")])
        for g in range(4):
            m, h = g // 2, g % 2
            dma("gpsimd", wpl[64 * h:64 * h + 64, m, 64 * h:64 * h + 64], w_pool_d[l, g], [], [B("wpl")])

    def load_mixer_wout(l):
        dma("gpsimd", wo_a[:], w_out_d[l, 0:DA, :].rearrange("(k p) n -> p k n", p=128), [], [B("wo_a", region="R")])
        dma("gpsimd", wo_b[0:96], w_out_d[l, DA:DA + DB, :].rearrange("(k p) n -> p k n", p=96), [],
            [B("wo_b", region="R")])
        dma("gpsimd", wo_c[:], w_out_d[l, DA + DB:D, :].rearrange("(k p) n -> p k n", p=128), [],
            [B("wo_c", region="R")])

    def load_ring(l, fp, slot):
        wv = w_gu_d[l].rearrange("(k p) c -> p k c", p=128)
        dma("gpsimd", ring[slot][:, 0], wv[:, :, 256 * fp:256 * fp + 256], [], [B("ring", slot, 0, region="R")])
        dma("gpsimd", ring[slot][:, 1], wv[:, :, DFF + 256 * fp:DFF + 256 * fp + 256], [],
            [B("ring", slot, 1, region="R")])

    def load_wd(l, ci):
        f0, nf = CHUNKS[ci]
        dma("gpsimd", wd[:, 0:nf, :], w_down_d[l, f0 * 128:(f0 + nf) * 128, :].rearrange("(f p) n -> p f n", p=128),
            [], [B("wd", region="D")])

    def rms_tok_stats(j, dst_rtok):
        cols = slice(j * NT, (j + 1) * NT)
        bk, bb = next_bank()
        groups = [[] for _ in range(4)]
        rd = []
        for k in range(8):
            sq = sqr[k % 2]
            bsq = B("sqr", k % 2)
            act_op(sq[:], xT[:, k, cols], AF.Square, [B("xT", k, j)], [bsq])
            mmg = []
            for q in range(4):
                mmg.append((bk[:, q:q + 1], sq[:, q * 128:(q + 1) * 128], ocS[:, 0:1], k))
            def fn(e, mmg=mmg):
                ins = None
                for out, l, r, k in mmg:
                    ins = e.matmul(out, l, r, start=(k == 0 and out is mmg[0][0]), stop=(k == 7), skip_group_check=True)
                return ins
            P.add("tensor", fn, reads=[bsq, bC], writes=[bb])
        brt = B("rtok", id(dst_rtok), j)
        ts(dst_rtok[:, 4 * j:4 * j + 4], bk[:, 0:4], RMS_EPS, None, ALU.add, None, [bb], [brt])
        pool(lambda e, j=j: e.tensor_tensor(out=dst_rtok[:, 4 * j:4 * j + 4], in0=dst_rtok[:, 4 * j:4 * j + 4],
                                             in1=mhalf[:, 0:4], op=ALU.pow), [brt, bC], [brt])
        return brt

    def bcast_tok(src_cols, reads):
        bk, bb = next_bank()
        for q in range(4):
            d = dg[q % 2]
            bd = B("dg", q % 2)
            ts(d[:], ident[:], src_cols[q], None, ALU.mult, None, list(reads) + [bC], [bd])
            mm(bk[:, q * 128:(q + 1) * 128], [(onesF[:], d[:])], [bd, bC], [bb])
        return bk, bb

    def rms_bcast(j, brt, src_rtok):
        return bcast_tok([src_rtok[:, 4 * j + q:4 * j + q + 1] for q in range(4)], [brt])

    def rms_scale(j, bkb, gcol, dst, dst_bufs, dst_cols):
        bk, bb = bkb
        cols = slice(j * NT, (j + 1) * NT)
        for k in range(8):
            stt(dst[:, k, dst_cols], xT[:, k, cols], gcol(k), bk[:], ALU.mult, ALU.mult,
                [B("xT", k, j), bb, bPV], [dst_bufs(k)])

    def rms_apply(j, brt, src_rtok, gcol, dst, dst_bufs, dst_cols, out_f32=False):
        rms_scale(j, rms_bcast(j, brt, src_rtok), gcol, dst, dst_bufs, dst_cols)

    def pvcol(l, i):
        return pv[:, l, i:i + 1]

    load_mixer_w1(0)
    load_mixer_wout(0)

    def param_prep():
        for l in range(L):
            bvst = B("vst", region="A")
            srcs = [(norm1_g_d[l], 0, 8), (norm2_g_d[l], 8, 8), (conv_b_d[l], 16, 3), (conv_ln_g_d[l], 19, 3),
                    (conv_ln_b_d[l], 22, 3), (pool_scale_d[l], 25, 2)]
            for src, r0, nr in srcs:
                dma("sync", vst[r0:r0 + nr, :], src.rearrange("(r c) -> r c", c=128), [], [bvst])
            bk, bb = next_bank()
            transposes([(bk[:, 0:27], vst[0:27, :], ident[0:27, 0:27])], [bvst, bC], [bb])
            vec(lambda e, bk=bk, l=l: e.tensor_copy(out=pv[:, l, 0:27], in_=bk[:, 0:27]), [bb], [bPV])
            bcw = B("cwst", region="A")
            dma("sync", cwst[0:CW, :], conv_w_d[l], [], [bcw])
            bk, bb = next_bank()
            transposes([(bk[:, ct * 32:ct * 32 + CW], cwst[0:CW, ct * 128:(ct + 1) * 128], ident[0:CW, 0:CW])
                        for ct in range(3)], [bcw, bC], [bb])
            vec(lambda e, bk=bk, l=l: e.tensor_copy(out=cwT[:, l, :, :],
                                                     in_=bk[:, 0:96].rearrange("p (c k) -> p c k", k=32)[:, :, 0:CW]),
                [bb], [bPV])
        bvst = B("vst", region="A")
        dma("sync", vst[0:8, :], final_g_d.rearrange("(r c) -> r c", c=128), [], [bvst])
        bk, bb = next_bank()
        transposes([(bk[:, 0:8], vst[0:8, :], ident[0:8, 0:8])], [bvst, bC], [bb])
        vec(lambda e, bk=bk: e.tensor_copy(out=gFv[:], in_=bk[:, 0:8]), [bb], [bPV])

    pend = {}

    def x_tr(s, j):
        for k in range(8):
            bk, bb = next_bank()
            transposes([(bk[:, q * 128:(q + 1) * 128], xstage[(4 * j + q) % 8][:, k * 128:(k + 1) * 128], ident[:])
                        for q in range(4)], [B("xstage", (4 * j + q) % 8, region="H") for q in range(4)] + [bC], [bb])
            if k % 2 == 0:
                act_op(xT[:, k, j * NT:(j + 1) * NT], bk[:], AF.Copy, [bb], [B("xT", k, j)])
            else:
                vec(lambda e, bk=bk, k=k, j=j: e.tensor_copy(out=xT[:, k, j * NT:(j + 1) * NT], in_=bk[:]),
                    [bb], [B("xT", k, j)])
        pend[j] = rms_tok_stats(j, rtok)

    for s in range(nseq):
        if s == 0:
            for j in range(NJ):
                x_tr(0, j)
                if j + 2 < NJ:
                    x_dma(0, j + 2)
            param_prep()

        for l in range(L):
            P.new_phase(["A", "H", "D", "E"])
            bbs = B("BT")
            dma("sync", BT[0:96].rearrange("p h t -> p (h t)"),
                b_s_d[l].rearrange("h t -> (h t)").partition_broadcast(96), [], [bbs])
            bsgB = B("sgB")
            dma("sync", sgB[:, 0, :], sg_ln_g_d[l].partition_broadcast(128), [], [bsgB])
            dma("sync", sgB[:, 1, :], sg_ln_b_d[l].partition_broadcast(128), [], [bsgB])
            bdg = [B("diag", i, region=diag_tiles[i][1]) for i in range(3 * NPE)]
            bws = B("wsst", region="H")
            dma("sync", wsst[:], w_s_d[l].rearrange("h t s -> t h s"), [], [bws])
            bk, bb = next_bank()
            transposes([(bk[:, h * 128:(h + 1) * 128], wsst[:, h, :], ident[:]) for h in range(4)], [bws, bC], [bb])
            bwsT = B("wsT")
            for h in range(4):
                tt(wsT[:, h, :], bk[:, h * 128:(h + 1) * 128], mask[:], ALU.mult, [bb, bC], [bwsT])

            brts = [pend[j] for j in range(NJ)]

            ybufs = [ybuf, ybuf1]
            by = [[B("ybuf", i, ct, region=("A", "H")[i]) for ct in range(3)] for i in range(2)]
            bz = B("zc", region="D")
            pool(lambda e: e.memset(ybuf[:, :, 0:HALO], 0.0), [], by[0])
            pool(lambda e: e.memset(zc[:, :, 0:ZH], 0.0), [], [bz])
            bh = [B("hT", k, region="H") for k in range(8)]
            bug = B("ug", region="D")
            bvg = [B("vg", cq, region="H") for cq in range(4)]
            bacc = [B("acc", ct, region="A") for ct in range(3)]
            baccb = [B("accb", ct, region="A") for ct in range(3)]
            bsqb = [B("sqb", ct, region="A") for ct in range(3)]
            pslots = [(accb[:, i, :], baccb[i]) for i in range(3)] + [(sqb[:, i, :], bsqb[i]) for i in range(3)]
            pslot_rr = [0]
            bsil = B("sil", region="A")
            bya = B("ya", region="H")
            byb = B("yb", region="H")
            byc = B("yc", region="H")
            bvn = [B("vn", cq, region="H") for cq in range(4)]
            bst = B("bnst")
            bsA = B("sA", region="E")
            bsB = B("sB", region="E")
            bpp = B("pp", region="D")
            blt = B("lnt")
            W = ZH + NT

            def proj(c0, m, bi):
                bk, bb = next_bank()
                mm(bk[0:m, :], [(w_in[:, k, c0:c0 + m], hT[:, k, :]) for k in range(8)],
                   bh + [B("w_in", bi)], [bb])
                return bk, bb

            def fa(j):
                rms_apply(j, brts[j], rtok, lambda k, l=l: pvcol(l, PV_G1 + k), hT, lambda k: bh[k], slice(0, NT))

            def fb1(j):
                for h in range(4):
                    ku, bu = proj(2 * DA + 96 * h, 96, 2)
                    act_op(ug[0:96, h, :], ku[0:96, :], AF.Gelu, [bu], [bug])
                for cq in range(4):
                    bk, bb = next_bank()
                    mm(bk[:, 0:DB], [(hT[:, k, cq * 128:(cq + 1) * 128], w_in[:, k, 2 * DA + DB:2 * DA + 2 * DB])
                                     for k in range(8)], bh + [B("w_in", 3)], [bb])
                    act_op(vg[:, cq, :], bk[:, 0:DB], AF.Gelu, [bb], [bvg[cq]])

            def fb2(j):
                yb_ = ybufs[j % 2]
                byj = by[j % 2]
                if j > 0:
                    for ct in range(3):
                        act_op(yb_[:, ct, 0:HALO], ybufs[(j - 1) % 2][:, ct, NT:NT + HALO], AF.Copy,
                               [by[(j - 1) % 2][ct]], [byj[ct]])
                for m in range(2):
                    kz, bzz = proj(2 * DA + 2 * DB + 128 * m, 128, 4)
                    act_op(zc[:, m, ZH:ZH + NT], kz[:], AF.Copy, [bzz], [bz])
                glu = []
                for ct in range(3):
                    ka, ba = proj(ct * 128, 128, 0)
                    kg, bg = proj(DA + ct * 128, 128, 1)
                    act_op(yb_[:, ct, HALO:HALO + NT], kg[:], AF.Sigmoid, [bg], [byj[ct]])
                    glu.append((ct, ka, ba))
                return glu

            def glu_mult(j, ct, ka, ba):
                yb_ = ybufs[j % 2]
                tt(yb_[:, ct, HALO:HALO + NT], yb_[:, ct, HALO:HALO + NT], ka[:], ALU.mult,
                   [by[j % 2][ct], ba], [by[j % 2][ct]])

            def lnv_stats(j):
                for cq in range(4):
                    vec(lambda e, cq=cq: e.bn_stats(out=bnst[0][:, cq * 6:cq * 6 + 6], in_=vg[:, cq, :]), [bvg[cq]], [bst])
                for cq in range(4):
                    vec(lambda e, cq=cq: e.bn_aggr(out=bnst[1][:, 2 * cq:2 * cq + 2], in_=bnst[0][:, cq * 6:cq * 6 + 6]),
                        [bst], [bst])
                mvv = bnst[1][:, 0:8].rearrange("p (q t) -> p q t", t=2)
                ts(bnst[1][:, 8:12], mvv[:, :, 1], LN_EPS, None, ALU.add, None, [bst], [bst])
                pool(lambda e: e.tensor_tensor(out=bnst[1][:, 8:12], in0=bnst[1][:, 8:12], in1=mhalf[:, 0:4], op=ALU.pow),
                     [bst, bC], [bst])

            def lnv_apply(j):
                for cq in range(4):
                    stt(vg[:, cq, :], vg[:, cq, :], bnst[1][:, 2 * cq:2 * cq + 1], sgB[:, 0, :], ALU.subtract, ALU.mult,
                        [bvg[cq], bst, bsgB], [bvg[cq]])
                    stt(vn[:, cq, :], vg[:, cq, :], bnst[1][:, 8 + cq:9 + cq], sgB[:, 1, :], ALU.mult, ALU.add,
                        [bvg[cq], bst, bsgB], [bvn[cq]])

            def pool_branch(j):
                tt(sA[:, :, 2:W], zc[:, :, 2:W], zc[:, :, 1:W - 1], ALU.add, [bz], [bsA])
                tt(sB[:, :, 4:W], sA[:, :, 4:W], sA[:, :, 2:W - 2], ALU.add, [bsA], [bsB])

                def pool_out(src, p0, m, w):
                    stt(pp[p0:p0 + 64, m, :], src[p0:p0 + 64, m, ZH:W], 1.0 / w, zc[p0:p0 + 64, m, ZH:W],
                        ALU.mult, ALU.subtract, [bsA, bsB, bz], [bpp])
                    if j == 0:
                        tt(src[p0:p0 + 64, m, ZH:2 * ZH], src[p0:p0 + 64, m, ZH:2 * ZH], invcnt[p0:p0 + 64, m, :],
                           ALU.mult, [bsA, bsB, bC], [bsA, bsB])
                        tt(pp[p0:p0 + 64, m, 0:ZH], src[p0:p0 + 64, m, ZH:2 * ZH], zc[p0:p0 + 64, m, ZH:2 * ZH],
                           ALU.subtract, [bsA, bsB, bz], [bpp])
                pool_out(sA, 0, 0, 2)
                pool_out(sB, 64, 0, 4)
                tt(sA[:, 1, 8:W], sB[:, 1, 8:W], sB[:, 1, 4:W - 4], ALU.add, [bsB], [bsA])
                pool_out(sA, 0, 1, 8)
                tt(sB[64:128, 1, 16:W], sA[64:128, 1, 16:W], sA[64:128, 1, 8:W - 8], ALU.add, [bsA], [bsB])
                pool_out(sB, 64, 1, 16)
                if j + 1 < NJ:
                    act_op(zc[:, :, 0:ZH], zc[:, :, NT:NT + ZH], AF.Copy, [bz], [bz])

            convps = {}

            def tap30(j):
                for ct in range(3):
                    act_op(acc[:, ct, :], ybufs[j % 2][:, ct, HALO:HALO + NT], AF.Identity, [by[j % 2][ct], bPV],
                           [bacc[ct]], scale=cwT[:, l, ct, CW - 1:CW], bias=pvcol(l, PV_CB + ct))

            def tap_ops(j):
                a_ = [(ct, k) for k in range(NPE, NPE + NACT) for ct in range(3)]
                d_ = [(ct, k) for k in range(NPE + NACT, CW - 1) for ct in range(3)]
                out_ = []
                for i in range(max(len(a_), len(d_))):
                    if i < len(d_):
                        out_.append(d_[i])
                    if i < len(a_):
                        out_.append(a_[i])
                return out_

            def tap(j, ct, k):
                if k >= NPE + NACT:
                    stt(acc[:, ct, :], ybufs[j % 2][:, ct, k:k + NT], cwT[:, l, ct, k:k + 1], acc[:, ct, :],
                        ALU.mult, ALU.add, [by[j % 2][ct], bPV, bacc[ct]], [bacc[ct]])
                    return
                slot, bsl = pslots[pslot_rr[0] % len(pslots)]
                pslot_rr[0] += 1
                act_op(slot, ybufs[j % 2][:, ct, k:k + NT], AF.Identity, [by[j % 2][ct], bPV], [bsl],
                       scale=cwT[:, l, ct, k:k + 1])
                bk, bb = convps[j][ct]
                mm(bk[:], [(identb[:], slot)], [bsl, bC], [bb], first=False, last=(k == NPE + NACT - 1))

            def conv_pe(j):
                convps[j] = []
                for ct in range(3):
                    bk, bb = next_bank()
                    mm(bk[:], [(diag_tiles[ct * NPE + k][0][:], ybufs[j % 2][:, ct, k:k + NT]) for k in range(NPE)],
                       [by[j % 2][ct]] + bdg[ct * NPE:(ct + 1) * NPE], [bb], first=True, last=(NACT == 0))
                    convps[j].append((bk, bb))

            def acc_add(j):
                for ct in range(3):
                    tt(acc[:, ct, :], acc[:, ct, :], convps[j][ct][0][:], ALU.add, [bacc[ct], convps[j][ct][1]],
                       [bacc[ct]])

            fa(0)
            for ct in range(3):
                for k in range(NPE):
                    ts(diag_tiles[ct * NPE + k][0][:], ident[:], cwT[:, l, ct, k:k + 1], None, ALU.mult, None,
                       [bC, bPV], [bdg[ct * NPE + k]])
            fb1(0)
            lnv_stats(0)
            glu0 = fb2(0)
            lnv_apply(0)
            for g_ in glu0:
                glu_mult(0, *g_)
            pool_branch(0)
            tap30(0)
            conv_pe(0)
            for ct, k in tap_ops(0):
                tap(0, ct, k)

            brts2 = {}
            for j in range(NJ):
                cols = slice(j * NT, (j + 1) * NT)
                nj = j + 1 if j + 1 < NJ else None
                acc_add(j)
                for ct in range(3):
                    act_op(accb[:, ct, :], acc[:, ct, :], AF.Copy, [bacc[ct]], [baccb[ct]])
                    act_op(sqb[:, ct, :], acc[:, ct, :], AF.Square, [bacc[ct]], [bsqb[ct]])
                if nj is not None and j == 0:
                    fa(nj)
                pmb = []
                for cq in range(4):
                    bk2, bb2 = next_bank()
                    mms([(bk2[0:96, h * 128:(h + 1) * 128], [(vn[:, cq, 96 * h:96 * h + 96], wsT[:, h, :])])
                         for h in range(4)], [bvn[cq], bwsT], [bb2])
                    pmb.append((bk2, bb2))
                for m in range(2):
                    bk2, bb2 = next_bank()
                    mm(bk2[:], [(wpl[:, m, :], pp[:, m, :])], [bpp, B("wpl")], [bb2])
                    act_op(yc[:, m, :], bk2[:], AF.Identity, [bb2, bPV], [byc], scale=pvcol(l, PV_PS + m))
                bk, bb = next_bank()
                groups = []
                for q in range(4):
                    groups.append((bk[:, 2 * q:2 * q + 1],
                                   [(accb[:, ct, q * 128:(q + 1) * 128], ocS[:, 1:2]) for ct in range(3)]))
                    groups.append((bk[:, 2 * q + 1:2 * q + 2],
                                   [(sqb[:, ct, q * 128:(q + 1) * 128], ocS[:, 1:2]) for ct in range(3)]))
                mms(groups, baccb + bsqb + [bC], [bb])
                ts(rtok2[:, 0:8], bk[:, 0:8], 1.0 / DA, None, ALU.mult, None, [bb], [blt])
                me = rtok2[:, 0:8].rearrange("p (q t) -> p q t", t=2)
                tt(rtok2[:, 8:12], me[:, :, 0], me[:, :, 0], ALU.mult, [blt], [blt])
                tt(rtok2[:, 8:12], me[:, :, 1], rtok2[:, 8:12], ALU.subtract, [blt], [blt])
                ts(rtok2[:, 8:12], rtok2[:, 8:12], LN_EPS, None, ALU.add, None, [blt], [blt])
                pool(lambda e: e.tensor_tensor(out=rtok2[:, 8:12], in0=rtok2[:, 8:12], in1=mhalf[:, 0:4], op=ALU.pow),
                     [blt, bC], [blt])
                for cq in range(4):
                    bk2, bb2 = pmb[cq]
                    t = tlnv[cq % 2]
                    bt = (bsA, bsB)[cq % 2]
                    tt(t[0:96].rearrange("p (h t) -> p h t", t=128), bk2[0:96, :].rearrange("p (h t) -> p h t", t=128),
                       BT[0:96], ALU.add, [bb2, bbs], [bt])
                    tt(yb[0:96, :, cq * 128:(cq + 1) * 128], ug[0:96, :, cq * 128:(cq + 1) * 128],
                       t[0:96].rearrange("p (h t) -> p h t", t=128), ALU.mult, [bug, bt], [byb])
                if nj is not None:
                    fb1(nj)
                stt(rtok2[:, 12:16], me[:, :, 0], -1.0, rtok2[:, 8:12], ALU.mult, ALU.mult, [blt], [blt])
                krs, brs = bcast_tok([rtok2[:, 8 + q:9 + q] for q in range(4)], [blt])
                knm, bnm = bcast_tok([rtok2[:, 12 + q:13 + q] for q in range(4)], [blt])
                for ct in range(3):
                    t = tlnv[ct % 2]
                    bt = (bsA, bsB)[ct % 2]
                    tt(t, acc[:, ct, :], krs[:], ALU.mult, [bacc[ct], brs], [bt])
                    tt(t, t, knm[:], ALU.add, [bt, bnm], [bt])
                    act_op(sil[:, ct, :], t, AF.Silu, [bt, bPV], [bsil],
                           scale=pvcol(l, PV_LNG + ct), bias=pvcol(l, PV_LNB + ct))
                glu = []
                if nj is not None:
                    lnv_stats(nj)
                    glu = fb2(nj)
                    lnv_apply(nj)
                    for g_ in glu:
                        glu_mult(nj, *g_)
                for co in range(3):
                    bk2, bb2 = next_bank()
                    mm(bk2[:], [(w_pw[:, ci, co * 128:(co + 1) * 128], sil[:, ci, :]) for ci in range(3)],
                       [bsil, B("w_pw")], [bb2])
                    act_op(ya[:, co, :], bk2[:], AF.Copy, [bb2], [bya])
                if nj is not None:
                    pool_branch(nj)
                    tap30(nj)
                    conv_pe(nj)
                ptaps = tap_ops(nj) if nj is not None else []
                per = (len(ptaps) + 7) // 8
                for n in range(8):
                    ncol = slice(n * 128, (n + 1) * 128)
                    bk2, bb2 = next_bank()
                    pairs = [(wo_b[0:96, h, ncol], yb[0:96, h, :]) for h in range(4)]
                    pairs += [(wo_c[:, m, ncol], yc[:, m, :]) for m in range(2)]
                    pairs += [(wo_a[:, i, ncol], ya[:, i, :]) for i in range(3)]
                    mm(bk2[:], pairs, [bya, byb, byc, B("wo_a", region="R"), B("wo_b", region="R"),
                                       B("wo_c", region="R")], [bb2])
                    for ct, k in ptaps[n * per:(n + 1) * per]:
                        tap(nj, ct, k)
                    if n == 3 and j + 2 < NJ:
                        fa(j + 2)
                    tt(xT[:, n, cols], xT[:, n, cols], bk2[:], ALU.add, [B("xT", n, j), bb2], [B("xT", n, j)])
                brts2[j] = rms_tok_stats(j, rtokB)

            P.new_phase(["A", "H", "D", "E", "R"])
            nfp = NF // 2
            for fp in range(min(NRING, nfp)):
                load_ring(l, fp, fp % NRING)
            load_wd(l, 0)
            bh2 = [[B("h2T", k, j, region="H") for k in range(8)] for j in range(NJ)]
            n2b = [rms_bcast(j, brts2[j], rtokB) for j in range(NJ)]

            def n2scale(j):
                rms_scale(j, n2b[j], lambda k, l=l: pvcol(l, PV_G2 + k), h2T, lambda k, j=j: bh2[j][k],
                          slice(j * NT, (j + 1) * NT))
            n2scale(0)
            nxt = None
            if l + 1 < L:
                nxt = l + 1
            elif s + 1 < nseq:
                nxt = 0
            if nxt is not None:
                load_mixer_w1(nxt)
            for ci, (f0, nf) in enumerate(CHUNKS):
                bact = [[B("act", fi, j, region="A") for j in range(NJ)] for fi in range(nf)]
                for fi in range(nf):
                    f = f0 + fi
                    fp, half = f // 2, f % 2
                    slot = fp % NRING
                    fc = slice(half * 128, half * 128 + 128)
                    for j in range(NJ):
                        cols = slice(j * NT, (j + 1) * NT)
                        kg, bg = next_bank()
                        mm(kg[:], [(ring[slot][:, 0, k, fc], h2T[:, k, cols]) for k in range(8)],
                           bh2[j] + [B("ring", slot, 0, region="R")], [bg])
                        ku, bu = next_bank()
                        mm(ku[:], [(ring[slot][:, 1, k, fc], h2T[:, k, cols]) for k in range(8)],
                           bh2[j] + [B("ring", slot, 1, region="R")], [bu])
                        t = tlnv[(fi * NJ + j) % 2]
                        bt = B(("sA", "sB")[(fi * NJ + j) % 2], region="E")
                        act_op(t, kg[:], AF.Silu, [bg], [bt])
                        if ci == 0 and fi == 0 and j + 1 < NJ:
                            n2scale(j + 1)
                        tt(act[:, fi, cols], t, ku[:], ALU.mult, [bt, bu], [bact[fi][j]])
                    if half == 1 and fp + NRING < nfp:
                        load_ring(l, fp + NRING, slot)
                last = ci + 1 == len(CHUNKS)
                order = [(n, j) for j in range(NJ) for n in range(8)] if last else \
                        [(n, j) for n in range(8) for j in range(NJ)]
                for n, j in order:
                    ncol = slice(n * 128, (n + 1) * 128)
                    cols = slice(j * NT, (j + 1) * NT)
                    bk, bb = next_bank()
                    mm(bk[:], [(wd[:, fi, ncol], act[:, fi, cols]) for fi in range(nf)],
                       [bact[fi][j] for fi in range(nf)] + [B("wd", region="D")], [bb])
                    tt(xT[:, n, cols], xT[:, n, cols], bk[:], ALU.add, [B("xT", n, j), bb], [B("xT", n, j)])
                    if last and n == 7 and j > 0:
                        pend[j - 1] = rms_tok_stats(j - 1, rtok)
                if last:
                    pend[NJ - 1] = rms_tok_stats(NJ - 1, rtok)
                if ci + 1 < len(CHUNKS):
                    load_wd(l, ci + 1)
            if nxt is not None:
                P.new_phase(["R"])
                load_mixer_wout(nxt)

        P.new_phase(["A", "H", "D", "E"])
        if s + 1 < nseq:
            x_dma(s + 1, 0)
            x_dma(s + 1, 1)
        bgfb = B("gFB", region="A")
        dma("sync", gFB[:], final_g_d.partition_broadcast(128), [], [bgfb])

        def fin_store(j):
            for q in range(4):
                tb = j * 4 + q
                os_ = ostage[tb % 4]
                bos = B("ostage", tb % 4, region="A")
                for hf in range(2):
                    bk, bb = next_bank()
                    transposes([(bk[:, kk * 128:(kk + 1) * 128],
                                 xT[:, hf * 4 + kk, j * NT + q * 128:j * NT + (q + 1) * 128], ident[:])
                                for kk in range(4)], [B("xT", hf * 4 + kk, j) for kk in range(4)] + [bC], [bb])
                    stt(os_[:, hf * 512:(hf + 1) * 512], bk[:], rtok[:, 4 * j + q:4 * j + q + 1],
                        gFB[:, hf * 512:(hf + 1) * 512], ALU.mult, ALU.mult, [bb, pend[j], bgfb], [bos])
                dma("sync", out_d[s, tb * 128:(tb + 1) * 128, :], os_[:], [bos], [], final=True)

        for j in range(NJ):
            fin_store(j)
            if s + 1 < nseq:
                x_tr(s + 1, j)
                if j + 2 < NJ:
                    x_dma(s + 1, j + 2)

    P.emit(nc)
    return nc


_NC_CACHE = {}


def _get_nc(nseq, depth):
    key = (nseq, depth)
    if key not in _NC_CACHE:
        _NC_CACHE[key] = build_program(nseq=nseq, depth=depth)
    return _NC_CACHE[key]


_WNAMES = ["norm1_g", "w_in", "conv_w", "conv_b", "conv_ln_g", "conv_ln_b", "w_pw", "sg_ln_g", "sg_ln_b",
           "w_s", "b_s", "w_pool", "pool_scale", "w_out", "norm2_g", "w_gate_up", "w_down", "final_g"]


def kernel(**inputs):
    x = np.ascontiguousarray(np.asarray(inputs["x"], dtype=np.float32))
    bsz = x.shape[0]
    nseq = bsz // N_CORES
    depth = int(np.asarray(inputs["w_in"]).shape[0])
    nc = _get_nc(nseq, depth)
    ws = {n: np.ascontiguousarray(np.asarray(inputs[n], dtype=np.float32)) for n in _WNAMES}
    in_maps = []
    for c in range(N_CORES):
        m = dict(ws)
        m["x"] = x[c * nseq:(c + 1) * nseq]
        in_maps.append(m)
    res = run_bass_kernel_spmd(nc, in_maps, core_ids=list(range(N_CORES)))
    return np.concatenate([np.asarray(r["out"]) for r in res.results], axis=0).astype(np.float32)
```

```python
import numpy as np
import concourse.bass as bass
import concourse.mybir as mybir
from concourse.bass_utils import run_bass_kernel_spmd

F32 = mybir.dt.float32
BF16 = mybir.dt.bfloat16
AF = mybir.ActivationFunctionType
ALU = mybir.AluOpType

D = 1024
S = 2048
NT = 512
NJ = S // NT
DA = 384
DB = 384
DC = 256
DIN = 1792
DFF = 2816
NF = DFF // 128
CW = 31
HALO = CW - 1
ZH = 16
RMS_EPS = 1e-6
LN_EPS = 1e-5
N_CORES = 8
CHUNKS = [(0, 6), (6, 6), (12, 6), (18, 4)]
NRING = 3
NACT = 6
NPE = 18


class Buf:
    __slots__ = ("name", "w", "r", "region")

    def __init__(self, name, region=None):
        self.name = name
        self.w = None
        self.r = []
        self.region = region


class Eng:
    def __init__(self, name):
        self.name = name
        self.ops = []
        self.gen = 0
        self.cnt = 0
        self.waited = {}
        self.dma_rr = 0
        self.dma_vals = {}

    @property
    def semkey(self):
        return ("e", self.name, self.gen)


SEM_LIMIT = 30000
N_DMA_SEMS = {"gpsimd": 20, "sync": 12, "scalar": 4}


class Prog:
    def __init__(self):
        self.eng = {n: Eng(n) for n in ("tensor", "vector", "scalar", "gpsimd", "sync")}
        self.bufs = {}
        self.fence = {}
        self.region_last = {}
        self.region_dma = {}
        self.final_tokens = []

    def buf(self, *key, region=None):
        b = self.bufs.get(key)
        if b is None:
            b = Buf(key, region)
            self.bufs[key] = b
        return b

    def new_phase(self, regions):
        for r in regions:
            toks = list(self.region_last.get(r, {}).values()) + list(self.region_dma.get(r, []))
            self.fence[r] = toks
            self.region_dma[r] = []

    def add(self, engname, fn, reads=(), writes=(), dma=False, final=False):
        e = self.eng[engname]
        deps = {}

        def need(tok):
            if tok is None:
                return
            k, v = tok
            if deps.get(k, 0) < v:
                deps[k] = v

        for b in reads:
            need(b.w)
            if b.region is not None:
                for t in self.fence.get(b.region, ()):
                    need(t)
        for b in writes:
            need(b.w)
            for t in b.r:
                need(t)
            if b.region is not None:
                for t in self.fence.get(b.region, ()):
                    need(t)
        if dma:
            n = N_DMA_SEMS[engname]
            idx = e.dma_rr % n
            e.dma_rr += 1
            k = ("d", engname, idx)
            prev = e.dma_vals.get(k, 0)
            if prev:
                need((k, prev))
            tok = (k, prev + 16)
            e.dma_vals[k] = prev + 16
        else:
            if e.cnt >= SEM_LIMIT:
                e.gen += 1
                e.cnt = 0
            e.cnt += 1
            tok = (e.semkey, e.cnt)
        waits = []
        for k, v in deps.items():
            if engname == "tensor" and k == e.semkey:
                continue
            if e.waited.get(k, 0) >= v:
                continue
            e.waited[k] = v
            waits.append((k, v))
        e.ops.append((fn, waits, tok, dma))
        ws = set(id(b) for b in writes)
        for b in writes:
            b.w = tok
            b.r = []
        for b in reads:
            if id(b) not in ws:
                b.r.append(tok)
        for b in list(reads) + list(writes):
            if b.region is not None:
                if dma:
                    self.region_dma.setdefault(b.region, []).append(tok)
                else:
                    self.region_last.setdefault(b.region, {})[e.semkey[:2]] = tok
        if final:
            self.final_tokens.append(tok)
        return tok

    def emit(self, nc):
        keys = set()
        for e in self.eng.values():
            for fn, waits, tok, dma in e.ops:
                keys.add(tok[0])
                for k, v in waits:
                    keys.add(k)
        sems = {}
        for i, k in enumerate(sorted(keys, key=str)):
            sems[k] = nc.alloc_semaphore("s%d" % i)
        finals = self.final_tokens
        with nc.Block() as block:
            def mk(e):
                def body(eh):
                    for fn, waits, tok, dma in e.ops:
                        for k, v in waits:
                            eh.wait_ge(sems[k], v)
                        ins = fn(eh)
                        ins.then_inc(sems[tok[0]], 16 if dma else 1)
                    if e.name == "sync":
                        for k, v in finals:
                            eh.wait_ge(sems[k], v)
                return body
            for name, e in self.eng.items():
                if not e.ops and name != "sync":
                    continue
                getattr(block, name)(mk(e))


def build_program(nseq=2, depth=2):
    nc = bass.Bass("TRN2", target_bir_lowering=False)
    P = Prog()
    L = depth

    def din(name, shape):
        return nc.dram_tensor(name, list(shape), F32, kind="ExternalInput").ap()

    x_d = din("x", [nseq, S, D])
    norm1_g_d = din("norm1_g", [L, D])
    w_in_d = din("w_in", [L, D, DIN])
    conv_w_d = din("conv_w", [L, CW, DA])
    conv_b_d = din("conv_b", [L, DA])
    conv_ln_g_d = din("conv_ln_g", [L, DA])
    conv_ln_b_d = din("conv_ln_b", [L, DA])
    w_pw_d = din("w_pw", [L, DA, DA])
    sg_ln_g_d = din("sg_ln_g", [L, DB])
    sg_ln_b_d = din("sg_ln_b", [L, DB])
    w_s_d = din("w_s", [L, 4, 128, 128])
    b_s_d = din("b_s", [L, 4, 128])
    w_pool_d = din("w_pool", [L, 4, 64, 64])
    pool_scale_d = din("pool_scale", [L, DC])
    w_out_d = din("w_out", [L, D, D])
    norm2_g_d = din("norm2_g", [L, D])
    w_gu_d = din("w_gate_up", [L, D, 2 * DFF])
    w_down_d = din("w_down", [L, DFF, D])
    final_g_d = din("final_g", [D])
    out_d = nc.dram_tensor("out", [nseq, S, D], F32, kind="ExternalOutput").ap()

    base = (nc.sbuf_base + 31) // 32 * 32
    top = nc.sbuf_top
    cur = [base]

    def alloc(name, shape, dt, at=None):
        nbytes = int(np.prod(shape[1:])) * (4 if dt == F32 else 2)
        nbytes = (nbytes + 31) // 32 * 32
        if at is None:
            off = cur[0]
            cur[0] += nbytes
        else:
            off = at
        assert off + nbytes <= top, (name, off, nbytes, top)
        return nc.alloc_sbuf_tensor_at(name, list(shape), dt, offset=off), off, nbytes

    class Region:
        def __init__(self, name, size):
            self.name = name
            self.size = size
            self.base = cur[0]
            cur[0] += size
            assert cur[0] <= top, ("region overflow", name, cur[0], top)

        def carve(self):
            return Carver(self)

    class Carver:
        def __init__(self, reg):
            self.reg = reg
            self.off = reg.base

        def alloc(self, name, shape, dt):
            t, off, nb = alloc(name, shape, dt, at=self.off)
            self.off += nb
            assert self.off <= self.reg.base + self.reg.size, ("carve overflow", self.reg.name, name)
            return t

    ident = alloc("ident", [128, 128], F32)[0]
    onesF = alloc("onesF", [128, 128], F32)[0]
    mask = alloc("mask", [128, 128], F32)[0]
    identb = alloc("identb", [128, 128], BF16)[0]
    mhalf = alloc("mhalf", [128, 16], F32)[0]
    invcnt = alloc("invcnt", [128, 2, 16], F32)[0]
    ocS = alloc("ocS", [128, 2], BF16)[0]
    pv = alloc("pv", [128, L, 32], F32)[0]
    gFv = alloc("gFv", [128, 8], F32)[0]
    cwT = alloc("cwT", [128, L, 3, CW], F32)[0]
    BT = alloc("BT", [128, 4, 128], F32)[0]
    sgB = alloc("sgB", [128, 2, DB], F32)[0]
    sqr = [alloc("sqr%d" % i, [128, NT], BF16)[0] for i in range(2)]
    rtok = alloc("rtok", [128, 16], F32)[0]
    rtok2 = alloc("rtok2", [128, 16], F32)[0]
    rtokB = alloc("rtokB", [128, 16], F32)[0]
    dg = [alloc("dg%d" % i, [128, 128], F32)[0] for i in range(2)]
    bnst = [alloc("bnst%d" % i, [128, 32], F32)[0] for i in range(2)]
    xT = alloc("xT", [128, 8, S], F32)[0]
    w_in = alloc("w_in_sb", [128, 8, DIN], BF16)[0]
    w_pw = alloc("w_pw_sb", [128, 3, DA], BF16)[0]
    wsT = alloc("wsT", [128, 4, 128], BF16)[0]
    wpl = alloc("wpl", [128, 2, 128], BF16)[0]
    RR = Region("R", 24576)
    RH = Region("H", 32768)
    RA = Region("A", 24576)
    RD = Region("D", 12288)
    RE = Region("E", (top - cur[0]) // 32 * 32)

    c = RR.carve()
    wo_a = c.alloc("wo_a", [128, 3, D], BF16)
    wo_b = c.alloc("wo_b", [128, 4, D], BF16)
    wo_c = c.alloc("wo_c", [128, 2, D], BF16)
    c = RR.carve()
    ring = [c.alloc("ring%d" % i, [128, 2, 8, 256], BF16) for i in range(NRING)]
    c = RH.carve()
    h2T = c.alloc("h2T", [128, 8, S], BF16)
    c = RH.carve()
    hT = c.alloc("hT", [128, 8, NT], BF16)
    ya = c.alloc("ya", [128, 3, NT], BF16)
    yb = c.alloc("yb", [128, 4, NT], BF16)
    yc = c.alloc("yc", [128, 2, NT], BF16)
    vg = c.alloc("vg", [128, 4, DB], F32)
    vn = c.alloc("vn", [128, 4, DB], BF16)
    wsst = c.alloc("wsst", [128, 4, 128], F32)
    ybuf1 = c.alloc("ybuf1", [128, 3, HALO + NT], BF16)
    c = RA.carve()
    act = c.alloc("act", [128, 6, S], BF16)
    c = RH.carve()
    xstage = [c.alloc("xstage%d" % i, [128, D], F32) for i in range(8)]
    c = RA.carve()
    vst = c.alloc("vst", [32, 128], F32)
    cwst = c.alloc("cwst", [32, DA], F32)
    c = RA.carve()
    ostage = [c.alloc("ostage%d" % i, [128, D], F32) for i in range(4)]
    gFB = c.alloc("gFB", [128, D], F32)
    c = RA.carve()
    ybuf = c.alloc("ybuf", [128, 3, HALO + NT], BF16)
    acc = c.alloc("acc", [128, 3, NT], F32)
    accb = c.alloc("accb", [128, 3, NT], BF16)
    sqb = c.alloc("sqb", [128, 3, NT], BF16)
    sil = c.alloc("sil", [128, 3, NT], BF16)
    diag_tiles = []
    cA = c
    c = RD.carve()
    wd = c.alloc("wd", [128, 6, D], BF16)
    c = RD.carve()
    ug = c.alloc("ug", [128, 4, NT], BF16)
    zc = c.alloc("zc", [128, 2, ZH + NT], F32)
    pp = c.alloc("pp", [128, 2, NT], BF16)
    cD = c
    c = RE.carve()
    sA = c.alloc("sA", [128, 2, ZH + NT], F32)
    sB = c.alloc("sB", [128, 2, ZH + NT], F32)
    tlnv = [sA[:, 0, 0:NT], sB[:, 0, 0:NT]]
    cE = c
    cR = RR.carve()
    cR.off = RR.base + 18432
    for cc, rn in ((cR, "R"), (cA, "A"), (cD, "D"), (cE, "E")):
        while cc.off + 256 <= cc.reg.base + cc.reg.size and len(diag_tiles) < 3 * NPE:
            diag_tiles.append((cc.alloc("diag%d" % len(diag_tiles), [128, 128], BF16), rn))
    assert len(diag_tiles) == 3 * NPE, len(diag_tiles)

    banks = [nc.alloc_psum_tensor("bank%d" % i, [128, 512], F32) for i in range(8)]
    bank_rr = [0]

    def next_bank():
        for _ in range(8):
            i = bank_rr[0] % 8
            bank_rr[0] += 1
            b_ = P.buf("bank", i)
            if b_.w is None or len(b_.r) > 0:
                return banks[i], b_
        raise AssertionError("all PSUM banks are held by writes whose readers were not emitted yet")

    B = P.buf

    def dma(q, out, in_, reads=(), writes=(), final=False, nonc=False):
        def fn(e, out=out, in_=in_):
            if nonc:
                return e.dma_start(out=out, in_=in_, allow_slow_non_contiguous=True)
            return e.dma_start(out=out, in_=in_)
        return P.add(q, fn, reads=reads, writes=writes, dma=True, final=final)

    def mm(out, pairs, reads, writes, first=True, last=True):
        def fn(e, out=out, pairs=pairs, first=first, last=last):
            n = len(pairs)
            ins = None
            for i, (l, r) in enumerate(pairs):
                ins = e.matmul(out, l, r, start=(first and i == 0), stop=(last and i == n - 1))
            return ins
        return P.add("tensor", fn, reads=reads, writes=writes)

    def mms(groups, reads, writes):
        def fn(e, groups=groups):
            ins = None
            for out, pairs in groups:
                n = len(pairs)
                for i, (l, r) in enumerate(pairs):
                    ins = e.matmul(out, l, r, start=(i == 0), stop=(i == n - 1))
            return ins
        return P.add("tensor", fn, reads=reads, writes=writes)

    def transposes(items, reads, writes):
        def fn(e, items=items):
            ins = None
            for out, in_, idn in items:
                ins = e.transpose(out, in_, idn)
            return ins
        return P.add("tensor", fn, reads=reads, writes=writes)

    def act_op(out, in_, func, reads, writes, scale=None, bias=None):
        def fn(e, out=out, in_=in_, func=func, scale=scale, bias=bias):
            kw = {}
            if scale is not None:
                kw["scale"] = scale
            if bias is not None:
                kw["bias"] = bias
            return e.activation(out=out, in_=in_, func=func, **kw)
        return P.add("scalar", fn, reads=reads, writes=writes)

    def vec(fn, reads, writes):
        return P.add("vector", fn, reads=reads, writes=writes)

    def pool(fn, reads, writes):
        return P.add("gpsimd", fn, reads=reads, writes=writes)

    def tt(out, in0, in1, op, reads, writes, eng="vector"):
        return P.add(eng, lambda e, out=out, in0=in0, in1=in1, op=op: e.tensor_tensor(out=out, in0=in0, in1=in1, op=op),
                     reads=reads, writes=writes)

    def ts(out, in0, s1, s2, op0, op1, reads, writes, eng="vector"):
        def fn(e, out=out, in0=in0, s1=s1, s2=s2, op0=op0, op1=op1):
            if op1 is None:
                return e.tensor_scalar(out=out, in0=in0, scalar1=s1, scalar2=None, op0=op0)
            return e.tensor_scalar(out=out, in0=in0, scalar1=s1, scalar2=s2, op0=op0, op1=op1)
        return P.add(eng, fn, reads=reads, writes=writes)

    def stt(out, in0, scalar, in1, op0, op1, reads, writes):
        return vec(lambda e, out=out, in0=in0, scalar=scalar, in1=in1, op0=op0, op1=op1:
                   e.scalar_tensor_tensor(out=out, in0=in0, scalar=scalar, in1=in1, op0=op0, op1=op1),
                   reads=reads, writes=writes)

    def x_dma(s, j):
        for q in range(4):
            tb = 4 * j + q
            dma("sync", xstage[tb % 8][:], x_d[s, tb * 128:(tb + 1) * 128, :], [], [B("xstage", tb % 8, region="H")])

    x_dma(0, 0)
    x_dma(0, 1)

    bC = B("consts")
    pool(lambda e: e.memset(onesF[:], 1.0), [], [bC])
    pool(lambda e: e.affine_select(out=ident[:], in_=onesF[:], pattern=[[-1, 128]], compare_op=ALU.is_equal,
                                   fill=0.0, base=0, channel_multiplier=1), [bC], [bC])
    pool(lambda e: e.affine_select(out=mask[:], in_=onesF[:], pattern=[[1, 128]], compare_op=ALU.is_ge,
                                   fill=0.0, base=0, channel_multiplier=-1), [bC], [bC])
    pool(lambda e: e.memset(mhalf[:], -0.5), [], [bC])
    pool(lambda e: e.tensor_copy(out=identb[:], in_=ident[:]), [bC], [bC])
    pool(lambda e: e.memset(ocS[:, 0:1], 1.0 / 1024.0), [], [bC])
    pool(lambda e: e.memset(ocS[:, 1:2], 1.0), [], [bC])
    for m in range(2):
        pool(lambda e, m=m: e.iota(invcnt[:, m, :], [[1, ZH]], base=1, channel_multiplier=0,
                                    allow_small_or_imprecise_dtypes=True), [], [bC])
    for m, p0, w in ((0, 0, 2.0), (0, 64, 4.0), (1, 0, 8.0), (1, 64, 16.0)):
        ts(invcnt[p0:p0 + 64, m, :], invcnt[p0:p0 + 64, m, :], w, None, ALU.min, None, [bC], [bC])
    vec(lambda e: e.reciprocal(out=invcnt[:], in_=invcnt[:]), [bC], [bC])

    PV_G1, PV_G2, PV_CB, PV_LNG, PV_LNB, PV_PS = 0, 8, 16, 19, 22, 25
    bPV = B("pv")

    WIN_BLOCKS = [(0, 384), (384, 768), (768, 1152), (1152, 1536), (1536, 1792)]

    def load_mixer_w1(l):
        wv = w_in_d[l].rearrange("(k p) c -> p k c", p=128)
        for bi, (c0, c1) in enumerate(WIN_BLOCKS):
            dma("gpsimd", w_in[:, :, c0:c1], wv[:, :, c0:c1], [], [B("w_in", bi)])
        dma("gpsimd", w_pw[:], w_pw_d[l].rearrange("(k p) c -> p k c", p=128), [], [B("w_pw")])
        pool(lambda e: e.memset(wpl[:], 0.0), [], [B("wpl")])
        for g in range(4):
            m, h = g // 2, g % 2
            dma("gpsimd", wpl[64 * h:64 * h + 64, m, 64 * h:64 * h + 64], w_pool_d[l, g], [], [B("wpl")])

    def load_mixer_wout(l):
        dma("gpsimd", wo_a[:], w_out_d[l, 0:DA, :].rearrange("(k p) n -> p k n", p=128), [], [B("wo_a", region="R")])
        dma("gpsimd", wo_b[0:96], w_out_d[l, DA:DA + DB, :].rearrange("(k p) n -> p k n", p=96), [],
            [B("wo_b", region="R")])
        dma("gpsimd", wo_c[:], w_out_d[l, DA + DB:D, :].rearrange("(k p) n -> p k n", p=128), [],
            [B("wo_c", region="R")])

    def load_ring(l, fp, slot):
        wv = w_gu_d[l].rearrange("(k p) c -> p k c", p=128)
        dma("gpsimd", ring[slot][:, 0], wv[:, :, 256 * fp:256 * fp + 256], [], [B("ring", slot, 0, region="R")])
        dma("gpsimd", ring[slot][:, 1], wv[:, :, DFF + 256 * fp:DFF + 256 * fp + 256], [],
            [B("ring", slot, 1, region="R")])

    def load_wd(l, ci):
        f0, nf = CHUNKS[ci]
        dma("gpsimd", wd[:, 0:nf, :], w_down_d[l, f0 * 128:(f0 + nf) * 128, :].rearrange("(f p) n -> p f n", p=128),
            [], [B("wd", region="D")])

    def rms_tok_stats(j, dst_rtok):
        cols = slice(j * NT, (j + 1) * NT)
        bk, bb = next_bank()
        groups = [[] for _ in range(4)]
        rd = []
        for k in range(8):
            sq = sqr[k % 2]
            bsq = B("sqr", k % 2)
            act_op(sq[:], xT[:, k, cols], AF.Square, [B("xT", k, j)], [bsq])
            mmg = []
            for q in range(4):
                mmg.append((bk[:, q:q + 1], sq[:, q * 128:(q + 1) * 128], ocS[:, 0:1], k))
            def fn(e, mmg=mmg):
                ins = None
                for out, l, r, k in mmg:
                    ins = e.matmul(out, l, r, start=(k == 0 and out is mmg[0][0]), stop=(k == 7), skip_group_check=True)
                return ins
            P.add("tensor", fn, reads=[bsq, bC], writes=[bb])
        brt = B("rtok", id(dst_rtok), j)
        ts(dst_rtok[:, 4 * j:4 * j + 4], bk[:, 0:4], RMS_EPS, None, ALU.add, None, [bb], [brt])
        pool(lambda e, j=j: e.tensor_tensor(out=dst_rtok[:, 4 * j:4 * j + 4], in0=dst_rtok[:, 4 * j:4 * j + 4],
                                             in1=mhalf[:, 0:4], op=ALU.pow), [brt, bC], [brt])
        return brt

    def bcast_tok(src_cols, reads):
        bk, bb = next_bank()
        for q in range(4):
            d = dg[q % 2]
            bd = B("dg", q % 2)
            ts(d[:], ident[:], src_cols[q], None, ALU.mult, None, list(reads) + [bC], [bd])
            mm(bk[:, q * 128:(q + 1) * 128], [(onesF[:], d[:])], [bd, bC], [bb])
        return bk, bb

    def rms_bcast(j, brt, src_rtok):
        return bcast_tok([src_rtok[:, 4 * j + q:4 * j + q + 1] for q in range(4)], [brt])

    def rms_scale(j, bkb, gcol, dst, dst_bufs, dst_cols):
        bk, bb = bkb
        cols = slice(j * NT, (j + 1) * NT)
        for k in range(8):
            stt(dst[:, k, dst_cols], xT[:, k, cols], gcol(k), bk[:], ALU.mult, ALU.mult,
                [B("xT", k, j), bb, bPV], [dst_bufs(k)])

    def rms_apply(j, brt, src_rtok, gcol, dst, dst_bufs, dst_cols, out_f32=False):
        rms_scale(j, rms_bcast(j, brt, src_rtok), gcol, dst, dst_bufs, dst_cols)

    def pvcol(l, i):
        return pv[:, l, i:i + 1]

    load_mixer_w1(0)
    load_mixer_wout(0)

    def param_prep():
        for l in range(L):
            bvst = B("vst", region="A")
            srcs = [(norm1_g_d[l], 0, 8), (norm2_g_d[l], 8, 8), (conv_b_d[l], 16, 3), (conv_ln_g_d[l], 19, 3),
                    (conv_ln_b_d[l], 22, 3), (pool_scale_d[l], 25, 2)]
            for src, r0, nr in srcs:
                dma("sync", vst[r0:r0 + nr, :], src.rearrange("(r c) -> r c", c=128), [], [bvst])
            bk, bb = next_bank()
            transposes([(bk[:, 0:27], vst[0:27, :], ident[0:27, 0:27])], [bvst, bC], [bb])
            vec(lambda e, bk=bk, l=l: e.tensor_copy(out=pv[:, l, 0:27], in_=bk[:, 0:27]), [bb], [bPV])
            bcw = B("cwst", region="A")
            dma("sync", cwst[0:CW, :], conv_w_d[l], [], [bcw])
            bk, bb = next_bank()
            transposes([(bk[:, ct * 32:ct * 32 + CW], cwst[0:CW, ct * 128:(ct + 1) * 128], ident[0:CW, 0:CW])
                        for ct in range(3)], [bcw, bC], [bb])
            vec(lambda e, bk=bk, l=l: e.tensor_copy(out=cwT[:, l, :, :],
                                                     in_=bk[:, 0:96].rearrange("p (c k) -> p c k", k=32)[:, :, 0:CW]),
                [bb], [bPV])
        bvst = B("vst", region="A")
        dma("sync", vst[0:8, :], final_g_d.rearrange("(r c) -> r c", c=128), [], [bvst])
        bk, bb = next_bank()
        transposes([(bk[:, 0:8], vst[0:8, :], ident[0:8, 0:8])], [bvst, bC], [bb])
        vec(lambda e, bk=bk: e.tensor_copy(out=gFv[:], in_=bk[:, 0:8]), [bb], [bPV])

    pend = {}

    def x_tr(s, j):
        for k in range(8):
            bk, bb = next_bank()
            transposes([(bk[:, q * 128:(q + 1) * 128], xstage[(4 * j + q) % 8][:, k * 128:(k + 1) * 128], ident[:])
                        for q in range(4)], [B("xstage", (4 * j + q) % 8, region="H") for q in range(4)] + [bC], [bb])
            if k % 2 == 0:
                act_op(xT[:, k, j * NT:(j + 1) * NT], bk[:], AF.Copy, [bb], [B("xT", k, j)])
            else:
                vec(lambda e, bk=bk, k=k, j=j: e.tensor_copy(out=xT[:, k, j * NT:(j + 1) * NT], in_=bk[:]),
                    [bb], [B("xT", k, j)])
        pend[j] = rms_tok_stats(j, rtok)

    for s in range(nseq):
        if s == 0:
            for j in range(NJ):
                x_tr(0, j)
                if j + 2 < NJ:
                    x_dma(0, j + 2)
            param_prep()

        for l in range(L):
            P.new_phase(["A", "H", "D", "E"])
            bbs = B("BT")
            dma("sync", BT[0:96].rearrange("p h t -> p (h t)"),
                b_s_d[l].rearrange("h t -> (h t)").partition_broadcast(96), [], [bbs])
            bsgB = B("sgB")
            dma("sync", sgB[:, 0, :], sg_ln_g_d[l].partition_broadcast(128), [], [bsgB])
            dma("sync", sgB[:, 1, :], sg_ln_b_d[l].partition_broadcast(128), [], [bsgB])
            bdg = [B("diag", i, region=diag_tiles[i][1]) for i in range(3 * NPE)]
            bws = B("wsst", region="H")
            dma("sync", wsst[:], w_s_d[l].rearrange("h t s -> t h s"), [], [bws])
            bk, bb = next_bank()
            transposes([(bk[:, h * 128:(h + 1) * 128], wsst[:, h, :], ident[:]) for h in range(4)], [bws, bC], [bb])
            bwsT = B("wsT")
            for h in range(4):
                tt(wsT[:, h, :], bk[:, h * 128:(h + 1) * 128], mask[:], ALU.mult, [bb, bC], [bwsT])

            brts = [pend[j] for j in range(NJ)]

            ybufs = [ybuf, ybuf1]
            by = [[B("ybuf", i, ct, region=("A", "H")[i]) for ct in range(3)] for i in range(2)]
            bz = B("zc", region="D")
            pool(lambda e: e.memset(ybuf[:, :, 0:HALO], 0.0), [], by[0])
            pool(lambda e: e.memset(zc[:, :, 0:ZH], 0.0), [], [bz])
            bh = [B("hT", k, region="H") for k in range(8)]
            bug = B("ug", region="D")
            bvg = [B("vg", cq, region="H") for cq in range(4)]
            bacc = [B("acc", ct, region="A") for ct in range(3)]
            baccb = [B("accb", ct, region="A") for ct in range(3)]
            bsqb = [B("sqb", ct, region="A") for ct in range(3)]
            pslots = [(accb[:, i, :], baccb[i]) for i in range(3)] + [(sqb[:, i, :], bsqb[i]) for i in range(3)]
            pslot_rr = [0]
            bsil = B("sil", region="A")
            bya = B("ya", region="H")
            byb = B("yb", region="H")
            byc = B("yc", region="H")
            bvn = [B("vn", cq, region="H") for cq in range(4)]
            bst = B("bnst")
            bsA = B("sA", region="E")
            bsB = B("sB", region="E")
            bpp = B("pp", region="D")
            blt = B("lnt")
            W = ZH + NT

            def proj(c0, m, bi):
                bk, bb = next_bank()
                mm(bk[0:m, :], [(w_in[:, k, c0:c0 + m], hT[:, k, :]) for k in range(8)],
                   bh + [B("w_in", bi)], [bb])
                return bk, bb

            def fa(j):
                rms_apply(j, brts[j], rtok, lambda k, l=l: pvcol(l, PV_G1 + k), hT, lambda k: bh[k], slice(0, NT))

            def fb1(j):
                for h in range(4):
                    ku, bu = proj(2 * DA + 96 * h, 96, 2)
                    act_op(ug[0:96, h, :], ku[0:96, :], AF.Gelu, [bu], [bug])
                for cq in range(4):
                    bk, bb = next_bank()
                    mm(bk[:, 0:DB], [(hT[:, k, cq * 128:(cq + 1) * 128], w_in[:, k, 2 * DA + DB:2 * DA + 2 * DB])
                                     for k in range(8)], bh + [B("w_in", 3)], [bb])
                    act_op(vg[:, cq, :], bk[:, 0:DB], AF.Gelu, [bb], [bvg[cq]])

            def fb2(j):
                yb_ = ybufs[j % 2]
                byj = by[j % 2]
                if j > 0:
                    for ct in range(3):
                        act_op(yb_[:, ct, 0:HALO], ybufs[(j - 1) % 2][:, ct, NT:NT + HALO], AF.Copy,
                               [by[(j - 1) % 2][ct]], [byj[ct]])
                for m in range(2):
                    kz, bzz = proj(2 * DA + 2 * DB + 128 * m, 128, 4)
                    act_op(zc[:, m, ZH:ZH + NT], kz[:], AF.Copy, [bzz], [bz])
                glu = []
                for ct in range(3):
                    ka, ba = proj(ct * 128, 128, 0)
                    kg, bg = proj(DA + ct * 128, 128, 1)
                    act_op(yb_[:, ct, HALO:HALO + NT], kg[:], AF.Sigmoid, [bg], [byj[ct]])
                    glu.append((ct, ka, ba))
                return glu

            def glu_mult(j, ct, ka, ba):
                yb_ = ybufs[j % 2]
                tt(yb_[:, ct, HALO:HALO + NT], yb_[:, ct, HALO:HALO + NT], ka[:], ALU.mult,
                   [by[j % 2][ct], ba], [by[j % 2][ct]])

            def lnv_stats(j):
                for cq in range(4):
                    vec(lambda e, cq=cq: e.bn_stats(out=bnst[0][:, cq * 6:cq * 6 + 6], in_=vg[:, cq, :]), [bvg[cq]], [bst])
                for cq in range(4):
                    vec(lambda e, cq=cq: e.bn_aggr(out=bnst[1][:, 2 * cq:2 * cq + 2], in_=bnst[0][:, cq * 6:cq * 6 + 6]),
                        [bst], [bst])
                mvv = bnst[1][:, 0:8].rearrange("p (q t) -> p q t", t=2)
                ts(bnst[1][:, 8:12], mvv[:, :, 1], LN_EPS, None, ALU.add, None, [bst], [bst])
                pool(lambda e: e.tensor_tensor(out=bnst[1][:, 8:12], in0=bnst[1][:, 8:12], in1=mhalf[:, 0:4], op=ALU.pow),
                     [bst, bC], [bst])

            def lnv_apply(j):
                for cq in range(4):
                    stt(vg[:, cq, :], vg[:, cq, :], bnst[1][:, 2 * cq:2 * cq + 1], sgB[:, 0, :], ALU.subtract, ALU.mult,
                        [bvg[cq], bst, bsgB], [bvg[cq]])
                    stt(vn[:, cq, :], vg[:, cq, :], bnst[1][:, 8 + cq:9 + cq], sgB[:, 1, :], ALU.mult, ALU.add,
                        [bvg[cq], bst, bsgB], [bvn[cq]])

            def pool_branch(j):
                tt(sA[:, :, 2:W], zc[:, :, 2:W], zc[:, :, 1:W - 1], ALU.add, [bz], [bsA])
                tt(sB[:, :, 4:W], sA[:, :, 4:W], sA[:, :, 2:W - 2], ALU.add, [bsA], [bsB])

                def pool_out(src, p0, m, w):
                    stt(pp[p0:p0 + 64, m, :], src[p0:p0 + 64, m, ZH:W], 1.0 / w, zc[p0:p0 + 64, m, ZH:W],
                        ALU.mult, ALU.subtract, [bsA, bsB, bz], [bpp])
                    if j == 0:
                        tt(src[p0:p0 + 64, m, ZH:2 * ZH], src[p0:p0 + 64, m, ZH:2 * ZH], invcnt[p0:p0 + 64, m, :],
                           ALU.mult, [bsA, bsB, bC], [bsA, bsB])
                        tt(pp[p0:p0 + 64, m, 0:ZH], src[p0:p0 + 64, m, ZH:2 * ZH], zc[p0:p0 + 64, m, ZH:2 * ZH],
                           ALU.subtract, [bsA, bsB, bz], [bpp])
                pool_out(sA, 0, 0, 2)
                pool_out(sB, 64, 0, 4)
                tt(sA[:, 1, 8:W], sB[:, 1, 8:W], sB[:, 1, 4:W - 4], ALU.add, [bsB], [bsA])
                pool_out(sA, 0, 1, 8)
                tt(sB[64:128, 1, 16:W], sA[64:128, 1, 16:W], sA[64:128, 1, 8:W - 8], ALU.add, [bsA], [bsB])
                pool_out(sB, 64, 1, 16)
                if j + 1 < NJ:
                    act_op(zc[:, :, 0:ZH], zc[:, :, NT:NT + ZH], AF.Copy, [bz], [bz])

            convps = {}

            def tap30(j):
                for ct in range(3):
                    act_op(acc[:, ct, :], ybufs[j % 2][:, ct, HALO:HALO + NT], AF.Identity, [by[j % 2][ct], bPV],
                           [bacc[ct]], scale=cwT[:, l, ct, CW - 1:CW], bias=pvcol(l, PV_CB + ct))

            def tap_ops(j):
                a_ = [(ct, k) for k in range(NPE, NPE + NACT) for ct in range(3)]
                d_ = [(ct, k) for k in range(NPE + NACT, CW - 1) for ct in range(3)]
                out_ = []
                for i in range(max(len(a_), len(d_))):
                    if i < len(d_):
                        out_.append(d_[i])
                    if i < len(a_):
                        out_.append(a_[i])
                return out_

            def tap(j, ct, k):
                if k >= NPE + NACT:
                    stt(acc[:, ct, :], ybufs[j % 2][:, ct, k:k + NT], cwT[:, l, ct, k:k + 1], acc[:, ct, :],
                        ALU.mult, ALU.add, [by[j % 2][ct], bPV, bacc[ct]], [bacc[ct]])
                    return
                slot, bsl = pslots[pslot_rr[0] % len(pslots)]
                pslot_rr[0] += 1
                act_op(slot, ybufs[j % 2][:, ct, k:k + NT], AF.Identity, [by[j % 2][ct], bPV], [bsl],
                       scale=cwT[:, l, ct, k:k + 1])
                bk, bb = convps[j][ct]
                mm(bk[:], [(identb[:], slot)], [bsl, bC], [bb], first=False, last=(k == NPE + NACT - 1))

            def conv_pe(j):
                convps[j] = []
                for ct in range(3):
                    bk, bb = next_bank()
                    mm(bk[:], [(diag_tiles[ct * NPE + k][0][:], ybufs[j % 2][:, ct, k:k + NT]) for k in range(NPE)],
                       [by[j % 2][ct]] + bdg[ct * NPE:(ct + 1) * NPE], [bb], first=True, last=(NACT == 0))
                    convps[j].append((bk, bb))

            def acc_add(j):
                for ct in range(3):
                    tt(acc[:, ct, :], acc[:, ct, :], convps[j][ct][0][:], ALU.add, [bacc[ct], convps[j][ct][1]],
                       [bacc[ct]])

            fa(0)
            for ct in range(3):
                for k in range(NPE):
                    ts(diag_tiles[ct * NPE + k][0][:], ident[:], cwT[:, l, ct, k:k + 1], None, ALU.mult, None,
                       [bC, bPV], [bdg[ct * NPE + k]])
            fb1(0)
            lnv_stats(0)
            glu0 = fb2(0)
            lnv_apply(0)
            for g_ in glu0:
                glu_mult(0, *g_)
            pool_branch(0)
            tap30(0)
            conv_pe(0)
            for ct, k in tap_ops(0):
                tap(0, ct, k)

            brts2 = {}
            for j in range(NJ):
                cols = slice(j * NT, (j + 1) * NT)
                nj = j + 1 if j + 1 < NJ else None
                acc_add(j)
                for ct in range(3):
                    act_op(accb[:, ct, :], acc[:, ct, :], AF.Copy, [bacc[ct]], [baccb[ct]])
                    act_op(sqb[:, ct, :], acc[:, ct, :], AF.Square, [bacc[ct]], [bsqb[ct]])
                if nj is not None and j == 0:
                    fa(nj)
                pmb = []
                for cq in range(4):
                    bk2, bb2 = next_bank()
                    mms([(bk2[0:96, h * 128:(h + 1) * 128], [(vn[:, cq, 96 * h:96 * h + 96], wsT[:, h, :])])
                         for h in range(4)], [bvn[cq], bwsT], [bb2])
                    pmb.append((bk2, bb2))
                for m in range(2):
                    bk2, bb2 = next_bank()
                    mm(bk2[:], [(wpl[:, m, :], pp[:, m, :])], [bpp, B("wpl")], [bb2])
                    act_op(yc[:, m, :], bk2[:], AF.Identity, [bb2, bPV], [byc], scale=pvcol(l, PV_PS + m))
                bk, bb = next_bank()
                groups = []
                for q in range(4):
                    groups.append((bk[:, 2 * q:2 * q + 1],
                                   [(accb[:, ct, q * 128:(q + 1) * 128], ocS[:, 1:2]) for ct in range(3)]))
                    groups.append((bk[:, 2 * q + 1:2 * q + 2],
                                   [(sqb[:, ct, q * 128:(q + 1) * 128], ocS[:, 1:2]) for ct in range(3)]))
                mms(groups, baccb + bsqb + [bC], [bb])
                ts(rtok2[:, 0:8], bk[:, 0:8], 1.0 / DA, None, ALU.mult, None, [bb], [blt])
                me = rtok2[:, 0:8].rearrange("p (q t) -> p q t", t=2)
                tt(rtok2[:, 8:12], me[:, :, 0], me[:, :, 0], ALU.mult, [blt], [blt])
                tt(rtok2[:, 8:12], me[:, :, 1], rtok2[:, 8:12], ALU.subtract, [blt], [blt])
                ts(rtok2[:, 8:12], rtok2[:, 8:12], LN_EPS, None, ALU.add, None, [blt], [blt])
                pool(lambda e: e.tensor_tensor(out=rtok2[:, 8:12], in0=rtok2[:, 8:12], in1=mhalf[:, 0:4], op=ALU.pow),
                     [blt, bC], [blt])
                for cq in range(4):
                    bk2, bb2 = pmb[cq]
                    t = tlnv[cq % 2]
                    bt = (bsA, bsB)[cq % 2]
                    tt(t[0:96].rearrange("p (h t) -> p h t", t=128), bk2[0:96, :].rearrange("p (h t) -> p h t", t=128),
                       BT[0:96], ALU.add, [bb2, bbs], [bt])
                    tt(yb[0:96, :, cq * 128:(cq + 1) * 128], ug[0:96, :, cq * 128:(cq + 1) * 128],
                       t[0:96].rearrange("p (h t) -> p h t", t=128), ALU.mult, [bug, bt], [byb])
                if nj is not None:
                    fb1(nj)
                stt(rtok2[:, 12:16], me[:, :, 0], -1.0, rtok2[:, 8:12], ALU.mult, ALU.mult, [blt], [blt])
                krs, brs = bcast_tok([rtok2[:, 8 + q:9 + q] for q in range(4)], [blt])
                knm, bnm = bcast_tok([rtok2[:, 12 + q:13 + q] for q in range(4)], [blt])
                for ct in range(3):
                    t = tlnv[ct % 2]
                    bt = (bsA, bsB)[ct % 2]
                    tt(t, acc[:, ct, :], krs[:], ALU.mult, [bacc[ct], brs], [bt])
                    tt(t, t, knm[:], ALU.add, [bt, bnm], [bt])
                    act_op(sil[:, ct, :], t, AF.Silu, [bt, bPV], [bsil],
                           scale=pvcol(l, PV_LNG + ct), bias=pvcol(l, PV_LNB + ct))
                glu = []
                if nj is not None:
                    lnv_stats(nj)
                    glu = fb2(nj)
                    lnv_apply(nj)
                    for g_ in glu:
                        glu_mult(nj, *g_)
                for co in range(3):
                    bk2, bb2 = next_bank()
                    mm(bk2[:], [(w_pw[:, ci, co * 128:(co + 1) * 128], sil[:, ci, :]) for ci in range(3)],
                       [bsil, B("w_pw")], [bb2])
                    act_op(ya[:, co, :], bk2[:], AF.Copy, [bb2], [bya])
                if nj is not None:
                    pool_branch(nj)
                    tap30(nj)
                    conv_pe(nj)
                ptaps = tap_ops(nj) if nj is not None else []
                per = (len(ptaps) + 7) // 8
                for n in range(8):
                    ncol = slice(n * 128, (n + 1) * 128)
                    bk2, bb2 = next_bank()
                    pairs = [(wo_b[0:96, h, ncol], yb[0:96, h, :]) for h in range(4)]
                    pairs += [(wo_c[:, m, ncol], yc[:, m, :]) for m in range(2)]
                    pairs += [(wo_a[:, i, ncol], ya[:, i, :]) for i in range(3)]
                    mm(bk2[:], pairs, [bya, byb, byc, B("wo_a", region="R"), B("wo_b", region="R"),
                                       B("wo_c", region="R")], [bb2])
                    for ct, k in ptaps[n * per:(n + 1) * per]:
                        tap(nj, ct, k)
                    if n == 0 and j + 2 < NJ:
                        fa(j + 2)
                    tt(xT[:, n, cols], xT[:, n, cols], bk2[:], ALU.add, [B("xT", n, j), bb2], [B("xT", n, j)])
                brts2[j] = rms_tok_stats(j, rtokB)

            P.new_phase(["A", "H", "D", "E", "R"])
            nfp = NF // 2
            for fp in range(min(NRING, nfp)):
                load_ring(l, fp, fp % NRING)
            load_wd(l, 0)
            bh2 = [[B("h2T", k, j, region="H") for k in range(8)] for j in range(NJ)]
            n2b = [rms_bcast(j, brts2[j], rtokB) for j in range(NJ)]

            def n2scale(j):
                rms_scale(j, n2b[j], lambda k, l=l: pvcol(l, PV_G2 + k), h2T, lambda k, j=j: bh2[j][k],
                          slice(j * NT, (j + 1) * NT))
            n2scale(0)
            nxt = None
            if l + 1 < L:
                nxt = l + 1
            elif s + 1 < nseq:
                nxt = 0
            if nxt is not None:
                load_mixer_w1(nxt)
            for ci, (f0, nf) in enumerate(CHUNKS):
                bact = [[B("act", fi, j, region="A") for j in range(NJ)] for fi in range(nf)]
                for fi in range(nf):
                    f = f0 + fi
                    fp, half = f // 2, f % 2
                    slot = fp % NRING
                    fc = slice(half * 128, half * 128 + 128)
                    for j in range(NJ):
                        cols = slice(j * NT, (j + 1) * NT)
                        kg, bg = next_bank()
                        mm(kg[:], [(ring[slot][:, 0, k, fc], h2T[:, k, cols]) for k in range(8)],
                           bh2[j] + [B("ring", slot, 0, region="R")], [bg])
                        ku, bu = next_bank()
                        mm(ku[:], [(ring[slot][:, 1, k, fc], h2T[:, k, cols]) for k in range(8)],
                           bh2[j] + [B("ring", slot, 1, region="R")], [bu])
                        t = tlnv[(fi * NJ + j) % 2]
                        bt = B(("sA", "sB")[(fi * NJ + j) % 2], region="E")
                        act_op(t, kg[:], AF.Silu, [bg], [bt])
                        if ci == 0 and fi == 0 and j + 1 < NJ:
                            n2scale(j + 1)
                        tt(act[:, fi, cols], t, ku[:], ALU.mult, [bt, bu], [bact[fi][j]])
                    if half == 1 and fp + NRING < nfp:
                        load_ring(l, fp + NRING, slot)
                last = ci + 1 == len(CHUNKS)
                order = [(n, j) for j in range(NJ) for n in range(8)] if last else \
                        [(n, j) for n in range(8) for j in range(NJ)]
                for n, j in order:
                    ncol = slice(n * 128, (n + 1) * 128)
                    cols = slice(j * NT, (j + 1) * NT)
                    bk, bb = next_bank()
                    mm(bk[:], [(wd[:, fi, ncol], act[:, fi, cols]) for fi in range(nf)],
                       [bact[fi][j] for fi in range(nf)] + [B("wd", region="D")], [bb])
                    tt(xT[:, n, cols], xT[:, n, cols], bk[:], ALU.add, [B("xT", n, j), bb], [B("xT", n, j)])
                    if last and n == 7 and j > 0:
                        pend[j - 1] = rms_tok_stats(j - 1, rtok)
                if last:
                    pend[NJ - 1] = rms_tok_stats(NJ - 1, rtok)
                if ci + 1 < len(CHUNKS):
                    load_wd(l, ci + 1)
            if nxt is not None:
                P.new_phase(["R"])
                load_mixer_wout(nxt)

        P.new_phase(["A", "H", "D", "E"])
        if s + 1 < nseq:
            x_dma(s + 1, 0)
            x_dma(s + 1, 1)
        bgfb = B("gFB", region="A")
        dma("sync", gFB[:], final_g_d.partition_broadcast(128), [], [bgfb])

        def fin_store(j):
            for q in range(4):
                tb = j * 4 + q
                os_ = ostage[tb % 4]
                bos = B("ostage", tb % 4, region="A")
                for hf in range(2):
                    bk, bb = next_bank()
                    transposes([(bk[:, kk * 128:(kk + 1) * 128],
                                 xT[:, hf * 4 + kk, j * NT + q * 128:j * NT + (q + 1) * 128], ident[:])
                                for kk in range(4)], [B("xT", hf * 4 + kk, j) for kk in range(4)] + [bC], [bb])
                    stt(os_[:, hf * 512:(hf + 1) * 512], bk[:], rtok[:, 4 * j + q:4 * j + q + 1],
                        gFB[:, hf * 512:(hf + 1) * 512], ALU.mult, ALU.mult, [bb, pend[j], bgfb], [bos])
                dma("sync", out_d[s, tb * 128:(tb + 1) * 128, :], os_[:], [bos], [], final=True)

        for j in range(NJ):
            fin_store(j)
            if s + 1 < nseq:
                x_tr(s + 1, j)
                if j + 2 < NJ:
                    x_dma(s + 1, j + 2)

    P.emit(nc)
    return nc


_NC_CACHE = {}


def _get_nc(nseq, depth):
    key = (nseq, depth)
    if key not in _NC_CACHE:
        _NC_CACHE[key] = build_program(nseq=nseq, depth=depth)
    return _NC_CACHE[key]


_WNAMES = ["norm1_g", "w_in", "conv_w", "conv_b", "conv_ln_g", "conv_ln_b", "w_pw", "sg_ln_g", "sg_ln_b",
           "w_s", "b_s", "w_pool", "pool_scale", "w_out", "norm2_g", "w_gate_up", "w_down", "final_g"]


def kernel(**inputs):
    x = np.ascontiguousarray(np.asarray(inputs["x"], dtype=np.float32))
    bsz = x.shape[0]
    nseq = bsz // N_CORES
    depth = int(np.asarray(inputs["w_in"]).shape[0])
    nc = _get_nc(nseq, depth)
    ws = {n: np.ascontiguousarray(np.asarray(inputs[n], dtype=np.float32)) for n in _WNAMES}
    in_maps = []
    for c in range(N_CORES):
        m = dict(ws)
        m["x"] = x[c * nseq:(c + 1) * nseq]
        in_maps.append(m)
    res = run_bass_kernel_spmd(nc, in_maps, core_ids=list(range(N_CORES)))
    return np.concatenate([np.asarray(r["out"]) for r in res.results], axis=0).astype(np.float32)
```
